# Optimizing a Trainium2 kernel written in Bass

```python
import jax
import jax.numpy as jnp
from jax import lax
import numpy as np

D_MODEL = 1024
BATCH = 2
SEQ = 8192
DEPTH = 4
DEC_BATCH = 128
DEC_SEQ = 1
PAST_LEN = 8192
PAGE_SIZE = 128

N_META = 16
D_FF = 2048
EPS = 1e-6
LRU_WIDTH = D_MODEL // 2
LRU_BLOCKS = 8
LRU_BW = LRU_WIDTH // LRU_BLOCKS
CONV_WIDTH = 4
LRU_C = 8.0
SWA_HEAD_DIM = 64
SWA_HEADS = D_MODEL // (2 * SWA_HEAD_DIM)
SWA_KV_HEADS = SWA_HEADS // 4
SWA_GROUP = SWA_HEADS // SWA_KV_HEADS
WINDOW = 128
RET_DK = 64
RET_DV = 128
RET_HEADS = D_MODEL // (2 * RET_DV)
RET_CHUNK = 128
ROPE_BASE = 10000.0
GN_EPS = 1e-5
N_BRANCH = 3
IN_SIZES = (LRU_WIDTH, LRU_WIDTH,
            SWA_HEADS * SWA_HEAD_DIM, SWA_KV_HEADS * SWA_HEAD_DIM, SWA_KV_HEADS * SWA_HEAD_DIM,
            RET_HEADS * RET_DK, RET_HEADS * RET_DK, RET_HEADS * RET_DV, RET_HEADS * RET_DV,
            N_BRANCH * D_MODEL)
IN_DIM = sum(IN_SIZES)
IN_SPLITS = tuple(int(c) for c in np.cumsum(IN_SIZES)[:-1])

kernel_name = 'hybrid_lru_swa_retention_step'


def rms_norm(x, g):
    x32 = x.astype(jnp.float32)
    y = x32 * lax.rsqrt(jnp.mean(x32 * x32, axis=-1, keepdims=True) + EPS)
    return (y * g.astype(jnp.float32)).astype(x.dtype)


def swiglu(x, w_gu, w_down):
    gate, up = jnp.split(x @ w_gu, 2, axis=-1)
    return (jax.nn.silu(gate) * up) @ w_down


def rope(x, pos):
    half = x.shape[-1] // 2
    inv = ROPE_BASE ** (-jnp.arange(half, dtype=jnp.float32) / half)
    ang = pos.astype(jnp.float32)[:, None] * inv[None, :]
    cos = jnp.cos(ang)[:, None, :]
    sin = jnp.sin(ang)[:, None, :]
    x1, x2 = x[..., :half], x[..., half:]
    return jnp.concatenate([x1 * cos - x2 * sin, x1 * sin + x2 * cos], axis=-1)


def causal_conv(x_ext, w, b):
    t = x_ext.shape[1] - (CONV_WIDTH - 1)
    out = b[None, None, :] + x_ext[:, 0:t] * w[0]
    for j in range(1, CONV_WIDTH):
        out = out + x_ext[:, j:j + t] * w[j]
    return out


def linear_scan(log_a, b, h0):
    def combine(e1, e2):
        la1, b1 = e1
        la2, b2 = e2
        return la1 + la2, jnp.exp(la2) * b1 + b2
    cum_la, h = lax.associative_scan(combine, (log_a, b), axis=1)
    return h + jnp.exp(cum_la) * h0[:, None, :]


def rg_lru(xc, h0, w_a, b_a, w_x, b_x, lam):
    bsz, t, _ = xc.shape
    xb = xc.reshape(bsz, t, LRU_BLOCKS, LRU_BW)
    r = jax.nn.sigmoid(jnp.einsum('btnc,ncd->btnd', xb, w_a).reshape(bsz, t, LRU_WIDTH) + b_a)
    i = jax.nn.sigmoid(jnp.einsum('btnc,ncd->btnd', xb, w_x).reshape(bsz, t, LRU_WIDTH) + b_x)
    log_a = -LRU_C * r * jax.nn.softplus(-lam)
    b = jnp.sqrt(-jnp.expm1(2.0 * log_a)) * (i * xc)
    return linear_scan(log_a, b, h0)


def sink_probs(s, mask, sinks):
    sk = sinks.astype(jnp.float32).reshape(SWA_KV_HEADS, SWA_GROUP, 1, 1)
    s = jnp.where(mask, s, -jnp.inf)
    m = jnp.maximum(jnp.max(s, axis=-1, keepdims=True), sk)
    e = jnp.exp(s - m)
    return e / (jnp.sum(e, axis=-1, keepdims=True) + jnp.exp(sk - m))


def swa_prompt(q, k, v, sinks):
    bsz, t = q.shape[:2]
    pad = (-t) % WINDOW
    nb = (t + pad) // WINDOW

    def blocks(a):
        a = jnp.pad(a, ((0, 0), (pad, 0), (0, 0), (0, 0)))
        return a.reshape((bsz, nb, WINDOW) + a.shape[2:])

    def with_prev(a):
        prev = jnp.pad(a[:, :-1], ((0, 0), (1, 0), (0, 0), (0, 0), (0, 0)))
        return jnp.concatenate([prev, a], axis=2)

    qb = blocks(q).reshape(bsz, nb, WINDOW, SWA_KV_HEADS, SWA_GROUP, SWA_HEAD_DIM)
    kb = with_prev(blocks(k))
    vb = with_prev(blocks(v))
    s = jnp.einsum('bnqhgd,bnkhd->bnhgqk', qb, kb) * SWA_HEAD_DIM ** -0.5
    qi = jnp.arange(WINDOW)[:, None] + WINDOW
    kj = jnp.arange(2 * WINDOW)[None, :]
    in_win = (qi - kj >= 0) & (qi - kj < WINDOW)
    key_idx = jnp.arange(nb)[:, None] * WINDOW - WINDOW + kj
    mask = in_win[None] & (key_idx >= pad)[:, None, :]
    p = sink_probs(s, mask[None, :, None, None], sinks)
    o = jnp.einsum('bnhgqk,bnkhd->bnqhgd', p, vb)
    return o.reshape(bsz, nb * WINDOW, SWA_HEADS * SWA_HEAD_DIM)[:, pad:]


def swa_sample(q, k, v, ck, cv, sinks):
    bsz, t = q.shape[:2]
    buf = ck.shape[1]
    kk = jnp.concatenate([ck, k], axis=1)
    vv = jnp.concatenate([cv, v], axis=1)
    qpos = PAST_LEN + jnp.arange(t)
    kpos = PAST_LEN - buf + jnp.arange(buf + t)
    diff = qpos[:, None] - kpos[None, :]
    mask = (diff >= 0) & (diff < WINDOW)
    qg = q.reshape(bsz, t, SWA_KV_HEADS, SWA_GROUP, SWA_HEAD_DIM)
    s = jnp.einsum('bqhgd,bkhd->bhgqk', qg, kk) * SWA_HEAD_DIM ** -0.5
    p = sink_probs(s, mask, sinks)
    o = jnp.einsum('bhgqk,bkhd->bqhgd', p, vv).reshape(bsz, t, SWA_HEADS * SWA_HEAD_DIM)
    return o, kk[:, -buf:], vv[:, -buf:]


def retention_log_gamma():
    return jnp.log1p(-jnp.exp2(-5.0 - jnp.arange(RET_HEADS, dtype=jnp.float32)))


def retention_chunk(state, q, k, v, log_g):
    c = q.shape[1]
    n = jnp.arange(c, dtype=jnp.float32)
    diff = n[:, None] - n[None, :]
    expo = jnp.where(diff[None] >= 0, diff[None] * log_g[:, None, None], -jnp.inf)
    scores = jnp.einsum('bihd,bjhd->bhij', q, k) * jnp.exp(expo)
    o = jnp.einsum('bhij,bjhe->bihe', scores, v)
    cross_decay = jnp.exp((n[:, None] + 1.0) * log_g[None, :])
    o = o + jnp.einsum('bihd,bhde->bihe', q, state) * cross_decay[None, :, :, None]
    k_decay = jnp.exp((c - 1.0 - n)[:, None] * log_g[None, :])
    new_state = jnp.exp(c * log_g)[None, :, None, None] * state + jnp.einsum('bjhd,bjhe,jh->bhde', k, v, k_decay)
    return new_state, o


def retention_prompt(q, k, v, log_g):
    bsz, t = q.shape[:2]
    pad = (-t) % RET_CHUNK
    nc = (t + pad) // RET_CHUNK

    def chunks(a):
        a = jnp.pad(a, ((0, 0), (pad, 0), (0, 0), (0, 0)))
        return jnp.moveaxis(a.reshape((bsz, nc, RET_CHUNK) + a.shape[2:]), 1, 0)

    def step(state, xs):
        return retention_chunk(state, xs[0], xs[1], xs[2], log_g)

    s0 = jnp.zeros((bsz, RET_HEADS, RET_DK, RET_DV), jnp.float32)
    state, o = lax.scan(step, s0, (chunks(q), chunks(k), chunks(v)))
    o = jnp.moveaxis(o, 0, 1).reshape(bsz, nc * RET_CHUNK, RET_HEADS, RET_DV)[:, pad:]
    return o, state


def group_norm(o, gain):
    bsz, t = o.shape[:2]
    mu = jnp.mean(o, axis=-1, keepdims=True)
    var = jnp.mean(jnp.square(o - mu), axis=-1, keepdims=True)
    y = (o - mu) * lax.rsqrt(var + GN_EPS)
    return y.reshape(bsz, t, RET_HEADS * RET_DV) * gain


def token_mixer(u, pos, lp, state, buf):
    f32 = jnp.float32
    bsz, t, _ = u.shape
    (xa, ya, qs, ks, vs, qr, kr, vr, gr, gates) = jnp.split((u @ lp['w_in']).astype(f32), IN_SPLITS, axis=-1)
    if state is None:
        conv_hist = jnp.zeros((bsz, CONV_WIDTH - 1, LRU_WIDTH), f32)
        h0 = jnp.zeros((bsz, LRU_WIDTH), f32)
        ck = cv = s_ret = None
    else:
        ck, cv, conv_hist, h0, s_ret = [a.astype(f32) for a in state]
    x_ext = jnp.concatenate([conv_hist, xa], axis=1)
    xc = causal_conv(x_ext, lp['conv_w'].astype(f32), lp['conv_b'].astype(f32))
    h = rg_lru(xc, h0, lp['lru_w_a'].astype(f32), lp['lru_b_a'].astype(f32),
               lp['lru_w_x'].astype(f32), lp['lru_b_x'].astype(f32), lp['lru_lambda'].astype(f32))
    o_a = h * jax.nn.gelu(ya)
    new_conv = x_ext[:, -(CONV_WIDTH - 1):]
    new_h = h[:, -1]
    q = qs.reshape(bsz, t, SWA_HEADS, SWA_HEAD_DIM)
    k = ks.reshape(bsz, t, SWA_KV_HEADS, SWA_HEAD_DIM)
    v = vs.reshape(bsz, t, SWA_KV_HEADS, SWA_HEAD_DIM)
    if state is None:
        o_b = swa_prompt(q, k, v, lp['swa_sinks'])
        new_k, new_v = k[:, -buf:], v[:, -buf:]
    else:
        o_b, new_k, new_v = swa_sample(q, k, v, ck, cv, lp['swa_sinks'])
    log_g = retention_log_gamma()
    qc = rope(qr.reshape(bsz, t, RET_HEADS, RET_DK), pos)
    kc = rope(kr.reshape(bsz, t, RET_HEADS, RET_DK), pos) * RET_DK ** -0.5
    vc = vr.reshape(bsz, t, RET_HEADS, RET_DV)
    if state is None:
        o_r, new_s = retention_prompt(qc, kc, vc, log_g)
    else:
        new_s, o_r = retention_chunk(s_ret, qc, kc, vc, log_g)
    o_c = group_norm(o_r, lp['ret_norm'].astype(f32)) * jax.nn.silu(gr)
    g_a, g_b, g_c = jnp.split(jax.nn.sigmoid(gates), N_BRANCH, axis=-1)
    merged = (g_a * (o_a @ lp['w_branch_a'].astype(f32))
              + g_b * (o_b @ lp['w_branch_b'].astype(f32))
              + g_c * (o_c @ lp['w_branch_c'].astype(f32)))
    out = (merged @ lp['w_out'].astype(f32)).astype(u.dtype)
    new_state = [a.astype(u.dtype) for a in (new_k, new_v, new_conv, new_h, new_s)]
    return out, new_state


def trunk(x, pos, cache, params, final_norm, buf):
    new = ([], [], [], [], [])
    for l in range(DEPTH):
        lp = {name: w[l] for name, w in params.items()}
        st = None if cache is None else [c[l] for c in cache]
        x = x + 0.5 * swiglu(rms_norm(x, lp['ffn1_norm']), lp['ffn1_w_gu'], lp['ffn1_w_down'])
        mo, ns = token_mixer(rms_norm(x, lp['mix_norm']), pos, lp, st, buf)
        x = x + mo
        x = x + 0.5 * swiglu(rms_norm(x, lp['ffn2_norm']), lp['ffn2_w_gu'], lp['ffn2_w_down'])
        for acc, s in zip(new, ns):
            acc.append(s)
    return rms_norm(x, final_norm), [jnp.stack(acc) for acc in new]


def setup_inputs(seed: int = 0) -> dict:
    key = jax.random.key(seed)
    ks = jax.random.split(key, 40)
    f32 = jnp.float32

    def nrm(k, shape, scale):
        return scale * jax.random.normal(k, shape, f32)

    def gain(k, shape):
        return 1.0 + 0.02 * jax.random.normal(k, shape, f32)

    buf = min(WINDOW, PAST_LEN)
    s = jax.random.uniform(ks[20], (DEPTH, LRU_WIDTH), f32, 0.9, 0.999) ** (1.0 / LRU_C)
    lam = jnp.log(s) - jnp.log1p(-s)
    hv = SWA_HEADS * SWA_HEAD_DIM
    rv = RET_HEADS * RET_DV
    return {
        'x_prompt': nrm(ks[0], (BATCH, SEQ, D_MODEL), 1.0),
        'x_sample': nrm(ks[1], (DEC_BATCH, DEC_SEQ, D_MODEL), 1.0),
        'cache_swa_k': nrm(ks[2], (DEPTH, DEC_BATCH, buf, SWA_KV_HEADS, SWA_HEAD_DIM), 1.0),
        'cache_swa_v': nrm(ks[3], (DEPTH, DEC_BATCH, buf, SWA_KV_HEADS, SWA_HEAD_DIM), 1.0),
        'state_conv': nrm(ks[4], (DEPTH, DEC_BATCH, CONV_WIDTH - 1, LRU_WIDTH), 1.0),
        'state_lru': nrm(ks[5], (DEPTH, DEC_BATCH, LRU_WIDTH), 0.5),
        'state_ret': nrm(ks[6], (DEPTH, DEC_BATCH, RET_HEADS, RET_DK, RET_DV), 1.0),
        'meta_tokens': nrm(ks[7], (N_META, D_MODEL), 1.0),
        'ffn1_norm': gain(ks[8], (DEPTH, D_MODEL)),
        'ffn1_w_gu': nrm(ks[9], (DEPTH, D_MODEL, 2 * D_FF), D_MODEL ** -0.5),
        'ffn1_w_down': nrm(ks[10], (DEPTH, D_FF, D_MODEL), D_FF ** -0.5),
        'mix_norm': gain(ks[11], (DEPTH, D_MODEL)),
        'w_in': nrm(ks[12], (DEPTH, D_MODEL, IN_DIM), D_MODEL ** -0.5),
        'conv_w': nrm(ks[13], (DEPTH, CONV_WIDTH, LRU_WIDTH), CONV_WIDTH ** -0.5),
        'conv_b': nrm(ks[14], (DEPTH, LRU_WIDTH), 0.01),
        'lru_w_a': nrm(ks[15], (DEPTH, LRU_BLOCKS, LRU_BW, LRU_BW), LRU_BW ** -0.5),
        'lru_b_a': nrm(ks[16], (DEPTH, LRU_WIDTH), 0.01),
        'lru_w_x': nrm(ks[17], (DEPTH, LRU_BLOCKS, LRU_BW, LRU_BW), LRU_BW ** -0.5),
        'lru_b_x': nrm(ks[18], (DEPTH, LRU_WIDTH), 0.01),
        'lru_lambda': lam,
        'swa_sinks': nrm(ks[21], (DEPTH, SWA_HEADS), 1.0),
        'ret_norm': gain(ks[22], (DEPTH, rv)),
        'w_branch_a': nrm(ks[23], (DEPTH, LRU_WIDTH, D_MODEL), LRU_WIDTH ** -0.5),
        'w_branch_b': nrm(ks[24], (DEPTH, hv, D_MODEL), hv ** -0.5),
        'w_branch_c': nrm(ks[25], (DEPTH, rv, D_MODEL), rv ** -0.5),
        'w_out': nrm(ks[26], (DEPTH, D_MODEL, D_MODEL), D_MODEL ** -0.5),
        'ffn2_norm': gain(ks[27], (DEPTH, D_MODEL)),
        'ffn2_w_gu': nrm(ks[28], (DEPTH, D_MODEL, 2 * D_FF), D_MODEL ** -0.5),
        'ffn2_w_down': nrm(ks[29], (DEPTH, D_FF, D_MODEL), D_FF ** -0.5),
        'final_norm': gain(ks[30], (D_MODEL,)),
    }


def reference(x_prompt, x_sample, cache_swa_k, cache_swa_v, state_conv, state_lru, state_ret,
              meta_tokens, ffn1_norm, ffn1_w_gu, ffn1_w_down, mix_norm, w_in, conv_w, conv_b,
              lru_w_a, lru_b_a, lru_w_x, lru_b_x, lru_lambda, swa_sinks, ret_norm,
              w_branch_a, w_branch_b, w_branch_c, w_out, ffn2_norm, ffn2_w_gu, ffn2_w_down, final_norm):
    params = {
        'ffn1_norm': ffn1_norm, 'ffn1_w_gu': ffn1_w_gu, 'ffn1_w_down': ffn1_w_down,
        'mix_norm': mix_norm, 'w_in': w_in, 'conv_w': conv_w, 'conv_b': conv_b,
        'lru_w_a': lru_w_a, 'lru_b_a': lru_b_a, 'lru_w_x': lru_w_x, 'lru_b_x': lru_b_x,
        'lru_lambda': lru_lambda, 'swa_sinks': swa_sinks, 'ret_norm': ret_norm,
        'w_branch_a': w_branch_a, 'w_branch_b': w_branch_b, 'w_branch_c': w_branch_c,
        'w_out': w_out, 'ffn2_norm': ffn2_norm, 'ffn2_w_gu': ffn2_w_gu, 'ffn2_w_down': ffn2_w_down,
    }
    buf = cache_swa_k.shape[2]
    bsz = x_prompt.shape[0]
    meta = jnp.broadcast_to(meta_tokens.astype(x_prompt.dtype)[None], (bsz, N_META, D_MODEL))
    xp = jnp.concatenate([meta, x_prompt], axis=1)
    pos_p = jnp.arange(xp.shape[1])
    pos_s = PAST_LEN + jnp.arange(x_sample.shape[1])
    yp, sp = trunk(xp, pos_p, None, params, final_norm, buf)
    ys, ss = trunk(x_sample, pos_s, [cache_swa_k, cache_swa_v, state_conv, state_lru, state_ret],
                   params, final_norm, buf)
    return (yp[:, N_META:], ys, sp[0], sp[1], sp[2], sp[3], sp[4], ss[0], ss[1], ss[2], ss[3], ss[4])
```

```python
import numpy as np
from contextlib import ExitStack
import concourse.bass as bass
import concourse.mybir as mybir
from concourse.bass_utils import run_bass_kernel_spmd

F32 = mybir.dt.float32
BF16 = mybir.dt.bfloat16
AF = mybir.ActivationFunctionType
ALU = mybir.AluOpType

D = 1024
DFF = 2048
L = 4
KC = 8
PADN = 112
NMETA = 16
SEQ = 8192
SEQP = PADN + NMETA + SEQ
NBLK = SEQP // 128
NS = 16
NTMAX = 512
EPS = 1e-6
GN_EPS = 1e-5
WSLOT = 5120
NWSLOT = 3

ENGS = ("pe", "act", "dve", "pool", "sp")
NDMA_SEMS = 40
NSW_SEMS = 12
SEM_EPOCH = 4000


class _Op:
    __slots__ = ("eng", "fn", "deps", "dma", "sig", "sigval", "dsem", "dval", "dprev")

    def __init__(self, eng, fn, deps, dma):
        self.eng = eng
        self.fn = fn
        self.deps = deps
        self.dma = dma
        self.sig = False
        self.sigval = 0
        self.dsem = None
        self.dval = 0
        self.dprev = None


class Sched:
    def __init__(self, nc):
        self.nc = nc
        self.ops = []
        self.lastw = {}
        self.readers = {}

    def op(self, eng, fn, r=(), w=(), dma=False):
        al = getattr(self, "alias", {})
        r = [al.get(k, k) if isinstance(k, str) else k for k in r]
        w = [al.get(k, k) if isinstance(k, str) else k for k in w]
        idx = len(self.ops)
        deps = set()
        for k in r:
            if k in self.lastw:
                deps.add(self.lastw[k])
        for k in w:
            if k in self.lastw:
                deps.add(self.lastw[k])
            deps.update(self.readers.get(k, ()))
        for k in w:
            self.lastw[k] = idx
            self.readers[k] = []
        for k in r:
            self.readers.setdefault(k, []).append(idx)
        deps.discard(idx)
        self.ops.append(_Op(eng, fn, deps, dma))
        return idx

    def emit(self, final_wait_eng="sp"):
        nc = self.nc
        ops = self.ops
        alldma = {i for i, o in enumerate(ops) if o.dma}
        ops.append(_Op(final_wait_eng, None, alldma, False))
        for o in ops:
            for d in o.deps:
                p = ops[d]
                if p.dma:
                    continue
                if p.eng == "pe" and o.eng == "pe" and not o.dma:
                    continue
                p.sig = True
        cnt = {e: 0 for e in ENGS}
        for o in ops:
            if o.dma:
                continue
            if o.sig:
                cnt[o.eng] += 1
                o.sigval = cnt[o.eng]
        dtot = [0] * NDMA_SEMS
        dlast = [None] * NDMA_SEMS
        nd = 0
        nsw = 0
        nhw = 0
        for i, o in enumerate(ops):
            if o.dma:
                if o.eng == "pool":
                    s = nsw % NSW_SEMS
                    nsw += 1
                else:
                    s = NSW_SEMS + nhw % (NDMA_SEMS - NSW_SEMS)
                    nhw += 1
                nd += 1
                o.dsem = s
                o.dprev = dlast[s]
                dtot[s] += 16
                o.dval = dtot[s]
                dlast[s] = i
        plans = []
        seen = {e: {} for e in ENGS}
        for i, o in enumerate(ops):
            need = {}
            deps = set(o.deps)
            if o.dma and o.dprev is not None:
                deps.add(o.dprev)
            for d in deps:
                p = ops[d]
                if p.dma:
                    key = ("d", p.dsem)
                    val = p.dval
                else:
                    if p.eng == "pe" and o.eng == "pe" and not o.dma:
                        continue
                    key = ("e", p.eng, (p.sigval - 1) // SEM_EPOCH)
                    val = (p.sigval - 1) % SEM_EPOCH + 1
                if need.get(key, 0) < val:
                    need[key] = val
            w = []
            sd = seen[o.eng]
            for key, val in need.items():
                if sd.get(key, 0) >= val:
                    continue
                sd[key] = val
                w.append((key, val))
            plans.append(w)
        self.stats = dict(cnt=cnt, ndma=nd, nops=len(ops))
        with ExitStack() as es:
            esem = {e: [es.enter_context(nc.semaphore("s_%s%d" % (e, i))) for i in range(cnt[e] // SEM_EPOCH + 1)] for e in ENGS}
            dsem = [es.enter_context(nc.semaphore("d%d" % i)) for i in range(NDMA_SEMS)]
            block = es.enter_context(nc.Block())

            def mk(ename):
                def body(eng):
                    for i, o in enumerate(ops):
                        if o.eng != ename:
                            continue
                        for key, val in plans[i]:
                            sem = dsem[key[1]] if key[0] == "d" else esem[key[1]][key[2]]
                            eng.wait_ge(sem, val)
                        if o.fn is None:
                            continue
                        ins = o.fn(eng)
                        if o.dma:
                            ins.then_inc(dsem[o.dsem], 16)
                        elif o.sig:
                            ins.then_inc(esem[ename][(o.sigval - 1) // SEM_EPOCH], 1)
                return body

            block.tensor(mk("pe"))
            block.scalar(mk("act"))
            block.vector(mk("dve"))
            block.gpsimd(mk("pool"))
            block.sync(mk("sp"))


def _wblocks():
    bl = []
    for f in (1, 2):
        for hf in range(2):
            for b in range(4):
                bl.append(("gu%d_%d" % (f, 4 * hf + b), 4096))
            for b in range(2):
                bl.append(("dn%d_%d" % (f, 2 * hf + b), 4096))
        if f == 1:
            bl += [("inXA", 4096), ("inYA", 4096), ("lru", 1024), ("inQK", 5120), ("inV", 5120), ("inRq", 4096), ("inRk", 4096),
                   ("inGR", 4096)]
            for m in range(8):
                bl.append(("gb%d" % m, 4608))
            bl += [("wo0", 4096), ("wo1", 4096)]
    off = {}
    o = 0
    for n, x in bl:
        off[n] = (o, x)
        o += x
    return bl, off, o


WBL, WOFF, XTOT = _wblocks()


def _tile_w(W, cols=None):
    K = W.shape[0]
    a = W.reshape(K // 128, 128, W.shape[1]).transpose(1, 0, 2)
    if cols is not None:
        a = a[:, :, cols]
    return np.ascontiguousarray(a).reshape(128, -1)


def _host_weights(inp):
    Wall = np.zeros((L, 128, XTOT), np.float32)
    r = np.arange
    qs0, ks0, vs0, qr0, kr0, vr0, gr0, g0 = 1024, 1536, 1664, 1792, 2048, 2304, 2816, 3328
    qs_cols = np.concatenate([np.concatenate([qs0 + 64 * c + r(64), qs0 + 64 * (4 + c) + r(64)]) for c in range(4)])

    def sw(base):
        return np.concatenate([np.concatenate([base + 64 * h + 32 + r(32), base + 64 * h + r(32)]) for h in range(4)])

    ob_rows = np.concatenate([np.array([(4 * (p // 64) + c) * 64 + p % 64 for p in range(128)]) for c in range(4)])
    for l in range(L):
        def put(name, arr):
            o, x = WOFF[name]
            assert arr.shape == (128, x), (name, arr.shape, x)
            Wall[l, :, o:o + x] = arr
        for f, (gu, dn) in ((1, ("ffn1_w_gu", "ffn1_w_down")), (2, ("ffn2_w_gu", "ffn2_w_down"))):
            Wg = inp[gu][l]
            Wd = inp[dn][l]
            for b in range(8):
                cols = np.concatenate([256 * b + r(256), 2048 + 256 * b + r(256)])
                put("gu%d_%d" % (f, b), _tile_w(Wg, cols))
            for hf in range(2):
                for b in range(2):
                    put("dn%d_%d" % (f, 2 * hf + b), _tile_w(Wd[1024 * hf:1024 * (hf + 1)], 512 * b + r(512)))
        Wi = inp["w_in"][l]
        put("inXA", _tile_w(Wi, r(512)))
        put("inYA", _tile_w(Wi, 512 + r(512)))
        put("inQK", _tile_w(Wi, np.concatenate([qs_cols, ks0 + r(128)])))
        put("inV", _tile_w(Wi, np.concatenate([vs0 + r(128), vr0 + r(512)])))
        put("inRq", _tile_w(Wi, np.concatenate([qr0 + r(256), sw(qr0)])))
        put("inRk", _tile_w(Wi, np.concatenate([kr0 + r(256), sw(kr0)])))
        put("inGR", _tile_w(Wi, gr0 + r(512)))
        bd = np.zeros((128, 2, 4, 128), np.float32)
        for wi, nm in enumerate(("lru_w_a", "lru_w_x")):
            Wl = inp[nm][l]
            for j in range(4):
                for hb in range(2):
                    bd[64 * hb:64 * hb + 64, wi, j, 64 * hb:64 * hb + 64] = Wl[2 * j + hb]
        put("lru", bd.reshape(128, -1))
        Wa = inp["w_branch_a"][l]
        Wb = inp["w_branch_b"][l][ob_rows]
        Wc = inp["w_branch_c"][l]
        for m in range(8):
            g = _tile_w(Wi, np.concatenate([g0 + 128 * m + r(128), g0 + 1024 + 128 * m + r(128), g0 + 2048 + 128 * m + r(128)]))
            br = np.stack([_tile_w(Wx, 128 * m + r(128)).reshape(128, 4, 128) for Wx in (Wa, Wb, Wc)], axis=1)
            put("gb%d" % m, np.concatenate([g, br.reshape(128, -1)], axis=1))
        Wo = inp["w_out"][l]
        put("wo0", _tile_w(Wo, r(512)))
        put("wo1", _tile_w(Wo, 512 + r(512)))
    return Wall


def _fm(v):
    v = np.asarray(v, np.float32)
    F = v.shape[-1]
    a = v.reshape(v.shape[:-1] + (F // 128, 128))
    return np.ascontiguousarray(np.moveaxis(a, -1, 0))


NPAR = 3 * 8 + 16 + 4 * 5 + 4 + 4


def _host_params(inp):
    P = np.zeros((128, L, NPAR), np.float32)
    for l in range(L):
        o = 0
        for nm in ("ffn1_norm", "mix_norm", "ffn2_norm"):
            P[:, l, o:o + 8] = _fm(inp[nm][l])
            o += 8
        cw = _fm(inp["conv_w"][l])
        P[:, l, o:o + 16] = cw.transpose(0, 2, 1).reshape(128, 16)
        o += 16
        for nm in ("conv_b", "lru_b_a", "lru_b_x", "lru_lambda"):
            P[:, l, o:o + 4] = _fm(inp[nm][l])
            o += 4
        sk = inp["swa_sinks"][l]
        P[:, l, o:o + 4] = np.array([[sk[4 * (p // 64) + c] for c in range(4)] for p in range(128)], np.float32)
        o += 4
        P[:, l, o:o + 4] = _fm(inp["ret_norm"][l])
        o += 4
    return P, _fm(inp["final_norm"])


def _host_tables():
    T = {}
    p = np.arange(128)
    dd = p % 64
    inv = (10000.0 ** (-(np.arange(32, dtype=np.float32)) / 32)).astype(np.float32)
    pos = np.concatenate([np.arange(SEQP) - PADN, np.full(NS, SEQ)]).astype(np.float32)
    ang = (pos[None, :] * inv[dd % 32][:, None]).astype(np.float32).astype(np.float64)
    cos = np.cos(ang)
    sin = np.sin(ang) * np.where(dd < 32, -1.0, 1.0)[:, None]
    T["rope"] = np.stack([cos, sin, cos / 8, sin / 8], axis=1).astype(np.float32)
    lg = np.log1p(-np.exp2(-5.0 - np.arange(4, dtype=np.float32))).astype(np.float32).astype(np.float64)
    j = np.arange(128)[:, None]
    i = np.arange(128)[None, :]
    DT = np.zeros((128, 4, 128))
    for h in range(4):
        DT[:, h, :] = np.where(i >= j, np.exp((i - j) * lg[h]), 0.0)
    n = np.arange(128)
    hd = lambda ch: 2 * ch + p // 64
    cross = np.stack([np.exp((n[None, :] + 1.0) * lg[hd(ch)][:, None]) for ch in range(2)], axis=1)
    kdec = np.stack([np.exp((127.0 - n[None, :]) * lg[hd(ch)][:, None]) for ch in range(2)], axis=1)
    g128 = np.stack([np.exp(128.0 * lg[hd(ch)]) for ch in range(2)], axis=1)
    g1 = np.stack([np.exp(lg[hd(ch)]) for ch in range(2)], axis=1)
    pm = np.ones((128, 128))
    pm[:, :PADN] = 0.0
    T["f32"] = np.concatenate([DT.reshape(128, 512), cross.reshape(128, 256), kdec.reshape(128, 256), g128, g1, pm],
                              axis=1).astype(np.float32)
    mC = (j <= i)
    mP = (j > i)
    big = (j >= PADN)
    ms = [np.tile(m.astype(np.float32), (1, 4)) for m in (mC, mP, mC & big, mP & big)]
    T["masks"] = np.concatenate(ms, axis=1).astype(np.float32)
    ident = np.eye(128, dtype=np.float32)
    T["c128"] = np.concatenate([np.ones((128, 128)), np.full((128, 128), 1.0 / 128), ident], axis=1).astype(np.float32)
    bd = np.zeros((16, 16, 128), np.float32)
    for s in range(16):
        bd[s, s, :] = 1.0
    T["bdmask"] = bd.reshape(16, 2048)
    T["gam"] = [float(np.exp(lg[h])) for h in range(4)]
    return T


TABF = 512 + 256 + 256 + 2 + 2 + 128


def build(ntiles=17, do_sample=True, nlayers=L, dbg=False):
    import os
    STOP = int(os.environ.get("STOP", "9"))
    STOPC = int(os.environ.get("STOPC", "9"))
    STOPB = int(os.environ.get("STOPB", "9"))
    nc = bass.Bass("TRN2", target_bir_lowering=False)
    TB = _host_tables()
    GAM = TB["gam"]

    def din(name, shape, dt=F32):
        return nc.dram_tensor(name, list(shape), dt, kind="ExternalInput").ap()

    def dout(name, shape, dt=F32):
        return nc.dram_tensor(name, list(shape), dt, kind="ExternalOutput").ap()

    xin = din("xin", [128, KC, SEQP])
    xs_in = din("xs_in", [128, KC, NS])
    wall = [din("wall%d" % l_, [128, XTOT]) for l_ in range(L)]
    par_in = din("par", [128, L, NPAR])
    fn_in = din("fnorm", [128, KC])
    rope_in = din("rope", [128, 4, SEQP + NS])
    tab_in = din("tabf", [128, TABF])
    mask_in = din("masks", [128, 2048])
    c128_in = din("c128", [128, 384])
    ck_in = din("ck", [L, 128, NS, 128])
    cv_in = din("cv", [L, 128, NS, 128])
    cconv_in = din("cconv", [L, 128, 4, 3, NS])
    clru_in = din("clru", [L, 128, 4, NS])
    cret_in = din("cret", [L, 128, 2, NS, 128])

    wbf = [nc.dram_tensor("wbf%d" % l_, [128, XTOT], BF16, kind="Internal").ap() for l_ in range(L)]

    y_out = dout("y", [128, KC, SEQP])
    ys_out = dout("ys", [128, KC, NS])
    pk_out = dout("pk", [L, 128, 128])
    pv_out = dout("pv", [L, 128, 128])
    pconv_out = dout("pconv", [L, 128, 4, 3])
    plru_out = dout("plru", [L, 128, 4])
    pret_out = dout("pret", [L, 128, 2, 128])
    sk_out = dout("sk", [L, 128, NS, 127])
    sv_out = dout("sv", [L, NS, 127, 128])
    skn_out = dout("skn", [L, 128, NS])
    svn_out = dout("svn", [L, 128, NS])
    sconv_out = dout("sconv", [L, 128, 4, 3, NS])
    slru_out = dout("slru", [L, 128, 4, NS])
    sret_out = dout("sret", [L, 128, 2, NS, 128])
    if dbg:
        dbg_oa = dout("dbg_oa", [128, 4, 512], BF16)
        dbg_ob = dout("dbg_ob", [128, 4, 512], BF16)
        dbg_oc = dout("dbg_oc", [128, 4, 512], BF16)
        dbg_mg = dout("dbg_mg", [128, 8, 512], BF16)
        dbg_x = dout("dbg_x", [128, 8, 512])
        dbg_u = dout("dbg_u", [128, 8, 512], BF16)
        dbg_qr = dout("dbg_qr", [128, 2, 512], BF16)
        dbg_kr = dout("dbg_kr", [128, 2, 512], BF16)
        dbg_vr = dout("dbg_vr", [128, 4, 512], BF16)
        dbg_sgr = dout("dbg_sgr", [128, 4, 512], BF16)
        dbg_aT = dout("dbg_aT", [128, 512], BF16)
        dbg_gno = dout("dbg_gno", [128, 512], BF16)
        dbg_S = dout("dbg_S", [128, 2, 128])

    es = ExitStack()
    with es:
        def sb(name, shape, dt=F32):
            return es.enter_context(nc.sbuf_tensor(name, list(shape), dt))

        S = Sched(nc)
        xt = sb("xt", [128, KC, NTMAX])
        ub = sb("ub", [128, KC, NTMAX], BF16)
        hb = sb("hb", [128, 8, NTMAX], BF16)
        rstd = sb("rstd", [128, 512])
        wsl = [sb("wsl%d" % i, [128, WSLOT], BF16) for i in range(NWSLOT)]
        par = sb("par_t", [128, L, NPAR])
        fnorm = sb("fnormt", [128, KC])
        c1 = sb("c1", [128, L, 4])
        c2 = sb("c2", [128, L, 4])
        es_t = sb("es_t", [128, L, 4])
        ropet = sb("ropet", [128, 4, NTMAX])
        tabf = sb("tabf_t", [128, TABF])
        maskb = sb("maskb", [128, 2048], BF16)
        c128 = sb("c128t", [128, 384])
        onesb = sb("onesb", [128, 128], BF16)
        onesm = sb("onesm", [128, 128], BF16)
        identb = sb("identb", [128, 128], BF16)
        xaE = sb("xaE", [128, 3 + NTMAX])
        xcb = sb("xcb", [128, NTMAX], BF16)
        oa = sb("oa", [128, 4, NTMAX], BF16)
        ob = sb("ob", [128, 4, NTMAX], BF16)
        oc = sb("oc", [128, 4, NTMAX], BF16)
        mg = sb("mg", [128, KC, NTMAX], BF16)
        hst = sb("hst", [128, L, 4])
        cst = sb("cst", [128, L, 4, 3])
        qT = sb("qT", [128, 4, NTMAX], BF16)
        kTe = sb("kTe", [128, 128 + NTMAX], BF16)
        ve = sb("ve", [128, 5, 128], BF16)
        kst = sb("kst", [128, L, 128], BF16)
        vst = sb("vst", [128, L, 128], BF16)
        kof = sb("kof", [128, NTMAX])
        vof = sb("vof", [128, 128])
        eP = sb("eP", [128, 512], BF16)
        eC = sb("eC", [128, 512], BF16)
        qr = sb("qr", [128, 2, NTMAX], BF16)
        kr = sb("kr", [128, 2, NTMAX], BF16)
        vr = sb("vr", [128, 4, 512], BF16)
        sgr = sb("sgr", [128, 4, NTMAX], BF16)
        kdt = sb("kdt", [128, 2, 128], BF16)
        ktm4 = sb("ktm4", [128, 4, 256], BF16)
        qdc4 = sb("qdc4", [128, 4, 2, 128], BF16)
        aT4 = sb("aT4", [128, 4, 512], BF16)
        Sst = sb("Sst", [128, L, 2, 128])
        Sbf = sb("Sbf", [128, L, 2, 128], BF16)
        gno = sb("gno", [128, 512], BF16)
        gnq = sb("gnq", [128, 512], BF16)
        sga = sb("sga", [128, 512])
        sgb = sb("sgb", [128, 512])
        sgc = sb("sgc", [128, 512])
        m1 = sb("m1", [128, 512])
        m2 = sb("m2", [128, 512])
        rd = sb("rd", [128, 512])
        tmpa = sb("tmpa", [128, 512])
        gmean, gt1, gt2, gcen = sga, sgb, sgc, m1
        rt1, rt2 = m2, tmpa
        la_r, la_i, la_a, la_g, la_h, xc, gy = sga, sgb, sgc, m1, m2, rd, tmpa
        eP2, eC2 = eP, eC
        KEYALIAS = dict(gmean="sg0", gt1="sg1", gt2="sg2", gcen="m1", rt1="m2", rt2="tmpa", la_r="sg0", la_i="sg1", la_a="sg2",
                        la_g="m1", la_h="m2", xc="rd", gy="tmpa", eP1="eP0", eC1="eC0")
        S.alias = KEYALIAS
        if do_sample:
            kmod = sb("kmod", [128, NS, 128], BF16)
            vmod = sb("vmod", [128, NS, 128], BF16)
            cconv = sb("cconv_t", [128, 4, 4, NS])
            clru = sb("clru_t", [128, 4, NS])
            sret = sb("sret_t", [128, 2, NS, 128])
            qrf = sb("qrf", [128, 2, NS])
            krf = sb("krf", [128, 2, NS])
            prodf = sb("prodf", [128, 2, NS])
            vTf = sb("vTf", [128, 4, NS])
            vtmf = sb("vtmf", [NS, 512])
            vnT = sb("vnT", [128, NS])
            qTf = sb("qTf", [128, 4, NS])
            prodq = sb("prodq", [128, 4, NS])
            en = sb("en", [128, 64])
            ktmf = sb("ktmf", [NS, 256])
            vbd = sb("vbd", [NS, 4, 128])
            onesf = sb("onesf", [128, 128])
            identf = sb("identf", [128, 128])
            osm = sb("osm", [128, 64])
            eS = sb("eS", [128, 128], BF16)
            hnew = sb("hnew", [128, 4, NS])

        psb = [es.enter_context(nc.psum_tensor("psb%d" % i, [128, 512], F32)) for i in range(7)]
        pst = es.enter_context(nc.psum_tensor("pst", [128, 1024], BF16))
        psi = [0]

        def PS():
            i = psi[0] % 7
            psi[0] += 1
            return psb[i], ("ps", i)

        def act(out, in_, func, r, w, **kw):
            S.op("act", lambda e: e.activation(out=out, in_=in_, func=func, **kw), r, w)

        def tt(out, in0, in1, op, r, w, eng="dve"):
            S.op(eng, lambda e: e.tensor_tensor(out=out, in0=in0, in1=in1, op=op), r, w)

        def ts(out, in0, s1, s2, op0, op1, r, w, eng="dve"):
            if op1 is None:
                S.op(eng, lambda e: e.tensor_scalar(out=out, in0=in0, scalar1=s1, scalar2=None, op0=op0), r, w)
            else:
                S.op(eng, lambda e: e.tensor_scalar(out=out, in0=in0, scalar1=s1, scalar2=s2, op0=op0, op1=op1), r, w)

        def stt(out, in0, sc, in1, op0, op1, r, w):
            S.op("dve", lambda e: e.scalar_tensor_tensor(out=out, in0=in0, scalar=sc, in1=in1, op0=op0, op1=op1), r, w)

        def cp(out, in_, r, w, eng="dve"):
            S.op(eng, lambda e: e.tensor_copy(out=out, in_=in_), r, w)

        def mm(out, lhsT, rhs, start, stop, r, w):
            S.op("pe", lambda e: e.matmul(out, lhsT, rhs, start=start, stop=stop), r, w)

        def dma(out, in_, r, w, eng="pool", **kw):
            S.op(eng, lambda e: e.dma_start(out=out, in_=in_, **kw), r, w, dma=True)

        def recip(out, in_, r, w):
            S.op("dve", lambda e: e.reciprocal(out=out, in_=in_), r, w)

        def memset(ap, val, w, eng="dve"):
            S.op(eng, lambda e: e.memset(ap, val), (), w)

        dma(par[:], par_in, [], ["par"], eng="sp")
        dma(fnorm[:], fn_in, [], ["fnorm"], eng="sp")
        dma(tabf[:], tab_in, [], ["tabf"], eng="sp")
        dma(c128[:], c128_in, [], ["c128"], eng="sp")
        for q_ in range(4):
            dma(rt1[:], mask_in[:, q_ * 512:(q_ + 1) * 512], [], ["rt1"], eng="sp")
            cp(maskb[:, q_ * 512:(q_ + 1) * 512], rt1[:], ["rt1"], ["maskb"])
        cp(onesb[:], c128[:, 0:128], ["c128"], ["onesb"])
        cp(onesm[:], c128[:, 128:256], ["c128"], ["onesm"])
        cp(identb[:], c128[:, 256:384], ["c128"], ["identb"])
        if do_sample:
            cp(onesf[:], c128[:, 0:128], ["c128"], ["onesf"])
            cp(identf[:], c128[:, 256:384], ["c128"], ["identf"])
        CH = 8192
        for l in range(nlayers):
            for o in range(0, XTOT, CH):
                n = min(CH, XTOT - o)
                dma(wbf[l][:, o:o + n], wall[l][:, o:o + n], [], [("wbf", l, o // CH), ("castslot", (o // CH) % 6)], eng="pool", max_dma_last_dim=8192)
        PO = dict(norm=0, cw=24, cb=40, ba=44, bx=48, lam=52, sk=56, rg=60)
        for l in range(nlayers):
            act(c1[:, l, :], par[:, l, 52:56], AF.Exp, ["par"], ["c1"], scale=-1.0)
            act(c1[:, l, :], c1[:, l, :], AF.Ln, ["c1"], ["c1"], bias=1.0)
            ts(c2[:, l, :], c1[:, l, :], -16.0, None, ALU.mult, None, ["c1"], ["c2"])
            ts(c1[:, l, :], c1[:, l, :], -8.0, None, ALU.mult, None, ["c1"], ["c1"])
            act(es_t[:, l, :], par[:, l, 56:60], AF.Exp, ["par"], ["es"])
        memset(hst[:], 0.0, ["hst"])
        memset(cst[:], 0.0, ["cst"])
        memset(kst[:], 0.0, ["kst"])
        memset(vst[:], 0.0, ["vst"])
        memset(Sst[:], 0.0, ["Sst"])
        memset(Sbf[:], 0.0, ["Sbf"])

        wctr = [0]

        def wload(l, name):
            o, x = WOFF[name]
            i = wctr[0] % NWSLOT
            wctr[0] += 1
            key = ("wsl", i)
            r = [("wbf", l, c) for c in range(o // CH, (o + x - 1) // CH + 1)]
            dma(wsl[i][:, 0:x], wbf[l][:, o:o + x], r, [key], eng="sp")
            return wsl[i], key

        def rmsnorm(groups, gvec, out_t, outkey, okeys_extra=()):
            for (s0, n) in groups:
                for kc in range(KC):
                    act(hb[:, kc, 0:n], xt[:, kc, s0:s0 + n], AF.Square, ["xt"], [("hb", kc)])
                p, pk = PS()
                for kc in range(KC):
                    mm(p[:, 0:n], onesb[:], hb[:, kc, 0:n], kc == 0, kc == KC - 1, [("hb", kc), "onesb"], [pk])
                act(rstd[:, 0:n], p[:, 0:n], AF.Sqrt, [pk], ["rstd"], scale=1.0 / D, bias=EPS)
                recip(rstd[:, 0:n], rstd[:, 0:n], ["rstd"], ["rstd"])
                for kc in range(KC):
                    stt(out_t[:, kc, s0:s0 + n], xt[:, kc, s0:s0 + n], gvec(kc), rstd[:, 0:n], ALU.mult, ALU.mult,
                        ["xt", "rstd", "par", "fnorm"], [outkey])

        def ffn(l, which, groups):
            f = which + 1
            gsel = 0 if which == 0 else 2
            rmsnorm(groups, lambda kc: par[:, l, gsel * 8 + kc:gsel * 8 + kc + 1], ub, "ub")
            for hf in range(2):
                for b in range(4):
                    wt, wk = wload(l, "gu%d_%d" % (f, 4 * hf + b))
                    w3 = wt[:, 0:4096].rearrange("p (k c) -> p k c", k=KC)
                    for (s0, n) in groups:
                        for fc in range(2):
                            pg, pgk = PS()
                            pu, puk = PS()
                            for kc in range(KC):
                                mm(pg[:, 0:n], w3[:, kc, fc * 128:(fc + 1) * 128], ub[:, kc, s0:s0 + n], kc == 0, kc == KC - 1,
                                   [wk, "ub"], [pgk])
                            for kc in range(KC):
                                mm(pu[:, 0:n], w3[:, kc, 256 + fc * 128:256 + (fc + 1) * 128], ub[:, kc, s0:s0 + n], kc == 0,
                                   kc == KC - 1, [wk, "ub"], [puk])
                            act(tmpa[:, 0:n], pg[:, 0:n], AF.Silu, [pgk], ["tmpa"])
                            tt(hb[:, 2 * b + fc, s0:s0 + n], tmpa[:, 0:n], pu[:, 0:n], ALU.mult, ["tmpa", puk], [("hb", 2 * b + fc)])
                for b in range(2):
                    wt, wk = wload(l, "dn%d_%d" % (f, 2 * hf + b))
                    w3 = wt[:, 0:4096].rearrange("p (k c) -> p k c", k=KC)
                    for (s0, n) in groups:
                        for mc in range(4):
                            p, pk = PS()
                            for fc in range(8):
                                mm(p[:, 0:n], w3[:, fc, mc * 128:(mc + 1) * 128], hb[:, fc, s0:s0 + n], fc == 0, fc == 7,
                                   [wk, ("hb", fc)], [pk])
                            m = 4 * b + mc
                            stt(xt[:, m, s0:s0 + n], p[:, 0:n], 0.5, xt[:, m, s0:s0 + n], ALU.mult, ALU.add, [pk, "xt"], ["xt"])

        def groupnorm_gate(l, osrc, okey, ncols, sg_ap, out_ap, h_of_col):
            act(gno[:, 0:ncols], osrc, AF.Copy, [okey], ["gno"])
            act(gnq[:, 0:ncols], osrc, AF.Square, [okey], ["gnq"])
            pm_, pmk = PS()
            pq_, pqk = PS()
            mm(pm_[:, 0:ncols], onesm[:], gno[:, 0:ncols], True, True, ["gno", "onesm"], [pmk])
            mm(pq_[:, 0:ncols], onesm[:], gnq[:, 0:ncols], True, True, ["gnq", "onesm"], [pqk])
            act(gmean[:, 0:ncols], pm_[:, 0:ncols], AF.Copy, [pmk], ["gmean"])
            tt(gt1[:, 0:ncols], gmean[:, 0:ncols], gmean[:, 0:ncols], ALU.mult, ["gmean"], ["gt1"])
            tt(gt1[:, 0:ncols], pq_[:, 0:ncols], gt1[:, 0:ncols], ALU.subtract, [pqk, "gt1"], ["gt1"])
            ts(gt1[:, 0:ncols], gt1[:, 0:ncols], 0.0, None, ALU.max, None, ["gt1"], ["gt1"])
            act(gt2[:, 0:ncols], gt1[:, 0:ncols], AF.Sqrt, ["gt1"], ["gt2"], bias=GN_EPS)
            recip(gt2[:, 0:ncols], gt2[:, 0:ncols], ["gt2"], ["gt2"])
            tt(gcen[:, 0:ncols], osrc, gmean[:, 0:ncols], ALU.subtract, [okey, "gmean"], ["gcen"])
            tt(gcen[:, 0:ncols], gcen[:, 0:ncols], gt2[:, 0:ncols], ALU.mult, ["gcen", "gt2"], ["gcen"])
            h_of_col(gcen, sg_ap, out_ap)

        def mixer(l, groups, nt, tile_idx, blk0, nb, sample):
            last = (not sample) and ((blk0 + nb == NBLK) or (os.environ.get("LASTDBG") and tile_idx == ntiles - 1))
            rmsnorm(groups, lambda kc: par[:, l, 8 + kc:8 + kc + 1], ub, "ub")
            wA, wAk = wload(l, "inXA")
            wA3 = wA[:, 0:4096].rearrange("p (k c) -> p k c", k=KC)
            wY, wYk = wload(l, "inYA")
            wY3 = wY[:, 0:4096].rearrange("p (k c) -> p k c", k=KC)
            wLt, wLk = wload(l, "lru")
            wL = wLt[:, 0:1024].rearrange("p (w j c) -> p w j c", w=2, j=4)
            for j in range(4):
                if sample:
                    tap = lambda t: cconv[:, j, t, :]
                    newx = cconv[:, j, 3, :]
                    xkey = "cconv"
                else:
                    tap = lambda t: xaE[:, t:t + nt]
                    newx = None
                    xkey = "xaE"
                    cp(xaE[:, 0:3], cst[:, l, j, :], ["cst"], ["xaE"])
                for (s0, n) in groups:
                    p, pk = PS()
                    for kc in range(KC):
                        mm(p[:, 0:n], wA3[:, kc, j * 128:(j + 1) * 128], ub[:, kc, s0:s0 + n], kc == 0, kc == KC - 1, [wAk, "ub"], [pk])
                    dst = newx if sample else xaE[:, 3 + s0:3 + s0 + n]
                    act(dst, p[:, 0:n], AF.Copy, [pk], [xkey])
                    p2, p2k = PS()
                    for kc in range(KC):
                        mm(p2[:, 0:n], wY3[:, kc, j * 128:(j + 1) * 128], ub[:, kc, s0:s0 + n], kc == 0, kc == KC - 1,
                           [wYk, "ub"], [p2k])
                    act(gy[:, s0:s0 + n], p2[:, 0:n], AF.Gelu_apprx_tanh, [p2k], ["gy"])
                cwb = 24 + 4 * j
                ts(xc[:, 0:nt], tap(0), par[:, l, cwb:cwb + 1], par[:, l, 40 + j:41 + j], ALU.mult, ALU.add, [xkey, "par"], ["xc"])
                for t in (1, 2, 3):
                    stt(xc[:, 0:nt], tap(t), par[:, l, cwb + t:cwb + t + 1], xc[:, 0:nt], ALU.mult, ALU.add, [xkey, "par", "xc"], ["xc"])
                act(xcb[:, 0:nt], xc[:, 0:nt], AF.Copy, ["xc"], ["xcb"])
                if sample:
                    dma(sconv_out[l, :, j, :, :], cconv[:, j, 1:4, :], ["cconv"], [])
                else:
                    cp(cst[:, l, j, :], xaE[:, nt:nt + 3], ["xaE"], ["cst"])
                for (s0, n) in groups:
                    pr, prk = PS()
                    mm(pr[:, 0:n], wL[:, 0, j, :], xcb[:, s0:s0 + n], True, True, [wLk, "xcb"], [prk])
                    pi_, pik = PS()
                    mm(pi_[:, 0:n], wL[:, 1, j, :], xcb[:, s0:s0 + n], True, True, [wLk, "xcb"], [pik])
                    act(la_r[:, s0:s0 + n], pr[:, 0:n], AF.Sigmoid, [prk, "par"], ["la_r"], bias=par[:, l, 44 + j:45 + j])
                    act(la_i[:, s0:s0 + n], pi_[:, 0:n], AF.Sigmoid, [pik, "par"], ["la_i"], bias=par[:, l, 48 + j:49 + j])
                act(la_a[:, 0:nt], la_r[:, 0:nt], AF.Exp, ["la_r", "c1"], ["la_a"], scale=c1[:, l, j:j + 1])
                act(la_g[:, 0:nt], la_r[:, 0:nt], AF.Exp, ["la_r", "c2"], ["la_g"], scale=c2[:, l, j:j + 1])
                act(la_g[:, 0:nt], la_g[:, 0:nt], AF.Sqrt, ["la_g"], ["la_g"], scale=-1.0, bias=1.0)
                tt(la_g[:, 0:nt], la_g[:, 0:nt], la_i[:, 0:nt], ALU.mult, ["la_g", "la_i"], ["la_g"])
                tt(la_g[:, 0:nt], la_g[:, 0:nt], xc[:, 0:nt], ALU.mult, ["la_g", "xc"], ["la_g"])
                if sample:
                    tt(la_h[:, 0:nt], la_a[:, 0:nt], clru[:, j, :], ALU.mult, ["la_a", "clru"], ["la_h"])
                    tt(hnew[:, j, :], la_h[:, 0:nt], la_g[:, 0:nt], ALU.add, ["la_h", "la_g"], ["hnew"])
                    tt(oa[:, j, 0:nt], hnew[:, j, :], gy[:, 0:nt], ALU.mult, ["hnew", "gy"], ["oa"])
                else:
                    if tile_idx == 0:
                        tt(la_g[:, 0:nt], la_g[:, 0:nt], tabf[:, 1028:1028 + nt], ALU.mult, ["la_g", "tabf"], ["la_g"])
                    S.op("dve", lambda e, j=j: e.tensor_tensor_scan(out=la_h[:, 0:nt], data0=la_a[:, 0:nt], data1=la_g[:, 0:nt],
                                                                   initial=hst[:, l, j:j + 1], op0=ALU.mult, op1=ALU.add),
                         ["la_a", "la_g", "hst"], ["la_h"])
                    cp(hst[:, l, j:j + 1], la_h[:, nt - 1:nt], ["la_h"], ["hst"])
                    tt(oa[:, j, 0:nt], la_h[:, 0:nt], gy[:, 0:nt], ALU.mult, ["la_h", "gy"], ["oa"])
            if sample:
                dma(slru_out[l], hnew[:], ["hnew"], [])
            if last:
                dma(pconv_out[l], cst[:, l, :, :], ["cst"], [])
                dma(plru_out[l], hst[:, l, :], ["hst"], [])

            if STOP < 3:
                return
            wQ, wQk = wload(l, "inQK")
            wQ3 = wQ[:, 0:5120].rearrange("p (k c) -> p k c", k=KC)
            wV, wVk = wload(l, "inV")
            wV3 = wV[:, 0:5120].rearrange("p (k c) -> p k c", k=KC)
            for (s0, n) in groups:
                for c in range(4):
                    p, pk = PS()
                    for kc in range(KC):
                        mm(p[:, 0:n], wQ3[:, kc, c * 128:(c + 1) * 128], ub[:, kc, s0:s0 + n], kc == 0, kc == KC - 1, [wQk, "ub"], [pk])
                    act(qT[:, c, s0:s0 + n], p[:, 0:n], AF.Copy, [pk], ["qT"], scale=0.125)
                p, pk = PS()
                for kc in range(KC):
                    mm(p[:, 0:n], wQ3[:, kc, 512:640], ub[:, kc, s0:s0 + n], kc == 0, kc == KC - 1, [wQk, "ub"], [pk])
                if sample:
                    act(kof[:, 0:n], p[:, 0:n], AF.Copy, [pk], ["kof"])
                else:
                    act(kTe[:, 128 + s0:128 + s0 + n], p[:, 0:n], AF.Copy, [pk], ["kTe"])
                    if last:
                        act(kof[:, s0:s0 + n], p[:, 0:n], AF.Copy, [pk], ["kof"])
            if sample:
                pv_, pvk = PS()
                for kc in range(KC):
                    mm(pv_[:, 0:NS], wV3[:, kc, 0:128], ub[:, kc, 0:NS], kc == 0, kc == KC - 1, [wVk, "ub"], [pvk])
                act(vnT[:], pv_[:, 0:NS], AF.Copy, [pvk], ["vnT"])
                dma(kmod[:], ck_in[l], [], ["kmod"])
                dma(vmod[:], cv_in[l], [], ["vmod"])
                dma(sk_out[l], ck_in[l, :, :, 1:128], [], [])
                dma(skn_out[l], kof[:, 0:NS], ["kof"], [])
                dma(sv_out[l], cv_in[l, 1:128, :, :].rearrange("k s f -> s k f"), [], [])
                dma(svn_out[l], vnT[:], ["vnT"], [])
                if STOPB < 1:
                    return
                pSb = [PS(), PS()]
                for s in range(NS):
                    for g in range(2):
                        pS_, pSk = pSb[g]
                        mm(pS_[:, s * 4:s * 4 + 4], kmod[64 * g:64 * g + 64, s, :], qT[64 * g:64 * g + 64, :, s], True, True,
                           ["kmod", "qT"], [pSk])
                for g in range(2):
                    act(eS[:, g * 64:(g + 1) * 64], pSb[g][0][:, 0:64], AF.Exp, [pSb[g][1]], ["eS"])
                memset(eS[0:1, :], 0.0, ["eS"])
                if STOPB < 2:
                    return
                pO, pOk = PS()
                pD, pDk = PS()
                for s in range(NS):
                    for g in range(2):
                        c0 = g * 64 + s * 4
                        mm(pO[64 * g:64 * g + 64, s * 4:s * 4 + 4], vmod[:, s, 64 * g:64 * g + 64], eS[:, c0:c0 + 4], True, True,
                           ["vmod", "eS"], [pOk])
                        mm(pD[64 * g:64 * g + 64, s * 4:s * 4 + 4], onesb[:, 0:64], eS[:, c0:c0 + 4], True, True,
                           ["onesb", "eS"], [pDk])
                if STOPB < 3:
                    return
                cp(qTf[:], qT[:, :, 0:NS], ["qT"], ["qTf"])
                tt(prodq[:], qTf[:], kof[:, 0:NS].unsqueeze(1).to_broadcast([128, 4, NS]), ALU.mult, ["qTf", "kof"], ["prodq"])
                for g in range(2):
                    gs = slice(64 * g, 64 * g + 64)
                    pN, pNk = PS()
                    mm(pN[gs, 0:64], onesf[gs, 0:64], prodq[gs, :, :].rearrange("p c s -> p (c s)"), True, True, ["onesf", "prodq"], [pNk])
                    act(en[gs, :], pN[gs, 0:64], AF.Exp, [pNk], ["en"])
                if STOPB < 4:
                    return
                en3 = en[:].rearrange("p (c s) -> p s c", c=4)
                rd3 = rd[:, 0:64].rearrange("p (s c) -> p s c", c=4)
                nm3 = m1[:, 0:64].rearrange("p (s c) -> p s c", c=4)
                pD3 = pD[:, 0:64].rearrange("p (s c) -> p s c", c=4)
                pO3 = pO[:, 0:64].rearrange("p (s c) -> p s c", c=4)
                tt(nm3, en3, vnT[:].unsqueeze(2).to_broadcast([128, NS, 4]), ALU.mult, ["en", "vnT"], ["m1"])
                tt(nm3, nm3, pO3, ALU.add, ["m1", pOk], ["m1"])
                tt(rd3, en3, pD3, ALU.add, ["en", pDk], ["rd"])
                for c in range(4):
                    ts(rd3[:, :, c], rd3[:, :, c], es_t[:, l, c:c + 1], None, ALU.add, None, ["rd", "es"], ["rd"])
                recip(rd[:, 0:64], rd[:, 0:64], ["rd"], ["rd"])
                tt(ob[:, :, 0:NS], m1[:, 0:64].rearrange("p (s c) -> p c s", c=4), rd[:, 0:64].rearrange("p (s c) -> p c s", c=4),
                   ALU.mult, ["m1", "rd"], ["ob"])
            else:
                cp(kTe[:, 0:128], kst[:, l, :], ["kst"], ["kTe"])
                cp(ve[:, 0, :], vst[:, l, :], ["vst"], ["ve"])
                for b in range(nb):
                    pv_, pvk = PS()
                    for kc in range(KC):
                        mm(pv_[:, 0:128], ub[:, kc, b * 128:(b + 1) * 128], wV3[:, kc, 0:128], kc == 0, kc == KC - 1, [wVk, "ub"], [pvk])
                    act(ve[:, 1 + b, :], pv_[:, 0:128], AF.Copy, [pvk], ["ve"])
                    if last and b == nb - 1:
                        act(vof[:], pv_[:, 0:128], AF.Copy, [pvk], ["vof"])
                cp(kst[:, l, :], kTe[:, nt:nt + 128], ["kTe"], ["kst"])
                cp(vst[:, l, :], ve[:, nb, :], ["ve"], ["vst"])
                if last:
                    dma(pk_out[l], kof[:, nt - 128:nt], ["kof"], [])
                    dma(pv_out[l], vof[:], ["vof"], [])
                for b in range(nb):
                    gb_ = blk0 + b
                    pO, pOk = PS()
                    pD, pDk = PS()
                    for g in range(2):
                        gs = slice(64 * g, 64 * g + 64)
                        e_p, e_c = (eP, eC) if g == 0 else (eP2, eC2)
                        kp, kc_ = ("eP%d" % g, "eC%d" % g)
                        pC, pCk = PS()
                        for c in range(4):
                            mm(pC[:, c * 128:(c + 1) * 128], kTe[gs, 128 + b * 128:128 + (b + 1) * 128], qT[gs, c, b * 128:(b + 1) * 128],
                               True, True, ["kTe", "qT"], [pCk])
                        act(e_c[:], pC[:], AF.Exp, [pCk], [kc_])
                        mo = 1024 if gb_ == 0 else 0
                        tt(e_c[:], e_c[:], maskb[:, mo:mo + 512], ALU.mult, [kc_, "maskb"], [kc_])
                        if gb_ > 0:
                            pP, pPk = PS()
                            for c in range(4):
                                mm(pP[:, c * 128:(c + 1) * 128], kTe[gs, b * 128:(b + 1) * 128], qT[gs, c, b * 128:(b + 1) * 128],
                                   True, True, ["kTe", "qT"], [pPk])
                            act(e_p[:], pP[:], AF.Exp, [pPk], [kp])
                            mo = 1536 if gb_ == 1 else 512
                            tt(e_p[:], e_p[:], maskb[:, mo:mo + 512], ALU.mult, [kp, "maskb"], [kp])
                            mm(pO[gs, :], ve[:, b, gs], e_p[:], True, False, ["ve", kp], [pOk])
                            mm(pD[gs, :], onesb[:, 0:64], e_p[:], True, False, ["onesb", kp], [pDk])
                        mm(pO[gs, :], ve[:, b + 1, gs], e_c[:], gb_ == 0, True, ["ve", kc_], [pOk])
                        mm(pD[gs, :], onesb[:, 0:64], e_c[:], gb_ == 0, True, ["onesb", kc_], [pDk])
                    for c in range(4):
                        ts(rd[:, c * 128:(c + 1) * 128], pD[:, c * 128:(c + 1) * 128], es_t[:, l, c:c + 1], None, ALU.add, None,
                           [pDk, "es"], ["rd"])
                    recip(rd[:], rd[:], ["rd"], ["rd"])
                    tt(ob[:, :, b * 128:(b + 1) * 128], pO[:].rearrange("p (c q) -> p c q", c=4), rd[:].rearrange("p (c q) -> p c q", c=4),
                       ALU.mult, [pOk, "rd"], ["ob"])

            if STOP < 4:
                return
            if sample:
                for h in range(4):
                    p, pk = PS()
                    for kc in range(KC):
                        mm(p[:, 0:NS], wV3[:, kc, 128 + h * 128:128 + (h + 1) * 128], ub[:, kc, 0:NS], kc == 0, kc == KC - 1,
                           [wVk, "ub"], [pk])
                    act(vTf[:, h, :], p[:, 0:NS], AF.Copy, [pk], ["vTf"])
                pvr, pvrk = PS()
                for kc in range(KC):
                    mm(pvr[0:NS, :], ub[:, kc, 0:NS], wV3[:, kc, 128:640], kc == 0, kc == KC - 1, [wVk, "ub"], [pvrk])
                cp(vtmf[:, :], pvr[0:NS, :], [pvrk], ["vtmf2"])
            else:
                for b in range(nb):
                    pvr, pvrk = PS()
                    for kc in range(KC):
                        mm(pvr[:], ub[:, kc, b * 128:(b + 1) * 128], wV3[:, kc, 128:640], kc == 0, kc == KC - 1, [wVk, "ub"], [pvrk])
                    act(vr[:, b, :], pvr[:], AF.Copy, [pvrk], [("vr", b)])
            wRq, wRqk = wload(l, "inRq")
            wRk_, wRkk = wload(l, "inRk")
            wR3s = (wRq[:, 0:4096].rearrange("p (k c) -> p k c", k=KC), wRk_[:, 0:4096].rearrange("p (k c) -> p k c", k=KC))
            wRks = (wRqk, wRkk)
            wG, wGk = wload(l, "inGR")
            wG3 = wG[:, 0:4096].rearrange("p (k c) -> p k c", k=KC)
            for (s0, n) in groups:
                for qk in range(2):
                    for ch in range(2):
                        pn, pnk = PS()
                        psw, pswk = PS()
                        cb_ = ch * 128
                        wR3 = wR3s[qk]
                        wRk = wRks[qk]
                        for kc in range(KC):
                            mm(pn[:, 0:n], wR3[:, kc, cb_:cb_ + 128], ub[:, kc, s0:s0 + n], kc == 0, kc == KC - 1, [wRk, "ub"], [pnk])
                        for kc in range(KC):
                            mm(psw[:, 0:n], wR3[:, kc, cb_ + 256:cb_ + 384], ub[:, kc, s0:s0 + n], kc == 0, kc == KC - 1, [wRk, "ub"], [pswk])
                        tb = 2 * qk
                        tt(rt1[:, 0:n], pn[:, 0:n], ropet[:, tb, s0:s0 + n], ALU.mult, [pnk, "ropet"], ["rt1"])
                        tt(rt2[:, 0:n], psw[:, 0:n], ropet[:, tb + 1, s0:s0 + n], ALU.mult, [pswk, "ropet"], ["rt2"])
                        dst = (qr, kr)[qk]
                        tt(dst[:, ch, s0:s0 + n], rt1[:, 0:n], rt2[:, 0:n], ALU.add, ["rt1", "rt2"], [("qr", "kr")[qk]])
                        if sample:
                            dstf = (qrf, krf)[qk]
                            tt(dstf[:, ch, :], rt1[:, 0:n], rt2[:, 0:n], ALU.add, ["rt1", "rt2"], [("qrf", "krf")[qk]])
                for h in range(4):
                    p, pk = PS()
                    for kc in range(KC):
                        mm(p[:, 0:n], wG3[:, kc, h * 128:(h + 1) * 128], ub[:, kc, s0:s0 + n], kc == 0, kc == KC - 1, [wGk, "ub"], [pk])
                    act(sgr[:, h, s0:s0 + n], p[:, 0:n], AF.Silu, [pk], ["sgr"])

            def gate_cols(gcen_t, sg_ap_fn, out_fn, ncols_per_h, heads=(0, 1, 2, 3)):
                for hi, h in enumerate(heads):
                    cs = slice(hi * ncols_per_h, (hi + 1) * ncols_per_h)
                    stt(out_fn(h), gcen_t[:, cs], par[:, l, 60 + h:61 + h], sg_ap_fn(h), ALU.mult, ALU.mult,
                        ["gcen", "par", "sgr"], ["oc"])

            if sample:
                dma(sret[:], cret_in[l], [], ["sret"])
                pQb = [PS(), PS()]
                for h in range(4):
                    hs = slice(64 * (h % 2), 64 * (h % 2) + 64)
                    pQ, pQk = pQb[h % 2]
                    for s in range(NS):
                        c_ = (h // 2) * NS + s
                        mm(pQ[:, c_:c_ + 1], sret[hs, h // 2, s, :], qrf[hs, h // 2, s:s + 1], True, True,
                           ["sret", "qrf"], [pQk])
                tt(prodf[:], qrf[:], krf[:], ALU.mult, ["qrf", "krf"], ["prodf"])
                pDb = [PS(), PS()]
                for h in range(4):
                    hs = slice(64 * (h % 2), 64 * (h % 2) + 64)
                    pDt, pDtk = pDb[h % 2]
                    mm(pDt[:, (h // 2) * NS:(h // 2 + 1) * NS], onesf[hs, :], prodf[hs, h // 2, :], True, True, ["onesf", "prodf"], [pDtk])
                for h in range(4):
                    pDt, pDtk = pDb[h % 2]
                    pQ, pQk = pQb[h % 2]
                    c_ = slice((h // 2) * NS, (h // 2 + 1) * NS)
                    tt(osm[:, h * NS:(h + 1) * NS], pDt[:, c_], vTf[:, h, :], ALU.mult, [pDtk, "vTf"], ["osm"])
                    stt(osm[:, h * NS:(h + 1) * NS], pQ[:, c_], GAM[h], osm[:, h * NS:(h + 1) * NS], ALU.mult, ALU.add,
                        [pQk, "osm"], ["osm"])
                groupnorm_gate(l, osm[:], "osm", 64, None, None,
                               lambda g_, a_, b_: gate_cols(g_, lambda h: sgr[:, h, 0:NS], lambda h: oc[:, h, 0:NS], NS))
                for ch in range(2):
                    ptr, ptrk = PS()
                    S.op("pe", lambda e, ch=ch, ptr=ptr: e.transpose(ptr[0:NS, 0:128], krf[:, ch, :], identf[:]), ["krf", "identf"], [ptrk])
                    cp(ktmf[:, ch * 128:(ch + 1) * 128], ptr[0:NS, 0:128], [ptrk], ["ktmf"])
                for h in range(4):
                    hs = slice(64 * (h % 2), 64 * (h % 2) + 64)
                    ch = h // 2
                    for q in range(4):
                        tt(vbd[:], vtmf[:, h * 128:(h + 1) * 128].unsqueeze(1).to_broadcast([NS, 4, 128]),
                           identf[0:NS, 4 * q:4 * q + 4].unsqueeze(2).to_broadcast([NS, 4, 128]), ALU.mult, ["vtmf2", "identf"], ["vbd"])
                        pk_, pkk = PS()
                        mm(pk_[hs, :], ktmf[:, h * 64:(h + 1) * 64], vbd[:].rearrange("p s e -> p (s e)"), True, True,
                           ["ktmf", "vbd"], [pkk])
                        stt(sret[hs, ch, 4 * q:4 * q + 4, :], sret[hs, ch, 4 * q:4 * q + 4, :], GAM[h],
                            pk_[hs, :].rearrange("p (s e) -> p s e", s=4), ALU.mult, ALU.add, ["sret", pkk], ["sret"])
                dma(sret_out[l], sret[:], ["sret"], [])
            else:
                for b in range(nb):
                    bs = slice(b * 128, (b + 1) * 128)
                    for ch in range(2):
                        tt(kdt[:, ch, :], kr[:, ch, bs], tabf[:, 768 + ch * 128:768 + (ch + 1) * 128], ALU.mult, ["kr", "tabf"], ["kdt"])
                        tt(qdc4[:, b, ch, :], qr[:, ch, bs], tabf[:, 512 + ch * 128:512 + (ch + 1) * 128], ALU.mult, ["qr", "tabf"], [("qdc", b)])
                    for ch in range(2):
                        S.op("pe", lambda e, ch=ch: e.transpose(pst[:, ch * 128:(ch + 1) * 128], kdt[:, ch, :], identb[:]),
                             ["kdt", "identb"], ["pst"])
                    cp(ktm4[:, b, :], pst[:, 0:256], ["pst"], [("ktm", b)])
                    pStb = [PS(), PS()]
                    for h in range(4):
                        hs = slice(64 * (h % 2), 64 * (h % 2) + 64)
                        pSt, pStk = pStb[h % 2]
                        mm(pSt[:, (h // 2) * 128:(h // 2 + 1) * 128], kr[hs, h // 2, bs], qr[hs, h // 2, bs], True, True, ["kr", "qr"], [pStk])
                    for h in range(4):
                        pSt, pStk = pStb[h % 2]
                        tt(aT4[:, b, h * 128:(h + 1) * 128], pSt[:, (h // 2) * 128:(h // 2 + 1) * 128], tabf[:, h * 128:(h + 1) * 128], ALU.mult,
                           [pStk, "tabf"], [("aT", b)])
                for b in range(nb):
                    bs = slice(b * 128, (b + 1) * 128)
                    pOb = [PS(), PS()]
                    for h in range(4):
                        hs = slice(64 * (h % 2), 64 * (h % 2) + 64)
                        pO, pOk = pOb[h % 2]
                        oc_ = slice((h // 2) * 128, (h // 2 + 1) * 128)
                        mm(pO[:, oc_], vr[:, b, h * 128:(h + 1) * 128], aT4[:, b, h * 128:(h + 1) * 128], True, False,
                           [("vr", b), ("aT", b)], [pOk])
                        mm(pO[:, oc_], Sbf[hs, l, h // 2, :], qdc4[hs, b, h // 2, :], False, True, ["Sbf", ("qdc", b)], [pOk])
                    pU, pUk = PS()
                    for h in range(4):
                        hs = slice(64 * (h % 2), 64 * (h % 2) + 64)
                        mm(pU[hs, (h // 2) * 128:(h // 2 + 1) * 128], ktm4[:, b, h * 64:(h + 1) * 64], vr[:, b, h * 128:(h + 1) * 128], True, True,
                           [("ktm", b), ("vr", b)], [pUk])
                    for ch in range(2):
                        stt(Sst[:, l, ch, :], Sst[:, l, ch, :], tabf[:, 1024 + ch:1025 + ch], pU[:, ch * 128:(ch + 1) * 128], ALU.mult, ALU.add,
                            ["Sst", "tabf", pUk], ["Sst"])
                    act(Sbf[:, l, :, :], Sst[:, l, :, :], AF.Copy, ["Sst"], ["Sbf"])
                    for r_ in range(2):
                        pO, pOk = pOb[r_]
                        groupnorm_gate(l, pO[:, 0:256], pOk, 256, None, None,
                                       lambda g_, a_, b_, bs=bs, r_=r_: gate_cols(g_, lambda h: sgr[:, h, bs], lambda h: oc[:, h, bs], 128,
                                                                                 heads=(r_, r_ + 2)))
                if last:
                    dma(pret_out[l], Sst[:, l, :, :], ["Sst"], [])

            if STOP < 5:
                return
            for m in range(8):
                wt, wk = wload(l, "gb%d" % m)
                wg3 = wt[:, 0:3072].rearrange("p (k c) -> p k c", k=KC)
                wb4 = wt[:, 3072:4608].rearrange("p (b k c) -> p b k c", b=3, k=4)
                for (s0, n) in groups:
                    sgs = (sga, sgb, sgc)
                    for bi in range(3):
                        p, pk = PS()
                        for kc in range(KC):
                            mm(p[:, 0:n], wg3[:, kc, bi * 128:(bi + 1) * 128], ub[:, kc, s0:s0 + n], kc == 0, kc == KC - 1, [wk, "ub"], [pk])
                        act(sgs[bi][:, 0:n], p[:, 0:n], AF.Sigmoid, [pk], ["sg%d" % bi])
                    srcs = ((oa, "oa"), (ob, "ob"), (oc, "oc"))
                    pks = []
                    for bi in range(3):
                        p, pk = PS()
                        for kc in range(4):
                            mm(p[:, 0:n], wb4[:, bi, kc, :], srcs[bi][0][:, kc, s0:s0 + n], kc == 0, kc == 3, [wk, srcs[bi][1]], [pk])
                        pks.append((p, pk))
                    tt(m1[:, 0:n], sga[:, 0:n], pks[0][0][:, 0:n], ALU.mult, ["sg0", pks[0][1]], ["m1"])
                    tt(m2[:, 0:n], sgb[:, 0:n], pks[1][0][:, 0:n], ALU.mult, ["sg1", pks[1][1]], ["m2"])
                    tt(m1[:, 0:n], m1[:, 0:n], m2[:, 0:n], ALU.add, ["m1", "m2"], ["m1"])
                    tt(m2[:, 0:n], sgc[:, 0:n], pks[2][0][:, 0:n], ALU.mult, ["sg2", pks[2][1]], ["m2"])
                    tt(mg[:, m, s0:s0 + n], m1[:, 0:n], m2[:, 0:n], ALU.add, ["m1", "m2"], [("mg", m)])
            if dbg and l == 0 and tile_idx == 1:
                dma(dbg_oa, oa[:, :, 0:512], ["oa"], [])
                dma(dbg_ob, ob[:, :, 0:512], ["ob"], [])
                dma(dbg_oc, oc[:, :, 0:512], ["oc"], [])
                dma(dbg_mg, mg[:, :, 0:512], [("mg", m_) for m_ in range(8)], [])
                dma(dbg_u, ub[:, :, 0:512], ["ub"], [])
                dma(dbg_qr, qr[:, :, 0:512], ["qr"], [])
                dma(dbg_kr, kr[:, :, 0:512], ["kr"], [])
                dma(dbg_vr, vr[:, :, :], [("vr", b_) for b_ in range(4)], [])
                dma(dbg_sgr, sgr[:, :, 0:512], ["sgr"], [])
                dma(dbg_aT, aT4[:, 3, :], [("aT", 3)], [])
                dma(dbg_gno, gno[:], ["gno"], [])
                dma(dbg_S, Sst[:, 0, :, :], ["Sst"], [])
            for half in range(2):
                wt, wk = wload(l, "wo%d" % half)
                w3 = wt[:, 0:4096].rearrange("p (k c) -> p k c", k=KC)
                for (s0, n) in groups:
                    for mc in range(4):
                        p, pk = PS()
                        for kc in range(KC):
                            mm(p[:, 0:n], w3[:, kc, mc * 128:(mc + 1) * 128], mg[:, kc, s0:s0 + n], kc == 0, kc == KC - 1, [wk, ("mg", kc)], [pk])
                        m = 4 * half + mc
                        tt(xt[:, m, s0:s0 + n], p[:, 0:n], xt[:, m, s0:s0 + n], ALU.add, [pk, "xt"], ["xt"])
            if dbg and l == 0 and tile_idx == 1:
                dma(dbg_x, xt[:, :, 0:512], ["xt"], [])

        tiles = []
        for t in range(ntiles):
            if t == 0:
                tiles.append((0, 1))
            else:
                tiles.append((4 * (t - 1) + 1, 4))
        for ti, (blk0, nb) in enumerate(tiles):
            nt = nb * 128
            t0 = blk0 * 128
            groups = [(0, nt)]
            dma(xt[:, :, 0:nt], xin[:, :, t0:t0 + nt], [], ["xt"])
            dma(ropet[:, :, 0:nt], rope_in[:, :, t0:t0 + nt], [], ["ropet"])
            for l in range(nlayers):
                if STOP >= 1:
                    ffn(l, 0, groups)
                if STOP >= 2:
                    mixer(l, groups, nt, ti, blk0, nb, False)
                if STOP >= 6:
                    ffn(l, 1, groups)
            rmsnorm(groups, lambda kc: fnorm[:, kc:kc + 1], xt, "xt")
            dma(y_out[:, :, t0:t0 + nt], xt[:, :, 0:nt], ["xt"], [])
        if do_sample:
            groups = [(0, NS)]
            dma(xt[:, :, 0:NS], xs_in, [], ["xt"])
            dma(ropet[:, :, 0:NS], rope_in[:, :, SEQP:SEQP + NS], [], ["ropet"])
            for l in range(nlayers):
                dma(cconv[:, :, 0:3, :], cconv_in[l], [], ["cconv"])
                dma(clru[:], clru_in[l], [], ["clru"])
                if STOP >= 1:
                    ffn(l, 0, groups)
                if STOP >= 2:
                    mixer(l, groups, NS, -1, 0, 0, True)
                if STOP >= 6:
                    ffn(l, 1, groups)
            rmsnorm(groups, lambda kc: fnorm[:, kc:kc + 1], xt, "xt")
            dma(ys_out, xt[:, :, 0:NS], ["xt"], [])
        S.emit()
        build.stats = S.stats
    return nc


def _prep_inputs(inp, ncores=8):
    inp = {k: np.asarray(v) for k, v in inp.items()}
    Wall = _host_weights(inp)
    P, FN = _host_params(inp)
    TB = _host_tables()
    maps = []
    for c in range(ncores):
        seq = c % 2
        xs = np.concatenate([np.zeros((PADN, D), np.float32), inp["meta_tokens"].astype(np.float32), inp["x_prompt"][seq]], axis=0)
        xin = np.ascontiguousarray(xs.reshape(SEQP, KC, 128).transpose(2, 1, 0))
        ss = slice(NS * c, NS * (c + 1))
        xsm = inp["x_sample"][ss, 0, :]
        xs_in = np.ascontiguousarray(xsm.reshape(NS, KC, 128).transpose(2, 1, 0))
        ck = inp["cache_swa_k"][:, ss]
        ck_t = np.ascontiguousarray(ck.transpose(0, 3, 4, 1, 2).reshape(L, 128, NS, 128))
        cv = inp["cache_swa_v"][:, ss]
        cv_t = np.ascontiguousarray(cv.transpose(0, 2, 1, 3, 4).reshape(L, 128, NS, 128))
        cc = inp["state_conv"][:, ss]
        cc_t = np.ascontiguousarray(cc.reshape(L, NS, 3, 4, 128).transpose(0, 4, 3, 2, 1))
        cl = inp["state_lru"][:, ss]
        cl_t = np.ascontiguousarray(cl.reshape(L, NS, 4, 128).transpose(0, 3, 2, 1))
        cr = inp["state_ret"][:, ss]
        cr_t = np.ascontiguousarray(cr.reshape(L, NS, 2, 2, 64, 128).transpose(0, 3, 4, 2, 1, 5).reshape(L, 128, 2, NS, 128))
        maps.append(dict(xin=xin, xs_in=xs_in, wall0=Wall[0], wall1=Wall[1], wall2=Wall[2], wall3=Wall[3], par=P, fnorm=FN, rope=TB["rope"], tabf=TB["f32"], masks=TB["masks"],
                         c128=TB["c128"], ck=ck_t, cv=cv_t, cconv=cc_t, clru=cl_t, cret=cr_t))
    return maps


def _assemble(results):
    f32 = np.float32
    y = np.zeros((2, SEQ, D), f32)
    ysamp = np.zeros((128, 1, D), f32)
    pk = np.zeros((L, 2, 128, 2, 64), f32)
    pv = np.zeros((L, 2, 128, 2, 64), f32)
    pconv = np.zeros((L, 2, 3, 512), f32)
    plru = np.zeros((L, 2, 512), f32)
    pret = np.zeros((L, 2, 4, 64, 128), f32)
    sk = np.zeros((L, 128, 128, 2, 64), f32)
    sv = np.zeros((L, 128, 128, 2, 64), f32)
    sconv = np.zeros((L, 128, 3, 512), f32)
    slru = np.zeros((L, 128, 512), f32)
    sret = np.zeros((L, 128, 4, 64, 128), f32)
    for c, r in enumerate(results):
        ss = slice(NS * c, NS * (c + 1))
        if c < 2:
            yy = r["y"]
            y[c] = yy.transpose(2, 1, 0).reshape(SEQP, D)[PADN + NMETA:]
            pk[:, c] = r["pk"].transpose(0, 2, 1).reshape(L, 128, 2, 64)
            pv[:, c] = r["pv"].reshape(L, 128, 2, 64)
            pconv[:, c] = r["pconv"].transpose(0, 3, 2, 1).reshape(L, 3, 512)
            plru[:, c] = r["plru"].transpose(0, 2, 1).reshape(L, 512)
            pret[:, c] = r["pret"].reshape(L, 2, 64, 2, 128).transpose(0, 3, 1, 2, 4).reshape(L, 4, 64, 128)
        ysamp[ss, 0] = r["ys"].transpose(2, 1, 0).reshape(NS, D)
        skf = np.concatenate([r["sk"], r["skn"][:, :, :, None]], axis=3)
        sk[:, ss] = skf.reshape(L, 2, 64, NS, 128).transpose(0, 3, 4, 1, 2)
        svf = np.concatenate([r["sv"], r["svn"].transpose(0, 2, 1)[:, :, None, :]], axis=2)
        sv[:, ss] = svf.reshape(L, NS, 128, 2, 64)
        sconv[:, ss] = r["sconv"].transpose(0, 4, 3, 2, 1).reshape(L, NS, 3, 512)
        slru[:, ss] = r["slru"].transpose(0, 3, 2, 1).reshape(L, NS, 512)
        sret[:, ss] = r["sret"].reshape(L, 2, 64, 2, NS, 128).transpose(0, 4, 3, 1, 2, 5).reshape(L, NS, 4, 64, 128)
    return (y, ysamp, pk, pv, pconv, plru, pret, sk, sv, sconv, slru, sret)


def kernel(**inputs):
    maps = _prep_inputs(inputs)
    nc = build()
    res = run_bass_kernel_spmd(nc, maps, core_ids=list(range(8)))
    return _assemble(res.results)
```

```python
import numpy as np
from contextlib import ExitStack
import concourse.bass as bass
import concourse.mybir as mybir
from concourse.bass_utils import run_bass_kernel_spmd

F32 = mybir.dt.float32
BF16 = mybir.dt.bfloat16
AF = mybir.ActivationFunctionType
ALU = mybir.AluOpType

D = 1024
DFF = 2048
L = 4
KC = 8
PADN = 112
NMETA = 16
SEQ = 8192
SEQP = PADN + NMETA + SEQ
NBLK = SEQP // 128
NS = 16
NTMAX = 512
EPS = 1e-6
GN_EPS = 1e-5
WSLOT = 5120
NWSLOT = 3

ENGS = ("pe", "act", "dve", "pool", "sp")
NDMA_SEMS = 40
NSW_SEMS = 12
SEM_EPOCH = 4000


class _Op:
    __slots__ = ("eng", "fn", "deps", "dma", "sig", "sigval", "dsem", "dval", "dprev")

    def __init__(self, eng, fn, deps, dma):
        self.eng = eng
        self.fn = fn
        self.deps = deps
        self.dma = dma
        self.sig = False
        self.sigval = 0
        self.dsem = None
        self.dval = 0
        self.dprev = None


class Sched:
    def __init__(self, nc):
        self.nc = nc
        self.ops = []
        self.lastw = {}
        self.readers = {}

    def op(self, eng, fn, r=(), w=(), dma=False):
        al = getattr(self, "alias", {})
        r = [al.get(k, k) if isinstance(k, str) else k for k in r]
        w = [al.get(k, k) if isinstance(k, str) else k for k in w]
        idx = len(self.ops)
        deps = set()
        for k in r:
            if k in self.lastw:
                deps.add(self.lastw[k])
        for k in w:
            if k in self.lastw:
                deps.add(self.lastw[k])
            deps.update(self.readers.get(k, ()))
        for k in w:
            self.lastw[k] = idx
            self.readers[k] = []
        for k in r:
            self.readers.setdefault(k, []).append(idx)
        deps.discard(idx)
        self.ops.append(_Op(eng, fn, deps, dma))
        return idx

    def emit(self, final_wait_eng="sp"):
        nc = self.nc
        ops = self.ops
        alldma = {i for i, o in enumerate(ops) if o.dma}
        ops.append(_Op(final_wait_eng, None, alldma, False))
        for o in ops:
            for d in o.deps:
                p = ops[d]
                if p.dma:
                    continue
                if p.eng == "pe" and o.eng == "pe" and not o.dma:
                    continue
                p.sig = True
        cnt = {e: 0 for e in ENGS}
        for o in ops:
            if o.dma:
                continue
            if o.sig:
                cnt[o.eng] += 1
                o.sigval = cnt[o.eng]
        dtot = [0] * NDMA_SEMS
        dlast = [None] * NDMA_SEMS
        nd = 0
        nsw = 0
        nhw = 0
        for i, o in enumerate(ops):
            if o.dma:
                if o.eng == "pool":
                    s = nsw % NSW_SEMS
                    nsw += 1
                else:
                    s = NSW_SEMS + nhw % (NDMA_SEMS - NSW_SEMS)
                    nhw += 1
                nd += 1
                o.dsem = s
                o.dprev = dlast[s]
                dtot[s] += 16
                o.dval = dtot[s]
                dlast[s] = i
        plans = []
        seen = {e: {} for e in ENGS}
        for i, o in enumerate(ops):
            need = {}
            deps = set(o.deps)
            if o.dma and o.dprev is not None:
                deps.add(o.dprev)
            for d in deps:
                p = ops[d]
                if p.dma:
                    key = ("d", p.dsem)
                    val = p.dval
                else:
                    if p.eng == "pe" and o.eng == "pe" and not o.dma:
                        continue
                    key = ("e", p.eng, (p.sigval - 1) // SEM_EPOCH)
                    val = (p.sigval - 1) % SEM_EPOCH + 1
                if need.get(key, 0) < val:
                    need[key] = val
            w = []
            sd = seen[o.eng]
            for key, val in need.items():
                if sd.get(key, 0) >= val:
                    continue
                sd[key] = val
                w.append((key, val))
            plans.append(w)
        self.stats = dict(cnt=cnt, ndma=nd, nops=len(ops))
        with ExitStack() as es:
            esem = {e: [es.enter_context(nc.semaphore("s_%s%d" % (e, i))) for i in range(cnt[e] // SEM_EPOCH + 1)] for e in ENGS}
            dsem = [es.enter_context(nc.semaphore("d%d" % i)) for i in range(NDMA_SEMS)]
            block = es.enter_context(nc.Block())

            def mk(ename):
                def body(eng):
                    for i, o in enumerate(ops):
                        if o.eng != ename:
                            continue
                        for key, val in plans[i]:
                            sem = dsem[key[1]] if key[0] == "d" else esem[key[1]][key[2]]
                            eng.wait_ge(sem, val)
                        if o.fn is None:
                            continue
                        ins = o.fn(eng)
                        if o.dma:
                            ins.then_inc(dsem[o.dsem], 16)
                        elif o.sig:
                            ins.then_inc(esem[ename][(o.sigval - 1) // SEM_EPOCH], 1)
                return body

            block.tensor(mk("pe"))
            block.scalar(mk("act"))
            block.vector(mk("dve"))
            block.gpsimd(mk("pool"))
            block.sync(mk("sp"))


def _wblocks():
    bl = []
    for f in (1, 2):
        for hf in range(2):
            for b in range(4):
                bl.append(("gu%d_%d" % (f, 4 * hf + b), 4096))
            for b in range(2):
                bl.append(("dn%d_%d" % (f, 2 * hf + b), 4096))
        if f == 1:
            bl += [("inXA", 4096), ("inYA", 4096), ("lru", 1024), ("inQK", 5120), ("inV", 5120), ("inRq", 4096), ("inRk", 4096),
                   ("inGR", 4096)]
            for m in range(8):
                bl.append(("gb%d" % m, 4608))
            bl += [("wo0", 4096), ("wo1", 4096)]
    off = {}
    o = 0
    for n, x in bl:
        off[n] = (o, x)
        o += x
    return bl, off, o


WBL, WOFF, XTOT = _wblocks()


def _tile_w(W, cols=None):
    K = W.shape[0]
    a = W.reshape(K // 128, 128, W.shape[1]).transpose(1, 0, 2)
    if cols is not None:
        a = a[:, :, cols]
    return np.ascontiguousarray(a).reshape(128, -1)


def _host_weights(inp):
    Wall = np.zeros((L, 128, XTOT), np.float32)
    r = np.arange
    qs0, ks0, vs0, qr0, kr0, vr0, gr0, g0 = 1024, 1536, 1664, 1792, 2048, 2304, 2816, 3328
    qs_cols = np.concatenate([np.concatenate([qs0 + 64 * c + r(64), qs0 + 64 * (4 + c) + r(64)]) for c in range(4)])

    def sw(base):
        return np.concatenate([np.concatenate([base + 64 * h + 32 + r(32), base + 64 * h + r(32)]) for h in range(4)])

    ob_rows = np.concatenate([np.array([(4 * (p // 64) + c) * 64 + p % 64 for p in range(128)]) for c in range(4)])
    for l in range(L):
        def put(name, arr):
            o, x = WOFF[name]
            assert arr.shape == (128, x), (name, arr.shape, x)
            Wall[l, :, o:o + x] = arr
        for f, (gu, dn) in ((1, ("ffn1_w_gu", "ffn1_w_down")), (2, ("ffn2_w_gu", "ffn2_w_down"))):
            Wg = inp[gu][l]
            Wd = inp[dn][l]
            for b in range(8):
                cols = np.concatenate([256 * b + r(256), 2048 + 256 * b + r(256)])
                put("gu%d_%d" % (f, b), _tile_w(Wg, cols))
            for hf in range(2):
                for b in range(2):
                    put("dn%d_%d" % (f, 2 * hf + b), _tile_w(Wd[1024 * hf:1024 * (hf + 1)], 512 * b + r(512)))
        Wi = inp["w_in"][l]
        put("inXA", _tile_w(Wi, r(512)))
        put("inYA", _tile_w(Wi, 512 + r(512)))
        put("inQK", _tile_w(Wi, np.concatenate([qs_cols, ks0 + r(128)])))
        put("inV", _tile_w(Wi, np.concatenate([vs0 + r(128), vr0 + r(512)])))
        put("inRq", _tile_w(Wi, np.concatenate([qr0 + r(256), sw(qr0)])))
        put("inRk", _tile_w(Wi, np.concatenate([kr0 + r(256), sw(kr0)])))
        put("inGR", _tile_w(Wi, gr0 + r(512)))
        bd = np.zeros((128, 2, 4, 128), np.float32)
        for wi, nm in enumerate(("lru_w_a", "lru_w_x")):
            Wl = inp[nm][l]
            for j in range(4):
                for hb in range(2):
                    bd[64 * hb:64 * hb + 64, wi, j, 64 * hb:64 * hb + 64] = Wl[2 * j + hb]
        put("lru", bd.reshape(128, -1))
        Wa = inp["w_branch_a"][l]
        Wb = inp["w_branch_b"][l][ob_rows]
        Wc = inp["w_branch_c"][l]
        for m in range(8):
            g = _tile_w(Wi, np.concatenate([g0 + 128 * m + r(128), g0 + 1024 + 128 * m + r(128), g0 + 2048 + 128 * m + r(128)]))
            br = np.stack([_tile_w(Wx, 128 * m + r(128)).reshape(128, 4, 128) for Wx in (Wa, Wb, Wc)], axis=1)
            put("gb%d" % m, np.concatenate([g, br.reshape(128, -1)], axis=1))
        Wo = inp["w_out"][l]
        put("wo0", _tile_w(Wo, r(512)))
        put("wo1", _tile_w(Wo, 512 + r(512)))
    return Wall


def _fm(v):
    v = np.asarray(v, np.float32)
    F = v.shape[-1]
    a = v.reshape(v.shape[:-1] + (F // 128, 128))
    return np.ascontiguousarray(np.moveaxis(a, -1, 0))


NPAR = 3 * 8 + 16 + 4 * 5 + 4 + 4


def _host_params(inp):
    P = np.zeros((128, L, NPAR), np.float32)
    for l in range(L):
        o = 0
        for nm in ("ffn1_norm", "mix_norm", "ffn2_norm"):
            P[:, l, o:o + 8] = _fm(inp[nm][l])
            o += 8
        cw = _fm(inp["conv_w"][l])
        P[:, l, o:o + 16] = cw.transpose(0, 2, 1).reshape(128, 16)
        o += 16
        for nm in ("conv_b", "lru_b_a", "lru_b_x", "lru_lambda"):
            P[:, l, o:o + 4] = _fm(inp[nm][l])
            o += 4
        sk = inp["swa_sinks"][l]
        P[:, l, o:o + 4] = np.array([[sk[4 * (p // 64) + c] for c in range(4)] for p in range(128)], np.float32)
        o += 4
        P[:, l, o:o + 4] = _fm(inp["ret_norm"][l])
        o += 4
    return P, _fm(inp["final_norm"])


def _host_tables():
    T = {}
    p = np.arange(128)
    dd = p % 64
    inv = (10000.0 ** (-(np.arange(32, dtype=np.float32)) / 32)).astype(np.float32)
    pos = np.concatenate([np.arange(SEQP) - PADN, np.full(NS, SEQ)]).astype(np.float32)
    ang = (pos[None, :] * inv[dd % 32][:, None]).astype(np.float32).astype(np.float64)
    cos = np.cos(ang)
    sin = np.sin(ang) * np.where(dd < 32, -1.0, 1.0)[:, None]
    T["rope"] = np.stack([cos, sin, cos / 8, sin / 8], axis=1).astype(np.float32)
    lg = np.log1p(-np.exp2(-5.0 - np.arange(4, dtype=np.float32))).astype(np.float32).astype(np.float64)
    j = np.arange(128)[:, None]
    i = np.arange(128)[None, :]
    DT = np.zeros((128, 4, 128))
    for h in range(4):
        DT[:, h, :] = np.where(i >= j, np.exp((i - j) * lg[h]), 0.0)
    n = np.arange(128)
    hd = lambda ch: 2 * ch + p // 64
    cross = np.stack([np.exp((n[None, :] + 1.0) * lg[hd(ch)][:, None]) for ch in range(2)], axis=1)
    kdec = np.stack([np.exp((127.0 - n[None, :]) * lg[hd(ch)][:, None]) for ch in range(2)], axis=1)
    g128 = np.stack([np.exp(128.0 * lg[hd(ch)]) for ch in range(2)], axis=1)
    g1 = np.stack([np.exp(lg[hd(ch)]) for ch in range(2)], axis=1)
    pm = np.ones((128, 128))
    pm[:, :PADN] = 0.0
    T["f32"] = np.concatenate([DT.reshape(128, 512), cross.reshape(128, 256), kdec.reshape(128, 256), g128, g1, pm],
                              axis=1).astype(np.float32)
    mC = (j <= i)
    mP = (j > i)
    big = (j >= PADN)
    ms = [np.tile(m.astype(np.float32), (1, 4)) for m in (mC, mP, mC & big, mP & big)]
    T["masks"] = np.concatenate(ms, axis=1).astype(np.float32)
    ident = np.eye(128, dtype=np.float32)
    T["c128"] = np.concatenate([np.ones((128, 128)), np.full((128, 128), 1.0 / 128), ident], axis=1).astype(np.float32)
    bd = np.zeros((16, 16, 128), np.float32)
    for s in range(16):
        bd[s, s, :] = 1.0
    T["bdmask"] = bd.reshape(16, 2048)
    T["gam"] = [float(np.exp(lg[h])) for h in range(4)]
    return T


TABF = 512 + 256 + 256 + 2 + 2 + 128


def build(ntiles=17, do_sample=True, nlayers=L, dbg=False):
    import os
    STOP = int(os.environ.get("STOP", "9"))
    STOPC = int(os.environ.get("STOPC", "9"))
    STOPB = int(os.environ.get("STOPB", "9"))
    nc = bass.Bass("TRN2", target_bir_lowering=False)
    TB = _host_tables()
    GAM = TB["gam"]

    def din(name, shape, dt=F32):
        return nc.dram_tensor(name, list(shape), dt, kind="ExternalInput").ap()

    def dout(name, shape, dt=F32):
        return nc.dram_tensor(name, list(shape), dt, kind="ExternalOutput").ap()

    xin = din("xin", [128, KC, SEQP])
    xs_in = din("xs_in", [128, KC, NS])
    wall = [din("wall%d" % l_, [128, XTOT]) for l_ in range(L)]
    par_in = din("par", [128, L, NPAR])
    fn_in = din("fnorm", [128, KC])
    rope_in = din("rope", [128, 4, SEQP + NS])
    tab_in = din("tabf", [128, TABF])
    mask_in = din("masks", [128, 2048])
    c128_in = din("c128", [128, 384])
    ck_in = din("ck", [L, 128, NS, 128])
    cv_in = din("cv", [L, 128, NS, 128])
    cconv_in = din("cconv", [L, 128, 4, 3, NS])
    clru_in = din("clru", [L, 128, 4, NS])
    cret_in = din("cret", [L, 128, 2, NS, 128])

    wbf = [nc.dram_tensor("wbf%d" % l_, [128, XTOT], BF16, kind="Internal").ap() for l_ in range(L)]

    y_out = dout("y", [128, KC, SEQP])
    ys_out = dout("ys", [128, KC, NS])
    pk_out = dout("pk", [L, 128, 128])
    pv_out = dout("pv", [L, 128, 128])
    pconv_out = dout("pconv", [L, 128, 4, 3])
    plru_out = dout("plru", [L, 128, 4])
    pret_out = dout("pret", [L, 128, 2, 128])
    sk_out = dout("sk", [L, 128, NS, 127])
    sv_out = dout("sv", [L, NS, 127, 128])
    skn_out = dout("skn", [L, 128, NS])
    svn_out = dout("svn", [L, 128, NS])
    sconv_out = dout("sconv", [L, 128, 4, 3, NS])
    slru_out = dout("slru", [L, 128, 4, NS])
    sret_out = dout("sret", [L, 128, 2, NS, 128])
    if dbg:
        dbg_oa = dout("dbg_oa", [128, 4, 512], BF16)
        dbg_ob = dout("dbg_ob", [128, 4, 512], BF16)
        dbg_oc = dout("dbg_oc", [128, 4, 512], BF16)
        dbg_mg = dout("dbg_mg", [128, 8, 512], BF16)
        dbg_x = dout("dbg_x", [128, 8, 512])
        dbg_u = dout("dbg_u", [128, 8, 512], BF16)
        dbg_qr = dout("dbg_qr", [128, 2, 512], BF16)
        dbg_kr = dout("dbg_kr", [128, 2, 512], BF16)
        dbg_vr = dout("dbg_vr", [128, 4, 512], BF16)
        dbg_sgr = dout("dbg_sgr", [128, 4, 512], BF16)
        dbg_aT = dout("dbg_aT", [128, 512], BF16)
        dbg_gno = dout("dbg_gno", [128, 512], BF16)
        dbg_S = dout("dbg_S", [128, 2, 128])

    es = ExitStack()
    with es:
        def sb(name, shape, dt=F32):
            return es.enter_context(nc.sbuf_tensor(name, list(shape), dt))

        S = Sched(nc)
        xt = sb("xt", [128, KC, NTMAX])
        ub = sb("ub", [128, KC, NTMAX], BF16)
        hb = sb("hb", [128, 8, NTMAX], BF16)
        rstd = sb("rstd", [128, 512])
        wsl = [sb("wsl%d" % i, [128, WSLOT], BF16) for i in range(NWSLOT)]
        par = sb("par_t", [128, L, NPAR])
        fnorm = sb("fnormt", [128, KC])
        c1 = sb("c1", [128, L, 4])
        c2 = sb("c2", [128, L, 4])
        es_t = sb("es_t", [128, L, 4])
        ropet = sb("ropet", [128, 4, NTMAX])
        tabf = sb("tabf_t", [128, TABF])
        maskb = sb("maskb", [128, 2048], BF16)
        c128 = sb("c128t", [128, 384])
        onesb = sb("onesb", [128, 128], BF16)
        onesm = sb("onesm", [128, 128], BF16)
        identb = sb("identb", [128, 128], BF16)
        xaE = sb("xaE", [128, 3 + NTMAX])
        xcb = sb("xcb", [128, NTMAX], BF16)
        oa = sb("oa", [128, 4, NTMAX], BF16)
        ob = sb("ob", [128, 4, NTMAX], BF16)
        oc = sb("oc", [128, 4, NTMAX], BF16)
        mg = sb("mg", [128, KC, NTMAX], BF16)
        hst = sb("hst", [128, L, 4])
        cst = sb("cst", [128, L, 4, 3])
        qT = sb("qT", [128, 4, NTMAX], BF16)
        kTe = sb("kTe", [128, 128 + NTMAX], BF16)
        ve = sb("ve", [128, 5, 128], BF16)
        kst = sb("kst", [128, L, 128], BF16)
        vst = sb("vst", [128, L, 128], BF16)
        kof = sb("kof", [128, NTMAX])
        vof = sb("vof", [128, 128])
        eP = sb("eP", [128, 512], BF16)
        eC = sb("eC", [128, 512], BF16)
        qr = sb("qr", [128, 2, NTMAX], BF16)
        kr = sb("kr", [128, 2, NTMAX], BF16)
        vr = sb("vr", [128, 4, 512], BF16)
        sgr = sb("sgr", [128, 4, NTMAX], BF16)
        kdt = sb("kdt", [128, 2, 128], BF16)
        ktm4 = sb("ktm4", [128, 4, 256], BF16)
        qdc4 = sb("qdc4", [128, 4, 2, 128], BF16)
        aT4 = sb("aT4", [128, 4, 512], BF16)
        Sst = sb("Sst", [128, L, 2, 128])
        Sbf = sb("Sbf", [128, L, 2, 128], BF16)
        gno = sb("gno", [128, 512], BF16)
        ocat = sb("ocat", [128, 512])
        gnq = sb("gnq", [128, 512], BF16)
        sga = sb("sga", [128, 512])
        sgb = sb("sgb", [128, 512])
        sgc = sb("sgc", [128, 512])
        m1 = sb("m1", [128, 512])
        m2 = sb("m2", [128, 512])
        rd = sb("rd", [128, 512])
        tmpa = sb("tmpa", [128, 512])
        gmean, gt1, gt2, gcen = sga, sgb, sgc, m1
        rt1, rt2 = m2, tmpa
        la_r, la_i, la_a, la_g, la_h, xc, gy = sga, sgb, sgc, m1, m2, rd, tmpa
        eP2 = sb("eP2", [128, 512], BF16)
        eC2 = sb("eC2", [128, 512], BF16)
        KEYALIAS = dict(gmean="sg0", gt1="sg1", gt2="sg2", gcen="m1", rt1="m2", rt2="tmpa", la_r="sg0", la_i="sg1", la_a="sg2",
                        la_g="m1", la_h="m2", xc="rd", gy="tmpa")
        S.alias = KEYALIAS
        if do_sample:
            kmod = sb("kmod", [128, NS, 128], BF16)
            vmod = sb("vmod", [128, NS, 128], BF16)
            cconv = sb("cconv_t", [128, 4, 4, NS])
            clru = sb("clru_t", [128, 4, NS])
            sret = sb("sret_t", [128, 2, NS, 128])
            qrf = sb("qrf", [128, 2, NS])
            krf = sb("krf", [128, 2, NS])
            prodf = sb("prodf", [128, 2, NS])
            vTf = sb("vTf", [128, 4, NS])
            vtmf = sb("vtmf", [NS, 512])
            vnT = sb("vnT", [128, NS])
            qTf = sb("qTf", [128, 4, NS])
            prodq = sb("prodq", [128, 4, NS])
            en = sb("en", [128, 64])
            ktmf = sb("ktmf", [NS, 256])
            vbd = sb("vbd", [NS, 4, 128])
            onesf = sb("onesf", [128, 128])
            identf = sb("identf", [128, 128])
            osm = sb("osm", [128, 64])
            eS = sb("eS", [128, 128], BF16)
            hnew = sb("hnew", [128, 4, NS])

        psb = [es.enter_context(nc.psum_tensor("psb%d" % i, [128, 512], F32)) for i in range(7)]
        pst = es.enter_context(nc.psum_tensor("pst", [128, 1024], BF16))
        psi = [0]

        def PS():
            i = psi[0] % 7
            psi[0] += 1
            return psb[i], ("ps", i)

        def act(out, in_, func, r, w, **kw):
            S.op("act", lambda e: e.activation(out=out, in_=in_, func=func, **kw), r, w)

        def tt(out, in0, in1, op, r, w, eng="dve"):
            S.op(eng, lambda e: e.tensor_tensor(out=out, in0=in0, in1=in1, op=op), r, w)

        def ts(out, in0, s1, s2, op0, op1, r, w, eng="dve"):
            if op1 is None:
                S.op(eng, lambda e: e.tensor_scalar(out=out, in0=in0, scalar1=s1, scalar2=None, op0=op0), r, w)
            else:
                S.op(eng, lambda e: e.tensor_scalar(out=out, in0=in0, scalar1=s1, scalar2=s2, op0=op0, op1=op1), r, w)

        def stt(out, in0, sc, in1, op0, op1, r, w):
            S.op("dve", lambda e: e.scalar_tensor_tensor(out=out, in0=in0, scalar=sc, in1=in1, op0=op0, op1=op1), r, w)

        def cp(out, in_, r, w, eng="dve"):
            S.op(eng, lambda e: e.tensor_copy(out=out, in_=in_), r, w)

        def mm(out, lhsT, rhs, start, stop, r, w):
            S.op("pe", lambda e: e.matmul(out, lhsT, rhs, start=start, stop=stop), r, w)

        def dma(out, in_, r, w, eng="pool", **kw):
            S.op(eng, lambda e: e.dma_start(out=out, in_=in_, **kw), r, w, dma=True)

        def recip(out, in_, r, w):
            S.op("dve", lambda e: e.reciprocal(out=out, in_=in_), r, w)

        def memset(ap, val, w, eng="dve"):
            S.op(eng, lambda e: e.memset(ap, val), (), w)

        dma(par[:], par_in, [], ["par"], eng="sp")
        dma(fnorm[:], fn_in, [], ["fnorm"], eng="sp")
        dma(tabf[:], tab_in, [], ["tabf"], eng="sp")
        dma(c128[:], c128_in, [], ["c128"], eng="sp")
        for q_ in range(4):
            dma(rt1[:], mask_in[:, q_ * 512:(q_ + 1) * 512], [], ["rt1"], eng="sp")
            cp(maskb[:, q_ * 512:(q_ + 1) * 512], rt1[:], ["rt1"], ["maskb"])
        cp(onesb[:], c128[:, 0:128], ["c128"], ["onesb"])
        cp(onesm[:], c128[:, 128:256], ["c128"], ["onesm"])
        cp(identb[:], c128[:, 256:384], ["c128"], ["identb"])
        if do_sample:
            cp(onesf[:], c128[:, 0:128], ["c128"], ["onesf"])
            cp(identf[:], c128[:, 256:384], ["c128"], ["identf"])
        CH = 8192
        for l in range(nlayers):
            for o in range(0, XTOT, CH):
                n = min(CH, XTOT - o)
                dma(wbf[l][:, o:o + n], wall[l][:, o:o + n], [], [("wbf", l, o // CH), ("castslot", (o // CH) % 6)], eng="pool", max_dma_last_dim=8192)
        PO = dict(norm=0, cw=24, cb=40, ba=44, bx=48, lam=52, sk=56, rg=60)
        for l in range(nlayers):
            act(c1[:, l, :], par[:, l, 52:56], AF.Exp, ["par"], ["c1"], scale=-1.0)
            act(c1[:, l, :], c1[:, l, :], AF.Ln, ["c1"], ["c1"], bias=1.0)
            ts(c2[:, l, :], c1[:, l, :], -16.0, None, ALU.mult, None, ["c1"], ["c2"])
            ts(c1[:, l, :], c1[:, l, :], -8.0, None, ALU.mult, None, ["c1"], ["c1"])
            act(es_t[:, l, :], par[:, l, 56:60], AF.Exp, ["par"], ["es"])
        memset(hst[:], 0.0, ["hst"])
        memset(cst[:], 0.0, ["cst"])
        memset(kst[:], 0.0, ["kst"])
        memset(vst[:], 0.0, ["vst"])
        memset(Sst[:], 0.0, ["Sst"])
        memset(Sbf[:], 0.0, ["Sbf"])

        wctr = [0]

        def wload(l, name):
            o, x = WOFF[name]
            i = wctr[0] % NWSLOT
            wctr[0] += 1
            key = ("wsl", i)
            r = [("wbf", l, c) for c in range(o // CH, (o + x - 1) // CH + 1)]
            dma(wsl[i][:, 0:x], wbf[l][:, o:o + x], r, [key], eng="sp")
            return wsl[i], key

        def rmsnorm(groups, gvec, out_t, outkey, okeys_extra=()):
            for (s0, n) in groups:
                for kc in range(KC):
                    act(hb[:, kc, 0:n], xt[:, kc, s0:s0 + n], AF.Square, ["xt"], [("hb", kc)])
                p, pk = PS()
                for kc in range(KC):
                    mm(p[:, 0:n], onesb[:], hb[:, kc, 0:n], kc == 0, kc == KC - 1, [("hb", kc), "onesb"], [pk])
                act(rstd[:, 0:n], p[:, 0:n], AF.Ln, [pk], ["rstd"], scale=1.0 / D, bias=EPS)
                act(rstd[:, 0:n], rstd[:, 0:n], AF.Exp, ["rstd"], ["rstd"], scale=-0.5)
                for kc in range(KC):
                    stt(out_t[:, kc, s0:s0 + n], xt[:, kc, s0:s0 + n], gvec(kc), rstd[:, 0:n], ALU.mult, ALU.mult,
                        ["xt", "rstd", "par", "fnorm"], [outkey])

        def ffn(l, which, groups):
            f = which + 1
            gsel = 0 if which == 0 else 2
            rmsnorm(groups, lambda kc: par[:, l, gsel * 8 + kc:gsel * 8 + kc + 1], ub, "ub")
            for hf in range(2):
                for b in range(4):
                    wt, wk = wload(l, "gu%d_%d" % (f, 4 * hf + b))
                    w3 = wt[:, 0:4096].rearrange("p (k c) -> p k c", k=KC)
                    for (s0, n) in groups:
                        for fc in range(2):
                            pg, pgk = PS()
                            pu, puk = PS()
                            for kc in range(KC):
                                mm(pg[:, 0:n], w3[:, kc, fc * 128:(fc + 1) * 128], ub[:, kc, s0:s0 + n], kc == 0, kc == KC - 1,
                                   [wk, "ub"], [pgk])
                            for kc in range(KC):
                                mm(pu[:, 0:n], w3[:, kc, 256 + fc * 128:256 + (fc + 1) * 128], ub[:, kc, s0:s0 + n], kc == 0,
                                   kc == KC - 1, [wk, "ub"], [puk])
                            act(tmpa[:, 0:n], pg[:, 0:n], AF.Silu, [pgk], ["tmpa"])
                            tt(hb[:, 2 * b + fc, s0:s0 + n], tmpa[:, 0:n], pu[:, 0:n], ALU.mult, ["tmpa", puk], [("hb", 2 * b + fc)])
                for b in range(2):
                    wt, wk = wload(l, "dn%d_%d" % (f, 2 * hf + b))
                    w3 = wt[:, 0:4096].rearrange("p (k c) -> p k c", k=KC)
                    for (s0, n) in groups:
                        for mc in range(4):
                            p, pk = PS()
                            for fc in range(8):
                                mm(p[:, 0:n], w3[:, fc, mc * 128:(mc + 1) * 128], hb[:, fc, s0:s0 + n], fc == 0, fc == 7,
                                   [wk, ("hb", fc)], [pk])
                            m = 4 * b + mc
                            stt(xt[:, m, s0:s0 + n], p[:, 0:n], 0.5, xt[:, m, s0:s0 + n], ALU.mult, ALU.add, [pk, "xt"], ["xt"])

        def groupnorm_gate(l, osrc, okey, ncols, sg_ap, out_ap, h_of_col):
            act(gno[:, 0:ncols], osrc, AF.Copy, [okey], ["gno"])
            act(gnq[:, 0:ncols], osrc, AF.Square, [okey], ["gnq"])
            pm_, pmk = PS()
            pq_, pqk = PS()
            mm(pm_[:, 0:ncols], onesm[:], gno[:, 0:ncols], True, True, ["gno", "onesm"], [pmk])
            mm(pq_[:, 0:ncols], onesm[:], gnq[:, 0:ncols], True, True, ["gnq", "onesm"], [pqk])
            act(gmean[:, 0:ncols], pm_[:, 0:ncols], AF.Copy, [pmk], ["gmean"])
            tt(gt1[:, 0:ncols], gmean[:, 0:ncols], gmean[:, 0:ncols], ALU.mult, ["gmean"], ["gt1"])
            tt(gt1[:, 0:ncols], pq_[:, 0:ncols], gt1[:, 0:ncols], ALU.subtract, [pqk, "gt1"], ["gt1"])
            ts(gt1[:, 0:ncols], gt1[:, 0:ncols], 0.0, None, ALU.max, None, ["gt1"], ["gt1"])
            act(gt2[:, 0:ncols], gt1[:, 0:ncols], AF.Ln, ["gt1"], ["gt2"], bias=GN_EPS)
            act(gt2[:, 0:ncols], gt2[:, 0:ncols], AF.Exp, ["gt2"], ["gt2"], scale=-0.5)
            tt(gcen[:, 0:ncols], osrc, gmean[:, 0:ncols], ALU.subtract, [okey, "gmean"], ["gcen"])
            tt(gcen[:, 0:ncols], gcen[:, 0:ncols], gt2[:, 0:ncols], ALU.mult, ["gcen", "gt2"], ["gcen"])
            h_of_col(gcen, sg_ap, out_ap)

        def mixer(l, groups, nt, tile_idx, blk0, nb, sample):
            last = (not sample) and ((blk0 + nb == NBLK) or (os.environ.get("LASTDBG") and tile_idx == ntiles - 1))
            rmsnorm(groups, lambda kc: par[:, l, 8 + kc:8 + kc + 1], ub, "ub")
            wA, wAk = wload(l, "inXA")
            wA3 = wA[:, 0:4096].rearrange("p (k c) -> p k c", k=KC)
            wY, wYk = wload(l, "inYA")
            wY3 = wY[:, 0:4096].rearrange("p (k c) -> p k c", k=KC)
            wLt, wLk = wload(l, "lru")
            wL = wLt[:, 0:1024].rearrange("p (w j c) -> p w j c", w=2, j=4)
            for j in range(4):
                if sample:
                    tap = lambda t: cconv[:, j, t, :]
                    newx = cconv[:, j, 3, :]
                    xkey = "cconv"
                else:
                    tap = lambda t: xaE[:, t:t + nt]
                    newx = None
                    xkey = "xaE"
                    cp(xaE[:, 0:3], cst[:, l, j, :], ["cst"], ["xaE"])
                for (s0, n) in groups:
                    p, pk = PS()
                    for kc in range(KC):
                        mm(p[:, 0:n], wA3[:, kc, j * 128:(j + 1) * 128], ub[:, kc, s0:s0 + n], kc == 0, kc == KC - 1, [wAk, "ub"], [pk])
                    dst = newx if sample else xaE[:, 3 + s0:3 + s0 + n]
                    act(dst, p[:, 0:n], AF.Copy, [pk], [xkey])
                    p2, p2k = PS()
                    for kc in range(KC):
                        mm(p2[:, 0:n], wY3[:, kc, j * 128:(j + 1) * 128], ub[:, kc, s0:s0 + n], kc == 0, kc == KC - 1,
                           [wYk, "ub"], [p2k])
                    act(gy[:, s0:s0 + n], p2[:, 0:n], AF.Gelu_apprx_tanh, [p2k], ["gy"])
                cwb = 24 + 4 * j
                ts(xc[:, 0:nt], tap(0), par[:, l, cwb:cwb + 1], par[:, l, 40 + j:41 + j], ALU.mult, ALU.add, [xkey, "par"], ["xc"])
                for t in (1, 2, 3):
                    stt(xc[:, 0:nt], tap(t), par[:, l, cwb + t:cwb + t + 1], xc[:, 0:nt], ALU.mult, ALU.add, [xkey, "par", "xc"], ["xc"])
                act(xcb[:, 0:nt], xc[:, 0:nt], AF.Copy, ["xc"], ["xcb"])
                if sample:
                    dma(sconv_out[l, :, j, :, :], cconv[:, j, 1:4, :], ["cconv"], [])
                else:
                    cp(cst[:, l, j, :], xaE[:, nt:nt + 3], ["xaE"], ["cst"])
                for (s0, n) in groups:
                    pr, prk = PS()
                    mm(pr[:, 0:n], wL[:, 0, j, :], xcb[:, s0:s0 + n], True, True, [wLk, "xcb"], [prk])
                    pi_, pik = PS()
                    mm(pi_[:, 0:n], wL[:, 1, j, :], xcb[:, s0:s0 + n], True, True, [wLk, "xcb"], [pik])
                    act(la_r[:, s0:s0 + n], pr[:, 0:n], AF.Sigmoid, [prk, "par"], ["la_r"], bias=par[:, l, 44 + j:45 + j])
                    act(la_i[:, s0:s0 + n], pi_[:, 0:n], AF.Sigmoid, [pik, "par"], ["la_i"], bias=par[:, l, 48 + j:49 + j])
                act(la_a[:, 0:nt], la_r[:, 0:nt], AF.Exp, ["la_r", "c1"], ["la_a"], scale=c1[:, l, j:j + 1])
                act(la_g[:, 0:nt], la_r[:, 0:nt], AF.Exp, ["la_r", "c2"], ["la_g"], scale=c2[:, l, j:j + 1])
                act(la_g[:, 0:nt], la_g[:, 0:nt], AF.Sqrt, ["la_g"], ["la_g"], scale=-1.0, bias=1.0)
                tt(la_g[:, 0:nt], la_g[:, 0:nt], la_i[:, 0:nt], ALU.mult, ["la_g", "la_i"], ["la_g"])
                tt(la_g[:, 0:nt], la_g[:, 0:nt], xc[:, 0:nt], ALU.mult, ["la_g", "xc"], ["la_g"])
                if sample:
                    tt(la_h[:, 0:nt], la_a[:, 0:nt], clru[:, j, :], ALU.mult, ["la_a", "clru"], ["la_h"])
                    tt(hnew[:, j, :], la_h[:, 0:nt], la_g[:, 0:nt], ALU.add, ["la_h", "la_g"], ["hnew"])
                    tt(oa[:, j, 0:nt], hnew[:, j, :], gy[:, 0:nt], ALU.mult, ["hnew", "gy"], ["oa"])
                else:
                    if tile_idx == 0:
                        tt(la_g[:, 0:nt], la_g[:, 0:nt], tabf[:, 1028:1028 + nt], ALU.mult, ["la_g", "tabf"], ["la_g"])
                    S.op("dve", lambda e, j=j: e.tensor_tensor_scan(out=la_h[:, 0:nt], data0=la_a[:, 0:nt], data1=la_g[:, 0:nt],
                                                                   initial=hst[:, l, j:j + 1], op0=ALU.mult, op1=ALU.add),
                         ["la_a", "la_g", "hst"], ["la_h"])
                    cp(hst[:, l, j:j + 1], la_h[:, nt - 1:nt], ["la_h"], ["hst"])
                    tt(oa[:, j, 0:nt], la_h[:, 0:nt], gy[:, 0:nt], ALU.mult, ["la_h", "gy"], ["oa"])
            if sample:
                dma(slru_out[l], hnew[:], ["hnew"], [])
            if last:
                dma(pconv_out[l], cst[:, l, :, :], ["cst"], [])
                dma(plru_out[l], hst[:, l, :], ["hst"], [])

            if STOP < 3:
                return
            wQ, wQk = wload(l, "inQK")
            wQ3 = wQ[:, 0:5120].rearrange("p (k c) -> p k c", k=KC)
            wV, wVk = wload(l, "inV")
            wV3 = wV[:, 0:5120].rearrange("p (k c) -> p k c", k=KC)
            for (s0, n) in groups:
                for c in range(4):
                    p, pk = PS()
                    for kc in range(KC):
                        mm(p[:, 0:n], wQ3[:, kc, c * 128:(c + 1) * 128], ub[:, kc, s0:s0 + n], kc == 0, kc == KC - 1, [wQk, "ub"], [pk])
                    act(qT[:, c, s0:s0 + n], p[:, 0:n], AF.Copy, [pk], ["qT"], scale=0.125)
                p, pk = PS()
                for kc in range(KC):
                    mm(p[:, 0:n], wQ3[:, kc, 512:640], ub[:, kc, s0:s0 + n], kc == 0, kc == KC - 1, [wQk, "ub"], [pk])
                if sample:
                    act(kof[:, 0:n], p[:, 0:n], AF.Copy, [pk], ["kof"])
                else:
                    act(kTe[:, 128 + s0:128 + s0 + n], p[:, 0:n], AF.Copy, [pk], ["kTe"])
                    if last:
                        act(kof[:, s0:s0 + n], p[:, 0:n], AF.Copy, [pk], ["kof"])
            if sample:
                pv_, pvk = PS()
                for kc in range(KC):
                    mm(pv_[:, 0:NS], wV3[:, kc, 0:128], ub[:, kc, 0:NS], kc == 0, kc == KC - 1, [wVk, "ub"], [pvk])
                act(vnT[:], pv_[:, 0:NS], AF.Copy, [pvk], ["vnT"])
                dma(kmod[:], ck_in[l], [], ["kmod"])
                dma(vmod[:], cv_in[l], [], ["vmod"])
                dma(sk_out[l], ck_in[l, :, :, 1:128], [], [])
                dma(skn_out[l], kof[:, 0:NS], ["kof"], [])
                dma(sv_out[l], cv_in[l, 1:128, :, :].rearrange("k s f -> s k f"), [], [])
                dma(svn_out[l], vnT[:], ["vnT"], [])
                if STOPB < 1:
                    return
                pSb = [PS(), PS()]
                for s in range(NS):
                    for g in range(2):
                        pS_, pSk = pSb[g]
                        mm(pS_[:, s * 4:s * 4 + 4], kmod[64 * g:64 * g + 64, s, :], qT[64 * g:64 * g + 64, :, s], True, True,
                           ["kmod", "qT"], [pSk])
                for g in range(2):
                    act(eS[:, g * 64:(g + 1) * 64], pSb[g][0][:, 0:64], AF.Exp, [pSb[g][1]], ["eS"])
                memset(eS[0:1, :], 0.0, ["eS"])
                if STOPB < 2:
                    return
                pO, pOk = PS()
                pD, pDk = PS()
                for s in range(NS):
                    for g in range(2):
                        c0 = g * 64 + s * 4
                        mm(pO[64 * g:64 * g + 64, s * 4:s * 4 + 4], vmod[:, s, 64 * g:64 * g + 64], eS[:, c0:c0 + 4], True, True,
                           ["vmod", "eS"], [pOk])
                        mm(pD[64 * g:64 * g + 64, s * 4:s * 4 + 4], onesb[:, 0:64], eS[:, c0:c0 + 4], True, True,
                           ["onesb", "eS"], [pDk])
                if STOPB < 3:
                    return
                cp(qTf[:], qT[:, :, 0:NS], ["qT"], ["qTf"])
                tt(prodq[:], qTf[:], kof[:, 0:NS].unsqueeze(1).to_broadcast([128, 4, NS]), ALU.mult, ["qTf", "kof"], ["prodq"])
                for g in range(2):
                    gs = slice(64 * g, 64 * g + 64)
                    pN, pNk = PS()
                    mm(pN[gs, 0:64], onesf[gs, 0:64], prodq[gs, :, :].rearrange("p c s -> p (c s)"), True, True, ["onesf", "prodq"], [pNk])
                    act(en[gs, :], pN[gs, 0:64], AF.Exp, [pNk], ["en"])
                if STOPB < 4:
                    return
                en3 = en[:].rearrange("p (c s) -> p s c", c=4)
                rd3 = rd[:, 0:64].rearrange("p (s c) -> p s c", c=4)
                nm3 = m1[:, 0:64].rearrange("p (s c) -> p s c", c=4)
                pD3 = pD[:, 0:64].rearrange("p (s c) -> p s c", c=4)
                pO3 = pO[:, 0:64].rearrange("p (s c) -> p s c", c=4)
                tt(nm3, en3, vnT[:].unsqueeze(2).to_broadcast([128, NS, 4]), ALU.mult, ["en", "vnT"], ["m1"])
                tt(nm3, nm3, pO3, ALU.add, ["m1", pOk], ["m1"])
                tt(rd3, en3, pD3, ALU.add, ["en", pDk], ["rd"])
                for c in range(4):
                    ts(rd3[:, :, c], rd3[:, :, c], es_t[:, l, c:c + 1], None, ALU.add, None, ["rd", "es"], ["rd"])
                act(rd[:, 0:64], rd[:, 0:64], AF.Ln, ["rd"], ["rd"])
                act(rd[:, 0:64], rd[:, 0:64], AF.Exp, ["rd"], ["rd"], scale=-1.0)
                tt(ob[:, :, 0:NS], m1[:, 0:64].rearrange("p (s c) -> p c s", c=4), rd[:, 0:64].rearrange("p (s c) -> p c s", c=4),
                   ALU.mult, ["m1", "rd"], ["ob"])
            else:
                cp(kTe[:, 0:128], kst[:, l, :], ["kst"], ["kTe"])
                cp(ve[:, 0, :], vst[:, l, :], ["vst"], ["ve"])
                for b in range(nb):
                    pv_, pvk = PS()
                    for kc in range(KC):
                        mm(pv_[:, 0:128], ub[:, kc, b * 128:(b + 1) * 128], wV3[:, kc, 0:128], kc == 0, kc == KC - 1, [wVk, "ub"], [pvk])
                    act(ve[:, 1 + b, :], pv_[:, 0:128], AF.Copy, [pvk], ["ve"])
                    if last and b == nb - 1:
                        act(vof[:], pv_[:, 0:128], AF.Copy, [pvk], ["vof"])
                cp(kst[:, l, :], kTe[:, nt:nt + 128], ["kTe"], ["kst"])
                cp(vst[:, l, :], ve[:, nb, :], ["ve"], ["vst"])
                if last:
                    dma(pk_out[l], kof[:, nt - 128:nt], ["kof"], [])
                    dma(pv_out[l], vof[:], ["vof"], [])
                for b in range(nb):
                    gb_ = blk0 + b
                    pO, pOk = PS()
                    pD, pDk = PS()
                    for g in range(2):
                        gs = slice(64 * g, 64 * g + 64)
                        e_p, e_c = (eP, eC) if g == 0 else (eP2, eC2)
                        kp, kc_ = ("eP%d" % g, "eC%d" % g)
                        pC, pCk = PS()
                        for c in range(4):
                            mm(pC[:, c * 128:(c + 1) * 128], kTe[gs, 128 + b * 128:128 + (b + 1) * 128], qT[gs, c, b * 128:(b + 1) * 128],
                               True, True, ["kTe", "qT"], [pCk])
                        act(e_c[:], pC[:], AF.Exp, [pCk], [kc_])
                        mo = 1024 if gb_ == 0 else 0
                        tt(e_c[:], e_c[:], maskb[:, mo:mo + 512], ALU.mult, [kc_, "maskb"], [kc_])
                        if gb_ > 0:
                            pP, pPk = PS()
                            for c in range(4):
                                mm(pP[:, c * 128:(c + 1) * 128], kTe[gs, b * 128:(b + 1) * 128], qT[gs, c, b * 128:(b + 1) * 128],
                                   True, True, ["kTe", "qT"], [pPk])
                            act(e_p[:], pP[:], AF.Exp, [pPk], [kp])
                            mo = 1536 if gb_ == 1 else 512
                            tt(e_p[:], e_p[:], maskb[:, mo:mo + 512], ALU.mult, [kp, "maskb"], [kp])
                    for g in range(2):
                        gs = slice(64 * g, 64 * g + 64)
                        e_p, e_c = (eP, eC) if g == 0 else (eP2, eC2)
                        kp, kc_ = ("eP%d" % g, "eC%d" % g)
                        if gb_ > 0:
                            mm(pO[gs, :], ve[:, b, gs], e_p[:], True, False, ["ve", kp], [pOk])
                            mm(pD[gs, :], onesb[:, 0:64], e_p[:], True, False, ["onesb", kp], [pDk])
                        mm(pO[gs, :], ve[:, b + 1, gs], e_c[:], gb_ == 0, True, ["ve", kc_], [pOk])
                        mm(pD[gs, :], onesb[:, 0:64], e_c[:], gb_ == 0, True, ["onesb", kc_], [pDk])
                    for c in range(4):
                        ts(rd[:, c * 128:(c + 1) * 128], pD[:, c * 128:(c + 1) * 128], es_t[:, l, c:c + 1], None, ALU.add, None,
                           [pDk, "es"], ["rd"])
                    act(rd[:], rd[:], AF.Ln, ["rd"], ["rd"])
                    act(rd[:], rd[:], AF.Exp, ["rd"], ["rd"], scale=-1.0)
                    tt(ob[:, :, b * 128:(b + 1) * 128], pO[:].rearrange("p (c q) -> p c q", c=4), rd[:].rearrange("p (c q) -> p c q", c=4),
                       ALU.mult, [pOk, "rd"], ["ob"])

            if STOP < 4:
                return
            if sample:
                for h in range(4):
                    p, pk = PS()
                    for kc in range(KC):
                        mm(p[:, 0:NS], wV3[:, kc, 128 + h * 128:128 + (h + 1) * 128], ub[:, kc, 0:NS], kc == 0, kc == KC - 1,
                           [wVk, "ub"], [pk])
                    act(vTf[:, h, :], p[:, 0:NS], AF.Copy, [pk], ["vTf"])
                pvr, pvrk = PS()
                for kc in range(KC):
                    mm(pvr[0:NS, :], ub[:, kc, 0:NS], wV3[:, kc, 128:640], kc == 0, kc == KC - 1, [wVk, "ub"], [pvrk])
                cp(vtmf[:, :], pvr[0:NS, :], [pvrk], ["vtmf2"])
            else:
                for b in range(nb):
                    pvr, pvrk = PS()
                    for kc in range(KC):
                        mm(pvr[:], ub[:, kc, b * 128:(b + 1) * 128], wV3[:, kc, 128:640], kc == 0, kc == KC - 1, [wVk, "ub"], [pvrk])
                    act(vr[:, b, :], pvr[:], AF.Copy, [pvrk], [("vr", b)])
            wRq, wRqk = wload(l, "inRq")
            wRk_, wRkk = wload(l, "inRk")
            wR3s = (wRq[:, 0:4096].rearrange("p (k c) -> p k c", k=KC), wRk_[:, 0:4096].rearrange("p (k c) -> p k c", k=KC))
            wRks = (wRqk, wRkk)
            wG, wGk = wload(l, "inGR")
            wG3 = wG[:, 0:4096].rearrange("p (k c) -> p k c", k=KC)
            for (s0, n) in groups:
                for qk in range(2):
                    for ch in range(2):
                        pn, pnk = PS()
                        psw, pswk = PS()
                        cb_ = ch * 128
                        wR3 = wR3s[qk]
                        wRk = wRks[qk]
                        for kc in range(KC):
                            mm(pn[:, 0:n], wR3[:, kc, cb_:cb_ + 128], ub[:, kc, s0:s0 + n], kc == 0, kc == KC - 1, [wRk, "ub"], [pnk])
                        for kc in range(KC):
                            mm(psw[:, 0:n], wR3[:, kc, cb_ + 256:cb_ + 384], ub[:, kc, s0:s0 + n], kc == 0, kc == KC - 1, [wRk, "ub"], [pswk])
                        tb = 2 * qk
                        tt(rt1[:, 0:n], pn[:, 0:n], ropet[:, tb, s0:s0 + n], ALU.mult, [pnk, "ropet"], ["rt1"])
                        tt(rt2[:, 0:n], psw[:, 0:n], ropet[:, tb + 1, s0:s0 + n], ALU.mult, [pswk, "ropet"], ["rt2"])
                        dst = (qr, kr)[qk]
                        tt(dst[:, ch, s0:s0 + n], rt1[:, 0:n], rt2[:, 0:n], ALU.add, ["rt1", "rt2"], [("qr", "kr")[qk]])
                        if sample:
                            dstf = (qrf, krf)[qk]
                            tt(dstf[:, ch, :], rt1[:, 0:n], rt2[:, 0:n], ALU.add, ["rt1", "rt2"], [("qrf", "krf")[qk]])
                for h in range(4):
                    p, pk = PS()
                    for kc in range(KC):
                        mm(p[:, 0:n], wG3[:, kc, h * 128:(h + 1) * 128], ub[:, kc, s0:s0 + n], kc == 0, kc == KC - 1, [wGk, "ub"], [pk])
                    act(sgr[:, h, s0:s0 + n], p[:, 0:n], AF.Silu, [pk], ["sgr"])

            def gate_cols(gcen_t, sg_ap_fn, out_fn, ncols_per_h, heads=(0, 1, 2, 3)):
                for hi, h in enumerate(heads):
                    cs = slice(hi * ncols_per_h, (hi + 1) * ncols_per_h)
                    stt(out_fn(h), gcen_t[:, cs], par[:, l, 60 + h:61 + h], sg_ap_fn(h), ALU.mult, ALU.mult,
                        ["gcen", "par", "sgr"], ["oc"])

            if sample:
                dma(sret[:], cret_in[l], [], ["sret"])
                pQb = [PS(), PS()]
                for h in range(4):
                    hs = slice(64 * (h % 2), 64 * (h % 2) + 64)
                    pQ, pQk = pQb[h % 2]
                    for s in range(NS):
                        c_ = (h // 2) * NS + s
                        mm(pQ[:, c_:c_ + 1], sret[hs, h // 2, s, :], qrf[hs, h // 2, s:s + 1], True, True,
                           ["sret", "qrf"], [pQk])
                tt(prodf[:], qrf[:], krf[:], ALU.mult, ["qrf", "krf"], ["prodf"])
                pDb = [PS(), PS()]
                for h in range(4):
                    hs = slice(64 * (h % 2), 64 * (h % 2) + 64)
                    pDt, pDtk = pDb[h % 2]
                    mm(pDt[:, (h // 2) * NS:(h // 2 + 1) * NS], onesf[hs, :], prodf[hs, h // 2, :], True, True, ["onesf", "prodf"], [pDtk])
                for h in range(4):
                    pDt, pDtk = pDb[h % 2]
                    pQ, pQk = pQb[h % 2]
                    c_ = slice((h // 2) * NS, (h // 2 + 1) * NS)
                    tt(osm[:, h * NS:(h + 1) * NS], pDt[:, c_], vTf[:, h, :], ALU.mult, [pDtk, "vTf"], ["osm"])
                    stt(osm[:, h * NS:(h + 1) * NS], pQ[:, c_], GAM[h], osm[:, h * NS:(h + 1) * NS], ALU.mult, ALU.add,
                        [pQk, "osm"], ["osm"])
                groupnorm_gate(l, osm[:], "osm", 64, None, None,
                               lambda g_, a_, b_: gate_cols(g_, lambda h: sgr[:, h, 0:NS], lambda h: oc[:, h, 0:NS], NS))
                for ch in range(2):
                    ptr, ptrk = PS()
                    S.op("pe", lambda e, ch=ch, ptr=ptr: e.transpose(ptr[0:NS, 0:128], krf[:, ch, :], identf[:]), ["krf", "identf"], [ptrk])
                    cp(ktmf[:, ch * 128:(ch + 1) * 128], ptr[0:NS, 0:128], [ptrk], ["ktmf"])
                for h in range(4):
                    hs = slice(64 * (h % 2), 64 * (h % 2) + 64)
                    ch = h // 2
                    for q in range(4):
                        tt(vbd[:], vtmf[:, h * 128:(h + 1) * 128].unsqueeze(1).to_broadcast([NS, 4, 128]),
                           identf[0:NS, 4 * q:4 * q + 4].unsqueeze(2).to_broadcast([NS, 4, 128]), ALU.mult, ["vtmf2", "identf"], ["vbd"])
                        pk_, pkk = PS()
                        mm(pk_[hs, :], ktmf[:, h * 64:(h + 1) * 64], vbd[:].rearrange("p s e -> p (s e)"), True, True,
                           ["ktmf", "vbd"], [pkk])
                        stt(sret[hs, ch, 4 * q:4 * q + 4, :], sret[hs, ch, 4 * q:4 * q + 4, :], GAM[h],
                            pk_[hs, :].rearrange("p (s e) -> p s e", s=4), ALU.mult, ALU.add, ["sret", pkk], ["sret"])
                dma(sret_out[l], sret[:], ["sret"], [])
            else:
                for b in range(nb):
                    bs = slice(b * 128, (b + 1) * 128)
                    for ch in range(2):
                        tt(kdt[:, ch, :], kr[:, ch, bs], tabf[:, 768 + ch * 128:768 + (ch + 1) * 128], ALU.mult, ["kr", "tabf"], ["kdt"])
                        tt(qdc4[:, b, ch, :], qr[:, ch, bs], tabf[:, 512 + ch * 128:512 + (ch + 1) * 128], ALU.mult, ["qr", "tabf"], [("qdc", b)])
                    for ch in range(2):
                        S.op("pe", lambda e, ch=ch: e.transpose(pst[:, ch * 128:(ch + 1) * 128], kdt[:, ch, :], identb[:]),
                             ["kdt", "identb"], ["pst"])
                    cp(ktm4[:, b, :], pst[:, 0:256], ["pst"], [("ktm", b)])
                    pStb = [PS(), PS()]
                    for h in range(4):
                        hs = slice(64 * (h % 2), 64 * (h % 2) + 64)
                        pSt, pStk = pStb[h % 2]
                        mm(pSt[:, (h // 2) * 128:(h // 2 + 1) * 128], kr[hs, h // 2, bs], qr[hs, h // 2, bs], True, True, ["kr", "qr"], [pStk])
                    for h in range(4):
                        pSt, pStk = pStb[h % 2]
                        tt(aT4[:, b, h * 128:(h + 1) * 128], pSt[:, (h // 2) * 128:(h // 2 + 1) * 128], tabf[:, h * 128:(h + 1) * 128], ALU.mult,
                           [pStk, "tabf"], [("aT", b)])
                for b in range(nb):
                    bs = slice(b * 128, (b + 1) * 128)
                    pOb = [PS(), PS()]
                    for h in range(4):
                        hs = slice(64 * (h % 2), 64 * (h % 2) + 64)
                        pO, pOk = pOb[h % 2]
                        oc_ = slice((h // 2) * 128, (h // 2 + 1) * 128)
                        mm(pO[:, oc_], vr[:, b, h * 128:(h + 1) * 128], aT4[:, b, h * 128:(h + 1) * 128], True, False,
                           [("vr", b), ("aT", b)], [pOk])
                        mm(pO[:, oc_], Sbf[hs, l, h // 2, :], qdc4[hs, b, h // 2, :], False, True, ["Sbf", ("qdc", b)], [pOk])
                    pU, pUk = PS()
                    for h in range(4):
                        hs = slice(64 * (h % 2), 64 * (h % 2) + 64)
                        mm(pU[hs, (h // 2) * 128:(h // 2 + 1) * 128], ktm4[:, b, h * 64:(h + 1) * 64], vr[:, b, h * 128:(h + 1) * 128], True, True,
                           [("ktm", b), ("vr", b)], [pUk])
                    for ch in range(2):
                        stt(Sst[:, l, ch, :], Sst[:, l, ch, :], tabf[:, 1024 + ch:1025 + ch], pU[:, ch * 128:(ch + 1) * 128], ALU.mult, ALU.add,
                            ["Sst", "tabf", pUk], ["Sst"])
                    act(Sbf[:, l, :, :], Sst[:, l, :, :], AF.Copy, ["Sst"], ["Sbf"])
                    ocat_v = ocat[:].rearrange("p (a r c) -> p a r c", a=2, r=2)
                    for r_ in range(2):
                        pO, pOk = pOb[r_]
                        act(ocat_v[:, :, r_, :], pO[:, 0:256].rearrange("p (a c) -> p a c", a=2), AF.Copy, [pOk], ["ocat"])
                    groupnorm_gate(l, ocat[:], "ocat", 512, None, None,
                                   lambda g_, a_, b_, bs=bs: gate_cols(g_, lambda h: sgr[:, h, bs], lambda h: oc[:, h, bs], 128))
                if last:
                    dma(pret_out[l], Sst[:, l, :, :], ["Sst"], [])

            if STOP < 5:
                return
            for m in range(8):
                wt, wk = wload(l, "gb%d" % m)
                wg3 = wt[:, 0:3072].rearrange("p (k c) -> p k c", k=KC)
                wb4 = wt[:, 3072:4608].rearrange("p (b k c) -> p b k c", b=3, k=4)
                for (s0, n) in groups:
                    sgs = (sga, sgb, sgc)
                    for bi in range(3):
                        p, pk = PS()
                        for kc in range(KC):
                            mm(p[:, 0:n], wg3[:, kc, bi * 128:(bi + 1) * 128], ub[:, kc, s0:s0 + n], kc == 0, kc == KC - 1, [wk, "ub"], [pk])
                        act(sgs[bi][:, 0:n], p[:, 0:n], AF.Sigmoid, [pk], ["sg%d" % bi])
                    srcs = ((oa, "oa"), (ob, "ob"), (oc, "oc"))
                    pks = []
                    for bi in range(3):
                        p, pk = PS()
                        for kc in range(4):
                            mm(p[:, 0:n], wb4[:, bi, kc, :], srcs[bi][0][:, kc, s0:s0 + n], kc == 0, kc == 3, [wk, srcs[bi][1]], [pk])
                        pks.append((p, pk))
                    tt(m1[:, 0:n], sga[:, 0:n], pks[0][0][:, 0:n], ALU.mult, ["sg0", pks[0][1]], ["m1"])
                    tt(m2[:, 0:n], sgb[:, 0:n], pks[1][0][:, 0:n], ALU.mult, ["sg1", pks[1][1]], ["m2"])
                    tt(m1[:, 0:n], m1[:, 0:n], m2[:, 0:n], ALU.add, ["m1", "m2"], ["m1"])
                    tt(m2[:, 0:n], sgc[:, 0:n], pks[2][0][:, 0:n], ALU.mult, ["sg2", pks[2][1]], ["m2"])
                    tt(mg[:, m, s0:s0 + n], m1[:, 0:n], m2[:, 0:n], ALU.add, ["m1", "m2"], [("mg", m)])
            if dbg and l == 0 and tile_idx == 1:
                dma(dbg_oa, oa[:, :, 0:512], ["oa"], [])
                dma(dbg_ob, ob[:, :, 0:512], ["ob"], [])
                dma(dbg_oc, oc[:, :, 0:512], ["oc"], [])
                dma(dbg_mg, mg[:, :, 0:512], [("mg", m_) for m_ in range(8)], [])
                dma(dbg_u, ub[:, :, 0:512], ["ub"], [])
                dma(dbg_qr, qr[:, :, 0:512], ["qr"], [])
                dma(dbg_kr, kr[:, :, 0:512], ["kr"], [])
                dma(dbg_vr, vr[:, :, :], [("vr", b_) for b_ in range(4)], [])
                dma(dbg_sgr, sgr[:, :, 0:512], ["sgr"], [])
                dma(dbg_aT, aT4[:, 3, :], [("aT", 3)], [])
                dma(dbg_gno, gno[:], ["gno"], [])
                dma(dbg_S, Sst[:, 0, :, :], ["Sst"], [])
            for half in range(2):
                wt, wk = wload(l, "wo%d" % half)
                w3 = wt[:, 0:4096].rearrange("p (k c) -> p k c", k=KC)
                for (s0, n) in groups:
                    for mc in range(4):
                        p, pk = PS()
                        for kc in range(KC):
                            mm(p[:, 0:n], w3[:, kc, mc * 128:(mc + 1) * 128], mg[:, kc, s0:s0 + n], kc == 0, kc == KC - 1, [wk, ("mg", kc)], [pk])
                        m = 4 * half + mc
                        tt(xt[:, m, s0:s0 + n], p[:, 0:n], xt[:, m, s0:s0 + n], ALU.add, [pk, "xt"], ["xt"])
            if dbg and l == 0 and tile_idx == 1:
                dma(dbg_x, xt[:, :, 0:512], ["xt"], [])

        tiles = []
        for t in range(ntiles):
            if t == 0:
                tiles.append((0, 1))
            else:
                tiles.append((4 * (t - 1) + 1, 4))
        for ti, (blk0, nb) in enumerate(tiles):
            nt = nb * 128
            t0 = blk0 * 128
            groups = [(0, nt)]
            dq = "sp" if ti == 0 else "pool"
            dma(xt[:, :, 0:nt], xin[:, :, t0:t0 + nt], [], ["xt"], eng=dq)
            dma(ropet[:, :, 0:nt], rope_in[:, :, t0:t0 + nt], [], ["ropet"], eng=dq)
            for l in range(nlayers):
                if STOP >= 1:
                    ffn(l, 0, groups)
                if STOP >= 2:
                    mixer(l, groups, nt, ti, blk0, nb, False)
                if STOP >= 6:
                    ffn(l, 1, groups)
            rmsnorm(groups, lambda kc: fnorm[:, kc:kc + 1], xt, "xt")
            dma(y_out[:, :, t0:t0 + nt], xt[:, :, 0:nt], ["xt"], [])
        if do_sample:
            groups = [(0, NS)]
            dma(xt[:, :, 0:NS], xs_in, [], ["xt"])
            dma(ropet[:, :, 0:NS], rope_in[:, :, SEQP:SEQP + NS], [], ["ropet"])
            for l in range(nlayers):
                dma(cconv[:, :, 0:3, :], cconv_in[l], [], ["cconv"])
                dma(clru[:], clru_in[l], [], ["clru"])
                if STOP >= 1:
                    ffn(l, 0, groups)
                if STOP >= 2:
                    mixer(l, groups, NS, -1, 0, 0, True)
                if STOP >= 6:
                    ffn(l, 1, groups)
            rmsnorm(groups, lambda kc: fnorm[:, kc:kc + 1], xt, "xt")
            dma(ys_out, xt[:, :, 0:NS], ["xt"], [])
        S.emit()
        build.stats = S.stats
    return nc


def _prep_inputs(inp, ncores=8):
    inp = {k: np.asarray(v) for k, v in inp.items()}
    Wall = _host_weights(inp)
    P, FN = _host_params(inp)
    TB = _host_tables()
    maps = []
    for c in range(ncores):
        seq = c % 2
        xs = np.concatenate([np.zeros((PADN, D), np.float32), inp["meta_tokens"].astype(np.float32), inp["x_prompt"][seq]], axis=0)
        xin = np.ascontiguousarray(xs.reshape(SEQP, KC, 128).transpose(2, 1, 0))
        ss = slice(NS * c, NS * (c + 1))
        xsm = inp["x_sample"][ss, 0, :]
        xs_in = np.ascontiguousarray(xsm.reshape(NS, KC, 128).transpose(2, 1, 0))
        ck = inp["cache_swa_k"][:, ss]
        ck_t = np.ascontiguousarray(ck.transpose(0, 3, 4, 1, 2).reshape(L, 128, NS, 128))
        cv = inp["cache_swa_v"][:, ss]
        cv_t = np.ascontiguousarray(cv.transpose(0, 2, 1, 3, 4).reshape(L, 128, NS, 128))
        cc = inp["state_conv"][:, ss]
        cc_t = np.ascontiguousarray(cc.reshape(L, NS, 3, 4, 128).transpose(0, 4, 3, 2, 1))
        cl = inp["state_lru"][:, ss]
        cl_t = np.ascontiguousarray(cl.reshape(L, NS, 4, 128).transpose(0, 3, 2, 1))
        cr = inp["state_ret"][:, ss]
        cr_t = np.ascontiguousarray(cr.reshape(L, NS, 2, 2, 64, 128).transpose(0, 3, 4, 2, 1, 5).reshape(L, 128, 2, NS, 128))
        maps.append(dict(xin=xin, xs_in=xs_in, wall0=Wall[0], wall1=Wall[1], wall2=Wall[2], wall3=Wall[3], par=P, fnorm=FN, rope=TB["rope"], tabf=TB["f32"], masks=TB["masks"],
                         c128=TB["c128"], ck=ck_t, cv=cv_t, cconv=cc_t, clru=cl_t, cret=cr_t))
    return maps


def _assemble(results):
    f32 = np.float32
    y = np.zeros((2, SEQ, D), f32)
    ysamp = np.zeros((128, 1, D), f32)
    pk = np.zeros((L, 2, 128, 2, 64), f32)
    pv = np.zeros((L, 2, 128, 2, 64), f32)
    pconv = np.zeros((L, 2, 3, 512), f32)
    plru = np.zeros((L, 2, 512), f32)
    pret = np.zeros((L, 2, 4, 64, 128), f32)
    sk = np.zeros((L, 128, 128, 2, 64), f32)
    sv = np.zeros((L, 128, 128, 2, 64), f32)
    sconv = np.zeros((L, 128, 3, 512), f32)
    slru = np.zeros((L, 128, 512), f32)
    sret = np.zeros((L, 128, 4, 64, 128), f32)
    for c, r in enumerate(results):
        ss = slice(NS * c, NS * (c + 1))
        if c < 2:
            yy = r["y"]
            y[c] = yy.transpose(2, 1, 0).reshape(SEQP, D)[PADN + NMETA:]
            pk[:, c] = r["pk"].transpose(0, 2, 1).reshape(L, 128, 2, 64)
            pv[:, c] = r["pv"].reshape(L, 128, 2, 64)
            pconv[:, c] = r["pconv"].transpose(0, 3, 2, 1).reshape(L, 3, 512)
            plru[:, c] = r["plru"].transpose(0, 2, 1).reshape(L, 512)
            pret[:, c] = r["pret"].reshape(L, 2, 64, 2, 128).transpose(0, 3, 1, 2, 4).reshape(L, 4, 64, 128)
        ysamp[ss, 0] = r["ys"].transpose(2, 1, 0).reshape(NS, D)
        skf = np.concatenate([r["sk"], r["skn"][:, :, :, None]], axis=3)
        sk[:, ss] = skf.reshape(L, 2, 64, NS, 128).transpose(0, 3, 4, 1, 2)
        svf = np.concatenate([r["sv"], r["svn"].transpose(0, 2, 1)[:, :, None, :]], axis=2)
        sv[:, ss] = svf.reshape(L, NS, 128, 2, 64)
        sconv[:, ss] = r["sconv"].transpose(0, 4, 3, 2, 1).reshape(L, NS, 3, 512)
        slru[:, ss] = r["slru"].transpose(0, 3, 2, 1).reshape(L, NS, 512)
        sret[:, ss] = r["sret"].reshape(L, 2, 64, 2, NS, 128).transpose(0, 4, 3, 1, 2, 5).reshape(L, NS, 4, 64, 128)
    return (y, ysamp, pk, pv, pconv, plru, pret, sk, sv, sconv, slru, sret)


def kernel(**inputs):
    maps = _prep_inputs(inputs)
    nc = build()
    res = run_bass_kernel_spmd(nc, maps, core_ids=list(range(8)))
    return _assemble(res.results)
```

```python
import numpy as np
from contextlib import ExitStack
import concourse.bass as bass
import concourse.mybir as mybir
from concourse.bass_utils import run_bass_kernel_spmd

F32 = mybir.dt.float32
BF16 = mybir.dt.bfloat16
AF = mybir.ActivationFunctionType
ALU = mybir.AluOpType

D = 1024
DFF = 2048
L = 4
KC = 8
PADN = 112
NMETA = 16
SEQ = 8192
SEQP = PADN + NMETA + SEQ
NBLK = SEQP // 128
NS = 16
NTMAX = 512
EPS = 1e-6
GN_EPS = 1e-5
WSLOT = 5120
NWSLOT = 3

ENGS = ("pe", "act", "dve", "pool", "sp")
NDMA_SEMS = 40
NSW_SEMS = 12
SEM_EPOCH = 4000


class _Op:
    __slots__ = ("eng", "fn", "deps", "dma", "sig", "sigval", "dsem", "dval", "dprev")

    def __init__(self, eng, fn, deps, dma):
        self.eng = eng
        self.fn = fn
        self.deps = deps
        self.dma = dma
        self.sig = False
        self.sigval = 0
        self.dsem = None
        self.dval = 0
        self.dprev = None


class Sched:
    def __init__(self, nc):
        self.nc = nc
        self.ops = []
        self.lastw = {}
        self.readers = {}

    def op(self, eng, fn, r=(), w=(), dma=False):
        al = getattr(self, "alias", {})
        r = [al.get(k, k) if isinstance(k, str) else k for k in r]
        w = [al.get(k, k) if isinstance(k, str) else k for k in w]
        idx = len(self.ops)
        deps = set()
        for k in r:
            if k in self.lastw:
                deps.add(self.lastw[k])
        for k in w:
            if k in self.lastw:
                deps.add(self.lastw[k])
            deps.update(self.readers.get(k, ()))
        for k in w:
            self.lastw[k] = idx
            self.readers[k] = []
        for k in r:
            self.readers.setdefault(k, []).append(idx)
        deps.discard(idx)
        self.ops.append(_Op(eng, fn, deps, dma))
        return idx

    def emit(self, final_wait_eng="sp"):
        nc = self.nc
        ops = self.ops
        alldma = {i for i, o in enumerate(ops) if o.dma}
        ops.append(_Op(final_wait_eng, None, alldma, False))
        for o in ops:
            for d in o.deps:
                p = ops[d]
                if p.dma:
                    continue
                if p.eng == "pe" and o.eng == "pe" and not o.dma:
                    continue
                p.sig = True
        cnt = {e: 0 for e in ENGS}
        for o in ops:
            if o.dma:
                continue
            if o.sig:
                cnt[o.eng] += 1
                o.sigval = cnt[o.eng]
        dtot = [0] * NDMA_SEMS
        dlast = [None] * NDMA_SEMS
        nd = 0
        nsw = 0
        nhw = 0
        for i, o in enumerate(ops):
            if o.dma:
                if o.eng == "pool":
                    s = nsw % NSW_SEMS
                    nsw += 1
                else:
                    s = NSW_SEMS + nhw % (NDMA_SEMS - NSW_SEMS)
                    nhw += 1
                nd += 1
                o.dsem = s
                o.dprev = dlast[s]
                dtot[s] += 16
                o.dval = dtot[s]
                dlast[s] = i
        plans = []
        seen = {e: {} for e in ENGS}
        for i, o in enumerate(ops):
            need = {}
            deps = set(o.deps)
            if o.dma and o.dprev is not None:
                deps.add(o.dprev)
            for d in deps:
                p = ops[d]
                if p.dma:
                    key = ("d", p.dsem)
                    val = p.dval
                else:
                    if p.eng == "pe" and o.eng == "pe" and not o.dma:
                        continue
                    key = ("e", p.eng, (p.sigval - 1) // SEM_EPOCH)
                    val = (p.sigval - 1) % SEM_EPOCH + 1
                if need.get(key, 0) < val:
                    need[key] = val
            w = []
            sd = seen[o.eng]
            for key, val in need.items():
                if sd.get(key, 0) >= val:
                    continue
                sd[key] = val
                w.append((key, val))
            plans.append(w)
        self.stats = dict(cnt=cnt, ndma=nd, nops=len(ops))
        with ExitStack() as es:
            esem = {e: [es.enter_context(nc.semaphore("s_%s%d" % (e, i))) for i in range(cnt[e] // SEM_EPOCH + 1)] for e in ENGS}
            dsem = [es.enter_context(nc.semaphore("d%d" % i)) for i in range(NDMA_SEMS)]
            block = es.enter_context(nc.Block())

            def mk(ename):
                def body(eng):
                    for i, o in enumerate(ops):
                        if o.eng != ename:
                            continue
                        for key, val in plans[i]:
                            sem = dsem[key[1]] if key[0] == "d" else esem[key[1]][key[2]]
                            eng.wait_ge(sem, val)
                        if o.fn is None:
                            continue
                        ins = o.fn(eng)
                        if o.dma:
                            ins.then_inc(dsem[o.dsem], 16)
                        elif o.sig:
                            ins.then_inc(esem[ename][(o.sigval - 1) // SEM_EPOCH], 1)
                return body

            block.tensor(mk("pe"))
            block.scalar(mk("act"))
            block.vector(mk("dve"))
            block.gpsimd(mk("pool"))
            block.sync(mk("sp"))


def _wblocks():
    bl = []
    for f in (1, 2):
        for hf in range(2):
            for b in range(4):
                bl.append(("gu%d_%d" % (f, 4 * hf + b), 4096))
            for b in range(2):
                bl.append(("dn%d_%d" % (f, 2 * hf + b), 4096))
        if f == 1:
            bl += [("inXA", 4096), ("inYA", 4096), ("lru", 1024), ("inQK", 5120), ("inV", 5120), ("inRq", 4096), ("inRk", 4096),
                   ("inGR", 4096)]
            for m in range(8):
                bl.append(("gb%d" % m, 4608))
            bl += [("wo0", 4096), ("wo1", 4096)]
    off = {}
    o = 0
    for n, x in bl:
        off[n] = (o, x)
        o += x
    return bl, off, o


WBL, WOFF, XTOT = _wblocks()


def _tile_w(W, cols=None):
    K = W.shape[0]
    a = W.reshape(K // 128, 128, W.shape[1]).transpose(1, 0, 2)
    if cols is not None:
        a = a[:, :, cols]
    return np.ascontiguousarray(a).reshape(128, -1)


def _host_weights(inp):
    Wall = np.zeros((L, 128, XTOT), np.float32)
    r = np.arange
    qs0, ks0, vs0, qr0, kr0, vr0, gr0, g0 = 1024, 1536, 1664, 1792, 2048, 2304, 2816, 3328
    qs_cols = np.concatenate([np.concatenate([qs0 + 64 * c + r(64), qs0 + 64 * (4 + c) + r(64)]) for c in range(4)])

    def sw(base):
        return np.concatenate([np.concatenate([base + 64 * h + 32 + r(32), base + 64 * h + r(32)]) for h in range(4)])

    ob_rows = np.concatenate([np.array([(4 * (p // 64) + c) * 64 + p % 64 for p in range(128)]) for c in range(4)])
    for l in range(L):
        def put(name, arr):
            o, x = WOFF[name]
            assert arr.shape == (128, x), (name, arr.shape, x)
            Wall[l, :, o:o + x] = arr
        for f, (gu, dn) in ((1, ("ffn1_w_gu", "ffn1_w_down")), (2, ("ffn2_w_gu", "ffn2_w_down"))):
            Wg = inp[gu][l]
            Wd = inp[dn][l]
            for b in range(8):
                cols = np.concatenate([256 * b + r(256), 2048 + 256 * b + r(256)])
                put("gu%d_%d" % (f, b), _tile_w(Wg, cols))
            for hf in range(2):
                for b in range(2):
                    put("dn%d_%d" % (f, 2 * hf + b), _tile_w(Wd[1024 * hf:1024 * (hf + 1)], 512 * b + r(512)))
        Wi = inp["w_in"][l]
        put("inXA", _tile_w(Wi, r(512)))
        put("inYA", _tile_w(Wi, 512 + r(512)))
        put("inQK", _tile_w(Wi, np.concatenate([qs_cols, ks0 + r(128)])))
        put("inV", _tile_w(Wi, np.concatenate([vs0 + r(128), vr0 + r(512)])))
        put("inRq", _tile_w(Wi, np.concatenate([qr0 + r(256), sw(qr0)])))
        put("inRk", _tile_w(Wi, np.concatenate([kr0 + r(256), sw(kr0)])))
        put("inGR", _tile_w(Wi, gr0 + r(512)))
        bd = np.zeros((128, 2, 4, 128), np.float32)
        for wi, nm in enumerate(("lru_w_a", "lru_w_x")):
            Wl = inp[nm][l]
            for j in range(4):
                for hb in range(2):
                    bd[64 * hb:64 * hb + 64, wi, j, 64 * hb:64 * hb + 64] = Wl[2 * j + hb]
        put("lru", bd.reshape(128, -1))
        Wa = inp["w_branch_a"][l]
        Wb = inp["w_branch_b"][l][ob_rows]
        Wc = inp["w_branch_c"][l]
        for m in range(8):
            g = _tile_w(Wi, np.concatenate([g0 + 128 * m + r(128), g0 + 1024 + 128 * m + r(128), g0 + 2048 + 128 * m + r(128)]))
            br = np.stack([_tile_w(Wx, 128 * m + r(128)).reshape(128, 4, 128) for Wx in (Wa, Wb, Wc)], axis=1)
            put("gb%d" % m, np.concatenate([g, br.reshape(128, -1)], axis=1))
        Wo = inp["w_out"][l]
        put("wo0", _tile_w(Wo, r(512)))
        put("wo1", _tile_w(Wo, 512 + r(512)))
    return Wall


def _fm(v):
    v = np.asarray(v, np.float32)
    F = v.shape[-1]
    a = v.reshape(v.shape[:-1] + (F // 128, 128))
    return np.ascontiguousarray(np.moveaxis(a, -1, 0))


NPAR = 3 * 8 + 16 + 4 * 5 + 4 + 4


def _host_params(inp):
    P = np.zeros((128, L, NPAR), np.float32)
    for l in range(L):
        o = 0
        for nm in ("ffn1_norm", "mix_norm", "ffn2_norm"):
            P[:, l, o:o + 8] = _fm(inp[nm][l])
            o += 8
        cw = _fm(inp["conv_w"][l])
        P[:, l, o:o + 16] = cw.transpose(0, 2, 1).reshape(128, 16)
        o += 16
        for nm in ("conv_b", "lru_b_a", "lru_b_x", "lru_lambda"):
            P[:, l, o:o + 4] = _fm(inp[nm][l])
            o += 4
        sk = inp["swa_sinks"][l]
        P[:, l, o:o + 4] = np.array([[sk[4 * (p // 64) + c] for c in range(4)] for p in range(128)], np.float32)
        o += 4
        P[:, l, o:o + 4] = _fm(inp["ret_norm"][l])
        o += 4
    return P, _fm(inp["final_norm"])


def _host_tables():
    T = {}
    p = np.arange(128)
    dd = p % 64
    inv = (10000.0 ** (-(np.arange(32, dtype=np.float32)) / 32)).astype(np.float32)
    pos = np.concatenate([np.arange(SEQP) - PADN, np.full(NS, SEQ)]).astype(np.float32)
    ang = (pos[None, :] * inv[dd % 32][:, None]).astype(np.float32).astype(np.float64)
    cos = np.cos(ang)
    sin = np.sin(ang) * np.where(dd < 32, -1.0, 1.0)[:, None]
    T["rope"] = np.stack([cos, sin, cos / 8, sin / 8], axis=1).astype(np.float32)
    lg = np.log1p(-np.exp2(-5.0 - np.arange(4, dtype=np.float32))).astype(np.float32).astype(np.float64)
    j = np.arange(128)[:, None]
    i = np.arange(128)[None, :]
    DT = np.zeros((128, 4, 128))
    for h in range(4):
        DT[:, h, :] = np.where(i >= j, np.exp((i - j) * lg[h]), 0.0)
    n = np.arange(128)
    hd = lambda ch: 2 * ch + p // 64
    cross = np.stack([np.exp((n[None, :] + 1.0) * lg[hd(ch)][:, None]) for ch in range(2)], axis=1)
    kdec = np.stack([np.exp((127.0 - n[None, :]) * lg[hd(ch)][:, None]) for ch in range(2)], axis=1)
    g128 = np.stack([np.exp(128.0 * lg[hd(ch)]) for ch in range(2)], axis=1)
    g1 = np.stack([np.exp(lg[hd(ch)]) for ch in range(2)], axis=1)
    pm = np.ones((128, 128))
    pm[:, :PADN] = 0.0
    T["f32"] = np.concatenate([DT.reshape(128, 512), cross.reshape(128, 256), kdec.reshape(128, 256), g128, g1, pm],
                              axis=1).astype(np.float32)
    mC = (j <= i)
    mP = (j > i)
    big = (j >= PADN)
    ms = [np.tile(m.astype(np.float32), (1, 4)) for m in (mC, mP, mC & big, mP & big)]
    T["masks"] = np.concatenate(ms, axis=1).astype(np.float32)
    ident = np.eye(128, dtype=np.float32)
    T["c128"] = np.concatenate([np.ones((128, 128)), np.full((128, 128), 1.0 / 128), ident], axis=1).astype(np.float32)
    bd = np.zeros((16, 16, 128), np.float32)
    for s in range(16):
        bd[s, s, :] = 1.0
    T["bdmask"] = bd.reshape(16, 2048)
    T["gam"] = [float(np.exp(lg[h])) for h in range(4)]
    return T


TABF = 512 + 256 + 256 + 2 + 2 + 128


def build(ntiles=17, do_sample=True, nlayers=L, dbg=False):
    import os
    STOP = int(os.environ.get("STOP", "9"))
    STOPC = int(os.environ.get("STOPC", "9"))
    STOPB = int(os.environ.get("STOPB", "9"))
    nc = bass.Bass("TRN2", target_bir_lowering=False)
    TB = _host_tables()
    GAM = TB["gam"]

    def din(name, shape, dt=F32):
        return nc.dram_tensor(name, list(shape), dt, kind="ExternalInput").ap()

    def dout(name, shape, dt=F32):
        return nc.dram_tensor(name, list(shape), dt, kind="ExternalOutput").ap()

    xin = din("xin", [128, KC, SEQP])
    xs_in = din("xs_in", [128, KC, NS])
    wall = [din("wall%d" % l_, [128, XTOT]) for l_ in range(L)]
    par_in = din("par", [128, L, NPAR])
    fn_in = din("fnorm", [128, KC])
    rope_in = din("rope", [128, 4, SEQP + NS])
    tab_in = din("tabf", [128, TABF])
    mask_in = din("masks", [128, 2048])
    c128_in = din("c128", [128, 384])
    ck_in = din("ck", [L, 128, NS, 128])
    cv_in = din("cv", [L, 128, NS, 128])
    cconv_in = din("cconv", [L, 128, 4, 3, NS])
    clru_in = din("clru", [L, 128, 4, NS])
    cret_in = din("cret", [L, 128, 2, NS, 128])

    wbf = [nc.dram_tensor("wbf%d" % l_, [128, XTOT], BF16, kind="Internal").ap() for l_ in range(L)]

    y_out = dout("y", [128, KC, SEQP])
    ys_out = dout("ys", [128, KC, NS])
    pk_out = dout("pk", [L, 128, 128])
    pv_out = dout("pv", [L, 128, 128])
    pconv_out = dout("pconv", [L, 128, 4, 3])
    plru_out = dout("plru", [L, 128, 4])
    pret_out = dout("pret", [L, 128, 2, 128])
    sk_out = dout("sk", [L, 128, NS, 127])
    sv_out = dout("sv", [L, NS, 127, 128])
    skn_out = dout("skn", [L, 128, NS])
    svn_out = dout("svn", [L, 128, NS])
    sconv_out = dout("sconv", [L, 128, 4, 3, NS])
    slru_out = dout("slru", [L, 128, 4, NS])
    sret_out = dout("sret", [L, 128, 2, NS, 128])
    if dbg:
        dbg_oa = dout("dbg_oa", [128, 4, 512], BF16)
        dbg_ob = dout("dbg_ob", [128, 4, 512], BF16)
        dbg_oc = dout("dbg_oc", [128, 4, 512], BF16)
        dbg_mg = dout("dbg_mg", [128, 8, 512], BF16)
        dbg_x = dout("dbg_x", [128, 8, 512])
        dbg_u = dout("dbg_u", [128, 8, 512], BF16)
        dbg_qr = dout("dbg_qr", [128, 2, 512], BF16)
        dbg_kr = dout("dbg_kr", [128, 2, 512], BF16)
        dbg_vr = dout("dbg_vr", [128, 4, 512], BF16)
        dbg_sgr = dout("dbg_sgr", [128, 4, 512], BF16)
        dbg_aT = dout("dbg_aT", [128, 512], BF16)
        dbg_gno = dout("dbg_gno", [128, 512], BF16)
        dbg_S = dout("dbg_S", [128, 2, 128])

    es = ExitStack()
    with es:
        def sb(name, shape, dt=F32):
            return es.enter_context(nc.sbuf_tensor(name, list(shape), dt))

        S = Sched(nc)
        xt = sb("xt", [128, KC, NTMAX])
        ub = sb("ub", [128, KC, NTMAX], BF16)
        hb = sb("hb", [128, 8, NTMAX], BF16)
        rstd = sb("rstd", [128, 512])
        wsl = [sb("wsl%d" % i, [128, WSLOT], BF16) for i in range(NWSLOT)]
        par = sb("par_t", [128, L, NPAR])
        fnorm = sb("fnormt", [128, KC])
        c1 = sb("c1", [128, L, 4])
        c2 = sb("c2", [128, L, 4])
        es_t = sb("es_t", [128, L, 4])
        ropet = sb("ropet", [128, 4, NTMAX])
        tabf = sb("tabf_t", [128, TABF])
        maskb = sb("maskb", [128, 2048], BF16)
        c128 = sb("c128t", [128, 384])
        onesb = sb("onesb", [128, 128], BF16)
        onesm = sb("onesm", [128, 128], BF16)
        identb = sb("identb", [128, 128], BF16)
        xaE = sb("xaE", [128, 3 + NTMAX])
        xcb = sb("xcb", [128, NTMAX], BF16)
        oa = sb("oa", [128, 4, NTMAX], BF16)
        ob = sb("ob", [128, 4, NTMAX], BF16)
        oc = sb("oc", [128, 4, NTMAX], BF16)
        mg = sb("mg", [128, KC, NTMAX], BF16)
        hst = sb("hst", [128, L, 4])
        cst = sb("cst", [128, L, 4, 3])
        qT = sb("qT", [128, 4, NTMAX], BF16)
        kTe = sb("kTe", [128, 128 + NTMAX], BF16)
        ve = sb("ve", [128, 5, 128], BF16)
        kst = sb("kst", [128, L, 128], BF16)
        vst = sb("vst", [128, L, 128], BF16)
        kof = sb("kof", [128, NTMAX])
        vof = sb("vof", [128, 128])
        eP = sb("eP", [128, 512], BF16)
        eC = sb("eC", [128, 512], BF16)
        qr = sb("qr", [128, 2, NTMAX], BF16)
        kr = sb("kr", [128, 2, NTMAX], BF16)
        vr = sb("vr", [128, 4, 512], BF16)
        sgr = sb("sgr", [128, 4, NTMAX], BF16)
        kdt = sb("kdt", [128, 2, 128], BF16)
        ktm4 = sb("ktm4", [128, 4, 256], BF16)
        qdc4 = sb("qdc4", [128, 4, 2, 128], BF16)
        aT4 = sb("aT4", [128, 4, 512], BF16)
        Sst = sb("Sst", [128, L, 2, 128])
        Sbf = sb("Sbf", [128, L, 2, 128], BF16)
        gno = sb("gno", [128, 512], BF16)
        ocat = sb("ocat", [128, 512])
        gnq = sb("gnq", [128, 512], BF16)
        sga = sb("sga", [128, 512])
        sgb = sb("sgb", [128, 512])
        sgc = sb("sgc", [128, 512])
        m1 = sb("m1", [128, 512])
        m2 = sb("m2", [128, 512])
        rd = sb("rd", [128, 512])
        tmpa = sb("tmpa", [128, 512])
        gmean, gt1, gt2, gcen = sga, sgb, sgc, m1
        rt1, rt2 = m2, tmpa
        la_r, la_i, la_a, la_g, la_h, xc, gy = sga, sgb, sgc, m1, m2, rd, tmpa
        eP2 = sb("eP2", [128, 512], BF16)
        eC2 = sb("eC2", [128, 512], BF16)
        KEYALIAS = dict(gmean="sg0", gt1="sg1", gt2="sg2", gcen="m1", rt1="m2", rt2="tmpa", la_r="sg0", la_i="sg1", la_a="sg2",
                        la_g="m1", la_h="m2", xc="rd", gy="tmpa")
        S.alias = KEYALIAS
        if do_sample:
            kmod = sb("kmod", [128, NS, 128], BF16)
            vmod = sb("vmod", [128, NS, 128], BF16)
            cconv = sb("cconv_t", [128, 4, 4, NS])
            clru = sb("clru_t", [128, 4, NS])
            sret = sb("sret_t", [128, 2, NS, 128])
            qrf = sb("qrf", [128, 2, NS])
            krf = sb("krf", [128, 2, NS])
            prodf = sb("prodf", [128, 2, NS])
            vTf = sb("vTf", [128, 4, NS])
            vtmf = sb("vtmf", [NS, 512])
            vnT = sb("vnT", [128, NS])
            qTf = sb("qTf", [128, 4, NS])
            prodq = sb("prodq", [128, 4, NS])
            en = sb("en", [128, 64])
            ktmf = sb("ktmf", [NS, 256])
            vbd = sb("vbd", [NS, 4, 128])
            onesf = sb("onesf", [128, 128])
            identf = sb("identf", [128, 128])
            osm = sb("osm", [128, 64])
            eS = sb("eS", [128, 128], BF16)
            hnew = sb("hnew", [128, 4, NS])

        psb = [es.enter_context(nc.psum_tensor("psb%d" % i, [128, 512], F32)) for i in range(7)]
        pst = es.enter_context(nc.psum_tensor("pst", [128, 1024], BF16))
        psi = [0]

        def PS():
            i = psi[0] % 7
            psi[0] += 1
            return psb[i], ("ps", i)

        def act(out, in_, func, r, w, **kw):
            S.op("act", lambda e: e.activation(out=out, in_=in_, func=func, **kw), r, w)

        def tt(out, in0, in1, op, r, w, eng="dve"):
            S.op(eng, lambda e: e.tensor_tensor(out=out, in0=in0, in1=in1, op=op), r, w)

        def ts(out, in0, s1, s2, op0, op1, r, w, eng="dve"):
            if op1 is None:
                S.op(eng, lambda e: e.tensor_scalar(out=out, in0=in0, scalar1=s1, scalar2=None, op0=op0), r, w)
            else:
                S.op(eng, lambda e: e.tensor_scalar(out=out, in0=in0, scalar1=s1, scalar2=s2, op0=op0, op1=op1), r, w)

        def stt(out, in0, sc, in1, op0, op1, r, w):
            S.op("dve", lambda e: e.scalar_tensor_tensor(out=out, in0=in0, scalar=sc, in1=in1, op0=op0, op1=op1), r, w)

        def cp(out, in_, r, w, eng="dve"):
            S.op(eng, lambda e: e.tensor_copy(out=out, in_=in_), r, w)

        def mm(out, lhsT, rhs, start, stop, r, w):
            S.op("pe", lambda e: e.matmul(out, lhsT, rhs, start=start, stop=stop), r, w)

        def dma(out, in_, r, w, eng="pool", **kw):
            S.op(eng, lambda e: e.dma_start(out=out, in_=in_, **kw), r, w, dma=True)

        def recip(out, in_, r, w):
            S.op("dve", lambda e: e.reciprocal(out=out, in_=in_), r, w)

        def memset(ap, val, w, eng="dve"):
            S.op(eng, lambda e: e.memset(ap, val), (), w)

        dma(par[:], par_in, [], ["par"], eng="sp")
        dma(fnorm[:], fn_in, [], ["fnorm"], eng="sp")
        dma(tabf[:], tab_in, [], ["tabf"], eng="sp")
        dma(c128[:], c128_in, [], ["c128"], eng="sp")
        for q_ in range(4):
            dma(rt1[:], mask_in[:, q_ * 512:(q_ + 1) * 512], [], ["rt1"], eng="sp")
            cp(maskb[:, q_ * 512:(q_ + 1) * 512], rt1[:], ["rt1"], ["maskb"])
        cp(onesb[:], c128[:, 0:128], ["c128"], ["onesb"])
        cp(onesm[:], c128[:, 128:256], ["c128"], ["onesm"])
        cp(identb[:], c128[:, 256:384], ["c128"], ["identb"])
        if do_sample:
            cp(onesf[:], c128[:, 0:128], ["c128"], ["onesf"])
            cp(identf[:], c128[:, 256:384], ["c128"], ["identf"])
        CH = 8192
        for l in range(nlayers):
            for o in range(0, XTOT, CH):
                n = min(CH, XTOT - o)
                dma(wbf[l][:, o:o + n], wall[l][:, o:o + n], [], [("wbf", l, o // CH), ("castslot", (o // CH) % 6)], eng="pool", max_dma_last_dim=8192)
        PO = dict(norm=0, cw=24, cb=40, ba=44, bx=48, lam=52, sk=56, rg=60)
        for l in range(nlayers):
            act(c1[:, l, :], par[:, l, 52:56], AF.Exp, ["par"], ["c1"], scale=-1.0)
            act(c1[:, l, :], c1[:, l, :], AF.Ln, ["c1"], ["c1"], bias=1.0)
            ts(c2[:, l, :], c1[:, l, :], -16.0, None, ALU.mult, None, ["c1"], ["c2"])
            ts(c1[:, l, :], c1[:, l, :], -8.0, None, ALU.mult, None, ["c1"], ["c1"])
            act(es_t[:, l, :], par[:, l, 56:60], AF.Exp, ["par"], ["es"])
        memset(hst[:], 0.0, ["hst"])
        memset(cst[:], 0.0, ["cst"])
        memset(kst[:], 0.0, ["kst"])
        memset(vst[:], 0.0, ["vst"])
        memset(Sst[:], 0.0, ["Sst"])
        memset(Sbf[:], 0.0, ["Sbf"])

        wctr = [0]

        def wload(l, name):
            o, x = WOFF[name]
            i = wctr[0] % NWSLOT
            wctr[0] += 1
            key = ("wsl", i)
            r = [("wbf", l, c) for c in range(o // CH, (o + x - 1) // CH + 1)]
            dma(wsl[i][:, 0:x], wbf[l][:, o:o + x], r, [key], eng="sp")
            return wsl[i], key

        def rmsnorm(groups, gvec, out_t, outkey, okeys_extra=()):
            for (s0, n) in groups:
                for kc in range(KC):
                    act(hb[:, kc, 0:n], xt[:, kc, s0:s0 + n], AF.Square, ["xt"], [("hb", kc)])
                p, pk = PS()
                for kc in range(KC):
                    mm(p[:, 0:n], onesb[:], hb[:, kc, 0:n], kc == 0, kc == KC - 1, [("hb", kc), "onesb"], [pk])
                act(rstd[:, 0:n], p[:, 0:n], AF.Ln, [pk], ["rstd"], scale=1.0 / D, bias=EPS)
                act(rstd[:, 0:n], rstd[:, 0:n], AF.Exp, ["rstd"], ["rstd"], scale=-0.5)
                for kc in range(KC):
                    stt(out_t[:, kc, s0:s0 + n], xt[:, kc, s0:s0 + n], gvec(kc), rstd[:, 0:n], ALU.mult, ALU.mult,
                        ["xt", "rstd", "par", "fnorm"], [outkey, (outkey, kc)])

        def ffn(l, which, groups):
            f = which + 1
            gsel = 0 if which == 0 else 2
            rmsnorm(groups, lambda kc: par[:, l, gsel * 8 + kc:gsel * 8 + kc + 1], ub, "ub")
            for hf in range(2):
                for b in range(4):
                    wt, wk = wload(l, "gu%d_%d" % (f, 4 * hf + b))
                    w3 = wt[:, 0:4096].rearrange("p (k c) -> p k c", k=KC)
                    for (s0, n) in groups:
                        if hf == 0 and b == 0:
                            tiles4 = [(PS(), PS()) for _ in range(2)]
                            for kc in range(KC):
                                for fc in range(2):
                                    (pg, pgk), (pu, puk) = tiles4[fc]
                                    mm(pg[:, 0:n], w3[:, kc, fc * 128:(fc + 1) * 128], ub[:, kc, s0:s0 + n], kc == 0, kc == KC - 1,
                                       [wk, ("ub", kc)], [pgk])
                                    mm(pu[:, 0:n], w3[:, kc, 256 + fc * 128:256 + (fc + 1) * 128], ub[:, kc, s0:s0 + n], kc == 0,
                                       kc == KC - 1, [wk, ("ub", kc)], [puk])
                            for fc in range(2):
                                (pg, pgk), (pu, puk) = tiles4[fc]
                                act(tmpa[:, 0:n], pg[:, 0:n], AF.Silu, [pgk], ["tmpa"])
                                tt(hb[:, 2 * b + fc, s0:s0 + n], tmpa[:, 0:n], pu[:, 0:n], ALU.mult, ["tmpa", puk], [("hb", 2 * b + fc)])
                            continue
                        for fc in range(2):
                            pg, pgk = PS()
                            pu, puk = PS()
                            for kc in range(KC):
                                mm(pg[:, 0:n], w3[:, kc, fc * 128:(fc + 1) * 128], ub[:, kc, s0:s0 + n], kc == 0, kc == KC - 1,
                                   [wk, "ub"], [pgk])
                            for kc in range(KC):
                                mm(pu[:, 0:n], w3[:, kc, 256 + fc * 128:256 + (fc + 1) * 128], ub[:, kc, s0:s0 + n], kc == 0,
                                   kc == KC - 1, [wk, "ub"], [puk])
                            act(tmpa[:, 0:n], pg[:, 0:n], AF.Silu, [pgk], ["tmpa"])
                            tt(hb[:, 2 * b + fc, s0:s0 + n], tmpa[:, 0:n], pu[:, 0:n], ALU.mult, ["tmpa", puk], [("hb", 2 * b + fc)])
                for b in range(2):
                    wt, wk = wload(l, "dn%d_%d" % (f, 2 * hf + b))
                    w3 = wt[:, 0:4096].rearrange("p (k c) -> p k c", k=KC)
                    for (s0, n) in groups:
                        for mc in range(4):
                            p, pk = PS()
                            for fc in range(8):
                                mm(p[:, 0:n], w3[:, fc, mc * 128:(mc + 1) * 128], hb[:, fc, s0:s0 + n], fc == 0, fc == 7,
                                   [wk, ("hb", fc)], [pk])
                            m = 4 * b + mc
                            stt(xt[:, m, s0:s0 + n], p[:, 0:n], 0.5, xt[:, m, s0:s0 + n], ALU.mult, ALU.add, [pk, "xt"], ["xt"])

        def groupnorm_gate(l, osrc, okey, ncols, sg_ap, out_ap, h_of_col):
            act(gno[:, 0:ncols], osrc, AF.Copy, [okey], ["gno"])
            tt(gnq[:, 0:ncols], osrc, osrc, ALU.mult, [okey], ["gnq"])
            pm_, pmk = PS()
            pq_, pqk = PS()
            mm(pm_[:, 0:ncols], onesm[:], gno[:, 0:ncols], True, True, ["gno", "onesm"], [pmk])
            mm(pq_[:, 0:ncols], onesm[:], gnq[:, 0:ncols], True, True, ["gnq", "onesm"], [pqk])
            act(gmean[:, 0:ncols], pm_[:, 0:ncols], AF.Copy, [pmk], ["gmean"])
            tt(gt1[:, 0:ncols], gmean[:, 0:ncols], gmean[:, 0:ncols], ALU.mult, ["gmean"], ["gt1"])
            tt(gt1[:, 0:ncols], pq_[:, 0:ncols], gt1[:, 0:ncols], ALU.subtract, [pqk, "gt1"], ["gt1"])
            ts(gt1[:, 0:ncols], gt1[:, 0:ncols], 0.0, None, ALU.max, None, ["gt1"], ["gt1"])
            act(gt2[:, 0:ncols], gt1[:, 0:ncols], AF.Ln, ["gt1"], ["gt2"], bias=GN_EPS)
            act(gt2[:, 0:ncols], gt2[:, 0:ncols], AF.Exp, ["gt2"], ["gt2"], scale=-0.5)
            tt(gcen[:, 0:ncols], osrc, gmean[:, 0:ncols], ALU.subtract, [okey, "gmean"], ["gcen"])
            tt(gcen[:, 0:ncols], gcen[:, 0:ncols], gt2[:, 0:ncols], ALU.mult, ["gcen", "gt2"], ["gcen"])
            h_of_col(gcen, sg_ap, out_ap)

        def mixer(l, groups, nt, tile_idx, blk0, nb, sample):
            last = (not sample) and ((blk0 + nb == NBLK) or (os.environ.get("LASTDBG") and tile_idx == ntiles - 1))
            rmsnorm(groups, lambda kc: par[:, l, 8 + kc:8 + kc + 1], ub, "ub")
            wA, wAk = wload(l, "inXA")
            wA3 = wA[:, 0:4096].rearrange("p (k c) -> p k c", k=KC)
            wY, wYk = wload(l, "inYA")
            wY3 = wY[:, 0:4096].rearrange("p (k c) -> p k c", k=KC)
            wLt, wLk = wload(l, "lru")
            wL = wLt[:, 0:1024].rearrange("p (w j c) -> p w j c", w=2, j=4)
            for j in range(4):
                if sample:
                    tap = lambda t: cconv[:, j, t, :]
                    newx = cconv[:, j, 3, :]
                    xkey = "cconv"
                else:
                    tap = lambda t: xaE[:, t:t + nt]
                    newx = None
                    xkey = "xaE"
                    cp(xaE[:, 0:3], cst[:, l, j, :], ["cst"], ["xaE"])
                for (s0, n) in groups:
                    p, pk = PS()
                    p2, p2k = PS()
                    if j == 0:
                        for kc in range(KC):
                            mm(p[:, 0:n], wA3[:, kc, j * 128:(j + 1) * 128], ub[:, kc, s0:s0 + n], kc == 0, kc == KC - 1, [wAk, ("ub", kc)], [pk])
                            mm(p2[:, 0:n], wY3[:, kc, j * 128:(j + 1) * 128], ub[:, kc, s0:s0 + n], kc == 0, kc == KC - 1,
                               [wYk, ("ub", kc)], [p2k])
                    else:
                        for kc in range(KC):
                            mm(p[:, 0:n], wA3[:, kc, j * 128:(j + 1) * 128], ub[:, kc, s0:s0 + n], kc == 0, kc == KC - 1, [wAk, "ub"], [pk])
                        for kc in range(KC):
                            mm(p2[:, 0:n], wY3[:, kc, j * 128:(j + 1) * 128], ub[:, kc, s0:s0 + n], kc == 0, kc == KC - 1,
                               [wYk, "ub"], [p2k])
                    dst = newx if sample else xaE[:, 3 + s0:3 + s0 + n]
                    cp(dst, p[:, 0:n], [pk], [xkey])
                    act(gy[:, s0:s0 + n], p2[:, 0:n], AF.Gelu_apprx_tanh, [p2k], ["gy"])
                cwb = 24 + 4 * j
                ts(xc[:, 0:nt], tap(0), par[:, l, cwb:cwb + 1], par[:, l, 40 + j:41 + j], ALU.mult, ALU.add, [xkey, "par"], ["xc"])
                for t in (1, 2, 3):
                    stt(xc[:, 0:nt], tap(t), par[:, l, cwb + t:cwb + t + 1], xc[:, 0:nt], ALU.mult, ALU.add, [xkey, "par", "xc"], ["xc"])
                cp(xcb[:, 0:nt], xc[:, 0:nt], ["xc"], ["xcb"])
                if sample:
                    dma(sconv_out[l, :, j, :, :], cconv[:, j, 1:4, :], ["cconv"], [])
                else:
                    cp(cst[:, l, j, :], xaE[:, nt:nt + 3], ["xaE"], ["cst"])
                for (s0, n) in groups:
                    pr, prk = PS()
                    mm(pr[:, 0:n], wL[:, 0, j, :], xcb[:, s0:s0 + n], True, True, [wLk, "xcb"], [prk])
                    pi_, pik = PS()
                    mm(pi_[:, 0:n], wL[:, 1, j, :], xcb[:, s0:s0 + n], True, True, [wLk, "xcb"], [pik])
                    act(la_r[:, s0:s0 + n], pr[:, 0:n], AF.Sigmoid, [prk, "par"], ["la_r"], bias=par[:, l, 44 + j:45 + j])
                    act(la_i[:, s0:s0 + n], pi_[:, 0:n], AF.Sigmoid, [pik, "par"], ["la_i"], bias=par[:, l, 48 + j:49 + j])
                act(la_a[:, 0:nt], la_r[:, 0:nt], AF.Exp, ["la_r", "c1"], ["la_a"], scale=c1[:, l, j:j + 1])
                act(la_g[:, 0:nt], la_r[:, 0:nt], AF.Exp, ["la_r", "c2"], ["la_g"], scale=c2[:, l, j:j + 1])
                act(la_g[:, 0:nt], la_g[:, 0:nt], AF.Sqrt, ["la_g"], ["la_g"], scale=-1.0, bias=1.0)
                tt(la_g[:, 0:nt], la_g[:, 0:nt], la_i[:, 0:nt], ALU.mult, ["la_g", "la_i"], ["la_g"])
                tt(la_g[:, 0:nt], la_g[:, 0:nt], xc[:, 0:nt], ALU.mult, ["la_g", "xc"], ["la_g"])
                if sample:
                    tt(la_h[:, 0:nt], la_a[:, 0:nt], clru[:, j, :], ALU.mult, ["la_a", "clru"], ["la_h"])
                    tt(hnew[:, j, :], la_h[:, 0:nt], la_g[:, 0:nt], ALU.add, ["la_h", "la_g"], ["hnew"])
                    tt(oa[:, j, 0:nt], hnew[:, j, :], gy[:, 0:nt], ALU.mult, ["hnew", "gy"], ["oa"])
                else:
                    if tile_idx == 0:
                        tt(la_g[:, 0:nt], la_g[:, 0:nt], tabf[:, 1028:1028 + nt], ALU.mult, ["la_g", "tabf"], ["la_g"])
                    S.op("dve", lambda e, j=j: e.tensor_tensor_scan(out=la_h[:, 0:nt], data0=la_a[:, 0:nt], data1=la_g[:, 0:nt],
                                                                   initial=hst[:, l, j:j + 1], op0=ALU.mult, op1=ALU.add),
                         ["la_a", "la_g", "hst"], ["la_h"])
                    cp(hst[:, l, j:j + 1], la_h[:, nt - 1:nt], ["la_h"], ["hst"])
                    tt(oa[:, j, 0:nt], la_h[:, 0:nt], gy[:, 0:nt], ALU.mult, ["la_h", "gy"], ["oa"])
            if sample:
                dma(slru_out[l], hnew[:], ["hnew"], [])
            if last:
                dma(pconv_out[l], cst[:, l, :, :], ["cst"], [])
                dma(plru_out[l], hst[:, l, :], ["hst"], [])

            if STOP < 3:
                return
            wQ, wQk = wload(l, "inQK")
            wQ3 = wQ[:, 0:5120].rearrange("p (k c) -> p k c", k=KC)
            wV, wVk = wload(l, "inV")
            wV3 = wV[:, 0:5120].rearrange("p (k c) -> p k c", k=KC)
            for (s0, n) in groups:
                for c in range(4):
                    p, pk = PS()
                    for kc in range(KC):
                        mm(p[:, 0:n], wQ3[:, kc, c * 128:(c + 1) * 128], ub[:, kc, s0:s0 + n], kc == 0, kc == KC - 1, [wQk, "ub"], [pk])
                    act(qT[:, c, s0:s0 + n], p[:, 0:n], AF.Copy, [pk], ["qT"], scale=0.125)
                p, pk = PS()
                for kc in range(KC):
                    mm(p[:, 0:n], wQ3[:, kc, 512:640], ub[:, kc, s0:s0 + n], kc == 0, kc == KC - 1, [wQk, "ub"], [pk])
                if sample:
                    act(kof[:, 0:n], p[:, 0:n], AF.Copy, [pk], ["kof"])
                else:
                    act(kTe[:, 128 + s0:128 + s0 + n], p[:, 0:n], AF.Copy, [pk], ["kTe"])
                    if last:
                        act(kof[:, s0:s0 + n], p[:, 0:n], AF.Copy, [pk], ["kof"])
            if sample:
                pv_, pvk = PS()
                for kc in range(KC):
                    mm(pv_[:, 0:NS], wV3[:, kc, 0:128], ub[:, kc, 0:NS], kc == 0, kc == KC - 1, [wVk, "ub"], [pvk])
                act(vnT[:], pv_[:, 0:NS], AF.Copy, [pvk], ["vnT"])
                dma(kmod[:], ck_in[l], [], ["kmod"])
                dma(vmod[:], cv_in[l], [], ["vmod"])
                dma(sk_out[l], ck_in[l, :, :, 1:128], [], [])
                dma(skn_out[l], kof[:, 0:NS], ["kof"], [])
                dma(sv_out[l], cv_in[l, 1:128, :, :].rearrange("k s f -> s k f"), [], [])
                dma(svn_out[l], vnT[:], ["vnT"], [])
                if STOPB < 1:
                    return
                pSb = [PS(), PS()]
                for s in range(NS):
                    for g in range(2):
                        pS_, pSk = pSb[g]
                        mm(pS_[:, s * 4:s * 4 + 4], kmod[64 * g:64 * g + 64, s, :], qT[64 * g:64 * g + 64, :, s], True, True,
                           ["kmod", "qT"], [pSk])
                for g in range(2):
                    act(eS[:, g * 64:(g + 1) * 64], pSb[g][0][:, 0:64], AF.Exp, [pSb[g][1]], ["eS"])
                memset(eS[0:1, :], 0.0, ["eS"])
                if STOPB < 2:
                    return
                pO, pOk = PS()
                pD, pDk = PS()
                for s in range(NS):
                    for g in range(2):
                        c0 = g * 64 + s * 4
                        mm(pO[64 * g:64 * g + 64, s * 4:s * 4 + 4], vmod[:, s, 64 * g:64 * g + 64], eS[:, c0:c0 + 4], True, True,
                           ["vmod", "eS"], [pOk])
                        mm(pD[64 * g:64 * g + 64, s * 4:s * 4 + 4], onesb[:, 0:64], eS[:, c0:c0 + 4], True, True,
                           ["onesb", "eS"], [pDk])
                if STOPB < 3:
                    return
                cp(qTf[:], qT[:, :, 0:NS], ["qT"], ["qTf"])
                tt(prodq[:], qTf[:], kof[:, 0:NS].unsqueeze(1).to_broadcast([128, 4, NS]), ALU.mult, ["qTf", "kof"], ["prodq"])
                for g in range(2):
                    gs = slice(64 * g, 64 * g + 64)
                    pN, pNk = PS()
                    mm(pN[gs, 0:64], onesf[gs, 0:64], prodq[gs, :, :].rearrange("p c s -> p (c s)"), True, True, ["onesf", "prodq"], [pNk])
                    act(en[gs, :], pN[gs, 0:64], AF.Exp, [pNk], ["en"])
                if STOPB < 4:
                    return
                en3 = en[:].rearrange("p (c s) -> p s c", c=4)
                rd3 = rd[:, 0:64].rearrange("p (s c) -> p s c", c=4)
                nm3 = m1[:, 0:64].rearrange("p (s c) -> p s c", c=4)
                pD3 = pD[:, 0:64].rearrange("p (s c) -> p s c", c=4)
                pO3 = pO[:, 0:64].rearrange("p (s c) -> p s c", c=4)
                tt(nm3, en3, vnT[:].unsqueeze(2).to_broadcast([128, NS, 4]), ALU.mult, ["en", "vnT"], ["m1"])
                tt(nm3, nm3, pO3, ALU.add, ["m1", pOk], ["m1"])
                tt(rd3, en3, pD3, ALU.add, ["en", pDk], ["rd"])
                for c in range(4):
                    ts(rd3[:, :, c], rd3[:, :, c], es_t[:, l, c:c + 1], None, ALU.add, None, ["rd", "es"], ["rd"])
                act(rd[:, 0:64], rd[:, 0:64], AF.Ln, ["rd"], ["rd"])
                act(rd[:, 0:64], rd[:, 0:64], AF.Exp, ["rd"], ["rd"], scale=-1.0)
                tt(ob[:, :, 0:NS], m1[:, 0:64].rearrange("p (s c) -> p c s", c=4), rd[:, 0:64].rearrange("p (s c) -> p c s", c=4),
                   ALU.mult, ["m1", "rd"], ["ob"])
            else:
                cp(kTe[:, 0:128], kst[:, l, :], ["kst"], ["kTe"])
                cp(ve[:, 0, :], vst[:, l, :], ["vst"], ["ve"])
                for b in range(nb):
                    pv_, pvk = PS()
                    for kc in range(KC):
                        mm(pv_[:, 0:128], ub[:, kc, b * 128:(b + 1) * 128], wV3[:, kc, 0:128], kc == 0, kc == KC - 1, [wVk, "ub"], [pvk])
                    act(ve[:, 1 + b, :], pv_[:, 0:128], AF.Copy, [pvk], ["ve"])
                    if last and b == nb - 1:
                        act(vof[:], pv_[:, 0:128], AF.Copy, [pvk], ["vof"])
                cp(kst[:, l, :], kTe[:, nt:nt + 128], ["kTe"], ["kst"])
                cp(vst[:, l, :], ve[:, nb, :], ["ve"], ["vst"])
                if last:
                    dma(pk_out[l], kof[:, nt - 128:nt], ["kof"], [])
                    dma(pv_out[l], vof[:], ["vof"], [])
                for b in range(nb):
                    gb_ = blk0 + b
                    pO, pOk = PS()
                    pD, pDk = PS()
                    for g in range(2):
                        gs = slice(64 * g, 64 * g + 64)
                        e_p, e_c = (eP, eC) if g == 0 else (eP2, eC2)
                        kp, kc_ = ("eP%d" % g, "eC%d" % g)
                        pC, pCk = PS()
                        for c in range(4):
                            mm(pC[:, c * 128:(c + 1) * 128], kTe[gs, 128 + b * 128:128 + (b + 1) * 128], qT[gs, c, b * 128:(b + 1) * 128],
                               True, True, ["kTe", "qT"], [pCk])
                        act(e_c[:], pC[:], AF.Exp, [pCk], [kc_])
                        mo = 1024 if gb_ == 0 else 0
                        tt(e_c[:], e_c[:], maskb[:, mo:mo + 512], ALU.mult, [kc_, "maskb"], [kc_])
                        if gb_ > 0:
                            pP, pPk = PS()
                            for c in range(4):
                                mm(pP[:, c * 128:(c + 1) * 128], kTe[gs, b * 128:(b + 1) * 128], qT[gs, c, b * 128:(b + 1) * 128],
                                   True, True, ["kTe", "qT"], [pPk])
                            act(e_p[:], pP[:], AF.Exp, [pPk], [kp])
                            mo = 1536 if gb_ == 1 else 512
                            tt(e_p[:], e_p[:], maskb[:, mo:mo + 512], ALU.mult, [kp, "maskb"], [kp])
                    for g in range(2):
                        gs = slice(64 * g, 64 * g + 64)
                        e_p, e_c = (eP, eC) if g == 0 else (eP2, eC2)
                        kp, kc_ = ("eP%d" % g, "eC%d" % g)
                        if gb_ > 0:
                            mm(pO[gs, :], ve[:, b, gs], e_p[:], True, False, ["ve", kp], [pOk])
                            mm(pD[gs, :], onesb[:, 0:64], e_p[:], True, False, ["onesb", kp], [pDk])
                        mm(pO[gs, :], ve[:, b + 1, gs], e_c[:], gb_ == 0, True, ["ve", kc_], [pOk])
                        mm(pD[gs, :], onesb[:, 0:64], e_c[:], gb_ == 0, True, ["onesb", kc_], [pDk])
                    for c in range(4):
                        ts(rd[:, c * 128:(c + 1) * 128], pD[:, c * 128:(c + 1) * 128], es_t[:, l, c:c + 1], None, ALU.add, None,
                           [pDk, "es"], ["rd"])
                    act(rd[:], rd[:], AF.Ln, ["rd"], ["rd"])
                    act(rd[:], rd[:], AF.Exp, ["rd"], ["rd"], scale=-1.0)
                    tt(ob[:, :, b * 128:(b + 1) * 128], pO[:].rearrange("p (c q) -> p c q", c=4), rd[:].rearrange("p (c q) -> p c q", c=4),
                       ALU.mult, [pOk, "rd"], ["ob"])

            if STOP < 4:
                return
            if sample:
                for h in range(4):
                    p, pk = PS()
                    for kc in range(KC):
                        mm(p[:, 0:NS], wV3[:, kc, 128 + h * 128:128 + (h + 1) * 128], ub[:, kc, 0:NS], kc == 0, kc == KC - 1,
                           [wVk, "ub"], [pk])
                    act(vTf[:, h, :], p[:, 0:NS], AF.Copy, [pk], ["vTf"])
                pvr, pvrk = PS()
                for kc in range(KC):
                    mm(pvr[0:NS, :], ub[:, kc, 0:NS], wV3[:, kc, 128:640], kc == 0, kc == KC - 1, [wVk, "ub"], [pvrk])
                cp(vtmf[:, :], pvr[0:NS, :], [pvrk], ["vtmf2"])
            else:
                for b in range(nb):
                    pvr, pvrk = PS()
                    for kc in range(KC):
                        mm(pvr[:], ub[:, kc, b * 128:(b + 1) * 128], wV3[:, kc, 128:640], kc == 0, kc == KC - 1, [wVk, "ub"], [pvrk])
                    act(vr[:, b, :], pvr[:], AF.Copy, [pvrk], [("vr", b)])
            wRq, wRqk = wload(l, "inRq")
            wRk_, wRkk = wload(l, "inRk")
            wR3s = (wRq[:, 0:4096].rearrange("p (k c) -> p k c", k=KC), wRk_[:, 0:4096].rearrange("p (k c) -> p k c", k=KC))
            wRks = (wRqk, wRkk)
            wG, wGk = wload(l, "inGR")
            wG3 = wG[:, 0:4096].rearrange("p (k c) -> p k c", k=KC)
            for (s0, n) in groups:
                for qk in range(2):
                    for ch in range(2):
                        pn, pnk = PS()
                        psw, pswk = PS()
                        cb_ = ch * 128
                        wR3 = wR3s[qk]
                        wRk = wRks[qk]
                        for kc in range(KC):
                            mm(pn[:, 0:n], wR3[:, kc, cb_:cb_ + 128], ub[:, kc, s0:s0 + n], kc == 0, kc == KC - 1, [wRk, "ub"], [pnk])
                        for kc in range(KC):
                            mm(psw[:, 0:n], wR3[:, kc, cb_ + 256:cb_ + 384], ub[:, kc, s0:s0 + n], kc == 0, kc == KC - 1, [wRk, "ub"], [pswk])
                        tb = 2 * qk
                        tt(rt1[:, 0:n], pn[:, 0:n], ropet[:, tb, s0:s0 + n], ALU.mult, [pnk, "ropet"], ["rt1"])
                        tt(rt2[:, 0:n], psw[:, 0:n], ropet[:, tb + 1, s0:s0 + n], ALU.mult, [pswk, "ropet"], ["rt2"])
                        dst = (qr, kr)[qk]
                        tt(dst[:, ch, s0:s0 + n], rt1[:, 0:n], rt2[:, 0:n], ALU.add, ["rt1", "rt2"], [("qr", "kr")[qk]])
                        if sample:
                            dstf = (qrf, krf)[qk]
                            tt(dstf[:, ch, :], rt1[:, 0:n], rt2[:, 0:n], ALU.add, ["rt1", "rt2"], [("qrf", "krf")[qk]])
                for h in range(4):
                    p, pk = PS()
                    for kc in range(KC):
                        mm(p[:, 0:n], wG3[:, kc, h * 128:(h + 1) * 128], ub[:, kc, s0:s0 + n], kc == 0, kc == KC - 1, [wGk, "ub"], [pk])
                    act(sgr[:, h, s0:s0 + n], p[:, 0:n], AF.Silu, [pk], ["sgr"])

            def gate_cols(gcen_t, sg_ap_fn, out_fn, ncols_per_h, heads=(0, 1, 2, 3)):
                for hi, h in enumerate(heads):
                    cs = slice(hi * ncols_per_h, (hi + 1) * ncols_per_h)
                    stt(out_fn(h), gcen_t[:, cs], par[:, l, 60 + h:61 + h], sg_ap_fn(h), ALU.mult, ALU.mult,
                        ["gcen", "par", "sgr"], ["oc"])

            if sample:
                dma(sret[:], cret_in[l], [], ["sret"])
                pQb = [PS(), PS()]
                for h in range(4):
                    hs = slice(64 * (h % 2), 64 * (h % 2) + 64)
                    pQ, pQk = pQb[h % 2]
                    for s in range(NS):
                        c_ = (h // 2) * NS + s
                        mm(pQ[:, c_:c_ + 1], sret[hs, h // 2, s, :], qrf[hs, h // 2, s:s + 1], True, True,
                           ["sret", "qrf"], [pQk])
                tt(prodf[:], qrf[:], krf[:], ALU.mult, ["qrf", "krf"], ["prodf"])
                pDb = [PS(), PS()]
                for h in range(4):
                    hs = slice(64 * (h % 2), 64 * (h % 2) + 64)
                    pDt, pDtk = pDb[h % 2]
                    mm(pDt[:, (h // 2) * NS:(h // 2 + 1) * NS], onesf[hs, :], prodf[hs, h // 2, :], True, True, ["onesf", "prodf"], [pDtk])
                for h in range(4):
                    pDt, pDtk = pDb[h % 2]
                    pQ, pQk = pQb[h % 2]
                    c_ = slice((h // 2) * NS, (h // 2 + 1) * NS)
                    tt(osm[:, h * NS:(h + 1) * NS], pDt[:, c_], vTf[:, h, :], ALU.mult, [pDtk, "vTf"], ["osm"])
                    stt(osm[:, h * NS:(h + 1) * NS], pQ[:, c_], GAM[h], osm[:, h * NS:(h + 1) * NS], ALU.mult, ALU.add,
                        [pQk, "osm"], ["osm"])
                groupnorm_gate(l, osm[:], "osm", 64, None, None,
                               lambda g_, a_, b_: gate_cols(g_, lambda h: sgr[:, h, 0:NS], lambda h: oc[:, h, 0:NS], NS))
                for ch in range(2):
                    ptr, ptrk = PS()
                    S.op("pe", lambda e, ch=ch, ptr=ptr: e.transpose(ptr[0:NS, 0:128], krf[:, ch, :], identf[:]), ["krf", "identf"], [ptrk])
                    cp(ktmf[:, ch * 128:(ch + 1) * 128], ptr[0:NS, 0:128], [ptrk], ["ktmf"])
                for h in range(4):
                    hs = slice(64 * (h % 2), 64 * (h % 2) + 64)
                    ch = h // 2
                    for q in range(4):
                        tt(vbd[:], vtmf[:, h * 128:(h + 1) * 128].unsqueeze(1).to_broadcast([NS, 4, 128]),
                           identf[0:NS, 4 * q:4 * q + 4].unsqueeze(2).to_broadcast([NS, 4, 128]), ALU.mult, ["vtmf2", "identf"], ["vbd"])
                        pk_, pkk = PS()
                        mm(pk_[hs, :], ktmf[:, h * 64:(h + 1) * 64], vbd[:].rearrange("p s e -> p (s e)"), True, True,
                           ["ktmf", "vbd"], [pkk])
                        stt(sret[hs, ch, 4 * q:4 * q + 4, :], sret[hs, ch, 4 * q:4 * q + 4, :], GAM[h],
                            pk_[hs, :].rearrange("p (s e) -> p s e", s=4), ALU.mult, ALU.add, ["sret", pkk], ["sret"])
                dma(sret_out[l], sret[:], ["sret"], [])
            else:
                for b in range(nb):
                    bs = slice(b * 128, (b + 1) * 128)
                    for ch in range(2):
                        tt(kdt[:, ch, :], kr[:, ch, bs], tabf[:, 768 + ch * 128:768 + (ch + 1) * 128], ALU.mult, ["kr", "tabf"], ["kdt"])
                        tt(qdc4[:, b, ch, :], qr[:, ch, bs], tabf[:, 512 + ch * 128:512 + (ch + 1) * 128], ALU.mult, ["qr", "tabf"], [("qdc", b)])
                    for ch in range(2):
                        S.op("pe", lambda e, ch=ch: e.transpose(pst[:, ch * 128:(ch + 1) * 128], kdt[:, ch, :], identb[:]),
                             ["kdt", "identb"], ["pst"])
                    cp(ktm4[:, b, :], pst[:, 0:256], ["pst"], [("ktm", b)])
                    pStb = [PS(), PS()]
                    for h in range(4):
                        hs = slice(64 * (h % 2), 64 * (h % 2) + 64)
                        pSt, pStk = pStb[h % 2]
                        mm(pSt[:, (h // 2) * 128:(h // 2 + 1) * 128], kr[hs, h // 2, bs], qr[hs, h // 2, bs], True, True, ["kr", "qr"], [pStk])
                    for h in range(4):
                        pSt, pStk = pStb[h % 2]
                        tt(aT4[:, b, h * 128:(h + 1) * 128], pSt[:, (h // 2) * 128:(h // 2 + 1) * 128], tabf[:, h * 128:(h + 1) * 128], ALU.mult,
                           [pStk, "tabf"], [("aT", b)])
                for b in range(nb):
                    bs = slice(b * 128, (b + 1) * 128)
                    pOb = [PS(), PS()]
                    for h in range(4):
                        hs = slice(64 * (h % 2), 64 * (h % 2) + 64)
                        pO, pOk = pOb[h % 2]
                        oc_ = slice((h // 2) * 128, (h // 2 + 1) * 128)
                        mm(pO[:, oc_], vr[:, b, h * 128:(h + 1) * 128], aT4[:, b, h * 128:(h + 1) * 128], True, False,
                           [("vr", b), ("aT", b)], [pOk])
                        mm(pO[:, oc_], Sbf[hs, l, h // 2, :], qdc4[hs, b, h // 2, :], False, True, ["Sbf", ("qdc", b)], [pOk])
                    pU, pUk = PS()
                    for h in range(4):
                        hs = slice(64 * (h % 2), 64 * (h % 2) + 64)
                        mm(pU[hs, (h // 2) * 128:(h // 2 + 1) * 128], ktm4[:, b, h * 64:(h + 1) * 64], vr[:, b, h * 128:(h + 1) * 128], True, True,
                           [("ktm", b), ("vr", b)], [pUk])
                    for ch in range(2):
                        stt(Sst[:, l, ch, :], Sst[:, l, ch, :], tabf[:, 1024 + ch:1025 + ch], pU[:, ch * 128:(ch + 1) * 128], ALU.mult, ALU.add,
                            ["Sst", "tabf", pUk], ["Sst"])
                    act(Sbf[:, l, :, :], Sst[:, l, :, :], AF.Copy, ["Sst"], ["Sbf"])
                    ocat_v = ocat[:].rearrange("p (a r c) -> p a r c", a=2, r=2)
                    for r_ in range(2):
                        pO, pOk = pOb[r_]
                        act(ocat_v[:, :, r_, :], pO[:, 0:256].rearrange("p (a c) -> p a c", a=2), AF.Copy, [pOk], ["ocat"])
                    groupnorm_gate(l, ocat[:], "ocat", 512, None, None,
                                   lambda g_, a_, b_, bs=bs: gate_cols(g_, lambda h: sgr[:, h, bs], lambda h: oc[:, h, bs], 128))
                if last:
                    dma(pret_out[l], Sst[:, l, :, :], ["Sst"], [])

            if STOP < 5:
                return
            for m in range(8):
                wt, wk = wload(l, "gb%d" % m)
                wg3 = wt[:, 0:3072].rearrange("p (k c) -> p k c", k=KC)
                wb4 = wt[:, 3072:4608].rearrange("p (b k c) -> p b k c", b=3, k=4)
                for (s0, n) in groups:
                    sgs = (sga, sgb, sgc)
                    for bi in range(3):
                        p, pk = PS()
                        for kc in range(KC):
                            mm(p[:, 0:n], wg3[:, kc, bi * 128:(bi + 1) * 128], ub[:, kc, s0:s0 + n], kc == 0, kc == KC - 1, [wk, "ub"], [pk])
                        act(sgs[bi][:, 0:n], p[:, 0:n], AF.Sigmoid, [pk], ["sg%d" % bi])
                    srcs = ((oa, "oa"), (ob, "ob"), (oc, "oc"))
                    pks = []
                    for bi in range(3):
                        p, pk = PS()
                        for kc in range(4):
                            mm(p[:, 0:n], wb4[:, bi, kc, :], srcs[bi][0][:, kc, s0:s0 + n], kc == 0, kc == 3, [wk, srcs[bi][1]], [pk])
                        pks.append((p, pk))
                    tt(m1[:, 0:n], sga[:, 0:n], pks[0][0][:, 0:n], ALU.mult, ["sg0", pks[0][1]], ["m1"])
                    tt(m2[:, 0:n], sgb[:, 0:n], pks[1][0][:, 0:n], ALU.mult, ["sg1", pks[1][1]], ["m2"])
                    tt(m1[:, 0:n], m1[:, 0:n], m2[:, 0:n], ALU.add, ["m1", "m2"], ["m1"])
                    tt(m2[:, 0:n], sgc[:, 0:n], pks[2][0][:, 0:n], ALU.mult, ["sg2", pks[2][1]], ["m2"])
                    tt(mg[:, m, s0:s0 + n], m1[:, 0:n], m2[:, 0:n], ALU.add, ["m1", "m2"], [("mg", m)])
            if dbg and l == 0 and tile_idx == 1:
                dma(dbg_oa, oa[:, :, 0:512], ["oa"], [])
                dma(dbg_ob, ob[:, :, 0:512], ["ob"], [])
                dma(dbg_oc, oc[:, :, 0:512], ["oc"], [])
                dma(dbg_mg, mg[:, :, 0:512], [("mg", m_) for m_ in range(8)], [])
                dma(dbg_u, ub[:, :, 0:512], ["ub"], [])
                dma(dbg_qr, qr[:, :, 0:512], ["qr"], [])
                dma(dbg_kr, kr[:, :, 0:512], ["kr"], [])
                dma(dbg_vr, vr[:, :, :], [("vr", b_) for b_ in range(4)], [])
                dma(dbg_sgr, sgr[:, :, 0:512], ["sgr"], [])
                dma(dbg_aT, aT4[:, 3, :], [("aT", 3)], [])
                dma(dbg_gno, gno[:], ["gno"], [])
                dma(dbg_S, Sst[:, 0, :, :], ["Sst"], [])
            for half in range(2):
                wt, wk = wload(l, "wo%d" % half)
                w3 = wt[:, 0:4096].rearrange("p (k c) -> p k c", k=KC)
                for (s0, n) in groups:
                    for mc in range(4):
                        p, pk = PS()
                        for kc in range(KC):
                            mm(p[:, 0:n], w3[:, kc, mc * 128:(mc + 1) * 128], mg[:, kc, s0:s0 + n], kc == 0, kc == KC - 1, [wk, ("mg", kc)], [pk])
                        m = 4 * half + mc
                        tt(xt[:, m, s0:s0 + n], p[:, 0:n], xt[:, m, s0:s0 + n], ALU.add, [pk, "xt"], ["xt"])
            if dbg and l == 0 and tile_idx == 1:
                dma(dbg_x, xt[:, :, 0:512], ["xt"], [])

        tiles = []
        for t in range(ntiles):
            if t == 0:
                tiles.append((0, 1))
            else:
                tiles.append((4 * (t - 1) + 1, 4))
        for ti, (blk0, nb) in enumerate(tiles):
            nt = nb * 128
            t0 = blk0 * 128
            groups = [(0, nt)]
            dq = "sp" if ti == 0 else "pool"
            dma(xt[:, :, 0:nt], xin[:, :, t0:t0 + nt], [], ["xt"], eng=dq)
            dma(ropet[:, :, 0:nt], rope_in[:, :, t0:t0 + nt], [], ["ropet"], eng=dq)
            for l in range(nlayers):
                if STOP >= 1:
                    ffn(l, 0, groups)
                if STOP >= 2:
                    mixer(l, groups, nt, ti, blk0, nb, False)
                if STOP >= 6:
                    ffn(l, 1, groups)
            rmsnorm(groups, lambda kc: fnorm[:, kc:kc + 1], xt, "xt")
            dma(y_out[:, :, t0:t0 + nt], xt[:, :, 0:nt], ["xt"], [])
        if do_sample:
            groups = [(0, NS)]
            dma(xt[:, :, 0:NS], xs_in, [], ["xt"])
            dma(ropet[:, :, 0:NS], rope_in[:, :, SEQP:SEQP + NS], [], ["ropet"])
            for l in range(nlayers):
                dma(cconv[:, :, 0:3, :], cconv_in[l], [], ["cconv"])
                dma(clru[:], clru_in[l], [], ["clru"])
                if STOP >= 1:
                    ffn(l, 0, groups)
                if STOP >= 2:
                    mixer(l, groups, NS, -1, 0, 0, True)
                if STOP >= 6:
                    ffn(l, 1, groups)
            rmsnorm(groups, lambda kc: fnorm[:, kc:kc + 1], xt, "xt")
            dma(ys_out, xt[:, :, 0:NS], ["xt"], [])
        S.emit()
        build.stats = S.stats
    return nc


def _prep_inputs(inp, ncores=8):
    inp = {k: np.asarray(v) for k, v in inp.items()}
    Wall = _host_weights(inp)
    P, FN = _host_params(inp)
    TB = _host_tables()
    maps = []
    for c in range(ncores):
        seq = c % 2
        xs = np.concatenate([np.zeros((PADN, D), np.float32), inp["meta_tokens"].astype(np.float32), inp["x_prompt"][seq]], axis=0)
        xin = np.ascontiguousarray(xs.reshape(SEQP, KC, 128).transpose(2, 1, 0))
        ss = slice(NS * c, NS * (c + 1))
        xsm = inp["x_sample"][ss, 0, :]
        xs_in = np.ascontiguousarray(xsm.reshape(NS, KC, 128).transpose(2, 1, 0))
        ck = inp["cache_swa_k"][:, ss]
        ck_t = np.ascontiguousarray(ck.transpose(0, 3, 4, 1, 2).reshape(L, 128, NS, 128))
        cv = inp["cache_swa_v"][:, ss]
        cv_t = np.ascontiguousarray(cv.transpose(0, 2, 1, 3, 4).reshape(L, 128, NS, 128))
        cc = inp["state_conv"][:, ss]
        cc_t = np.ascontiguousarray(cc.reshape(L, NS, 3, 4, 128).transpose(0, 4, 3, 2, 1))
        cl = inp["state_lru"][:, ss]
        cl_t = np.ascontiguousarray(cl.reshape(L, NS, 4, 128).transpose(0, 3, 2, 1))
        cr = inp["state_ret"][:, ss]
        cr_t = np.ascontiguousarray(cr.reshape(L, NS, 2, 2, 64, 128).transpose(0, 3, 4, 2, 1, 5).reshape(L, 128, 2, NS, 128))
        maps.append(dict(xin=xin, xs_in=xs_in, wall0=Wall[0], wall1=Wall[1], wall2=Wall[2], wall3=Wall[3], par=P, fnorm=FN, rope=TB["rope"], tabf=TB["f32"], masks=TB["masks"],
                         c128=TB["c128"], ck=ck_t, cv=cv_t, cconv=cc_t, clru=cl_t, cret=cr_t))
    return maps


def _assemble(results):
    f32 = np.float32
    y = np.zeros((2, SEQ, D), f32)
    ysamp = np.zeros((128, 1, D), f32)
    pk = np.zeros((L, 2, 128, 2, 64), f32)
    pv = np.zeros((L, 2, 128, 2, 64), f32)
    pconv = np.zeros((L, 2, 3, 512), f32)
    plru = np.zeros((L, 2, 512), f32)
    pret = np.zeros((L, 2, 4, 64, 128), f32)
    sk = np.zeros((L, 128, 128, 2, 64), f32)
    sv = np.zeros((L, 128, 128, 2, 64), f32)
    sconv = np.zeros((L, 128, 3, 512), f32)
    slru = np.zeros((L, 128, 512), f32)
    sret = np.zeros((L, 128, 4, 64, 128), f32)
    for c, r in enumerate(results):
        ss = slice(NS * c, NS * (c + 1))
        if c < 2:
            yy = r["y"]
            y[c] = yy.transpose(2, 1, 0).reshape(SEQP, D)[PADN + NMETA:]
            pk[:, c] = r["pk"].transpose(0, 2, 1).reshape(L, 128, 2, 64)
            pv[:, c] = r["pv"].reshape(L, 128, 2, 64)
            pconv[:, c] = r["pconv"].transpose(0, 3, 2, 1).reshape(L, 3, 512)
            plru[:, c] = r["plru"].transpose(0, 2, 1).reshape(L, 512)
            pret[:, c] = r["pret"].reshape(L, 2, 64, 2, 128).transpose(0, 3, 1, 2, 4).reshape(L, 4, 64, 128)
        ysamp[ss, 0] = r["ys"].transpose(2, 1, 0).reshape(NS, D)
        skf = np.concatenate([r["sk"], r["skn"][:, :, :, None]], axis=3)
        sk[:, ss] = skf.reshape(L, 2, 64, NS, 128).transpose(0, 3, 4, 1, 2)
        svf = np.concatenate([r["sv"], r["svn"].transpose(0, 2, 1)[:, :, None, :]], axis=2)
        sv[:, ss] = svf.reshape(L, NS, 128, 2, 64)
        sconv[:, ss] = r["sconv"].transpose(0, 4, 3, 2, 1).reshape(L, NS, 3, 512)
        slru[:, ss] = r["slru"].transpose(0, 3, 2, 1).reshape(L, NS, 512)
        sret[:, ss] = r["sret"].reshape(L, 2, 64, 2, NS, 128).transpose(0, 4, 3, 1, 2, 5).reshape(L, NS, 4, 64, 128)
    return (y, ysamp, pk, pv, pconv, plru, pret, sk, sv, sconv, slru, sret)


def kernel(**inputs):
    maps = _prep_inputs(inputs)
    nc = build()
    res = run_bass_kernel_spmd(nc, maps, core_ids=list(range(8)))
    return _assemble(res.results)
```
